# Optimizing a Trainium2 kernel written in Bass

```python
import math
import jax, jax.numpy as jnp
from jax import lax
import numpy as np

D_MODEL = 1024
BATCH = 16
SEQ = 4096
DEPTH = 4

N_EVEN = (DEPTH + 1) // 2
N_ODD = DEPTH // 2
D_FF = 2816
MIX_WIDTH = D_MODEL
HALF = MIX_WIDTH // 2
CONV_WIDTH = 4
RG_WIDTH = HALF
RG_BLOCKS = 8
RG_BLK = RG_WIDTH // RG_BLOCKS
RG_C = 8.0
ML_HEADS = 4
ML_DH = HALF // ML_HEADS
ML_CHUNK = 64
ML_NORM_EPS = 1e-6
RK_WIDTH = HALF
RK_HEADS = 8
RK_DH = RK_WIDTH // RK_HEADS
RK_DECAY_LORA = 32
RK_A_LORA = 32
RK_G_LORA = 64
RK_NORM_EPS = 64e-5
GLA_HEADS = 4
GLA_DK = 64
GLA_DV = HALF // GLA_HEADS
GLA_GATE_LORA = 16
GLA_TAU = 16.0
GLA_CHUNK = 64
GLA_NORM_EPS = 1e-5
DEEPNORM_ALPHA = (2.0 * DEPTH) ** 0.25
DEEPNORM_BETA = (8.0 * DEPTH) ** -0.25
LN_EPS = 1e-5

EVEN_SIZES = (RG_WIDTH, RG_WIDTH, 2 * HALF, HALF, HALF, 2 * ML_HEADS)
EVEN_IN = sum(EVEN_SIZES)
RK_SIZES = (RK_WIDTH, RK_WIDTH, RK_WIDTH, RK_DECAY_LORA, RK_A_LORA, RK_G_LORA)
RK_IN = sum(RK_SIZES)
GLA_SIZES = (GLA_HEADS * GLA_DK, GLA_HEADS * GLA_DK, HALF, GLA_GATE_LORA, HALF)
GLA_IN = sum(GLA_SIZES)
ODD_IN = RK_IN + GLA_IN

kernel_name = "hybrid_rglru_mlstm_rwkv7_gla_macaron_deepnorm"


def split_cols(p, sizes):
    return jnp.split(p, [int(s) for s in np.cumsum(sizes)[:-1]], axis=-1)


def layer_norm(x, g, b):
    xf = x.astype(jnp.float32)
    mu = jnp.mean(xf, -1, keepdims=True)
    var = jnp.mean(jnp.square(xf - mu), -1, keepdims=True)
    return ((xf - mu) * lax.rsqrt(var + LN_EPS) * g + b).astype(x.dtype)


def head_norm(h, g, b, eps):
    mu = jnp.mean(h, -1, keepdims=True)
    var = jnp.mean(jnp.square(h - mu), -1, keepdims=True)
    hn = ((h - mu) * lax.rsqrt(var + eps)).reshape(h.shape[0], h.shape[1], -1)
    hn = hn * g
    if b is not None:
        hn = hn + b
    return hn


def swiglu_ffn(x, wi, wo):
    gate, up = jnp.split(x @ wi, 2, axis=-1)
    return (jax.nn.silu(gate) * up) @ wo


def causal_depthwise_conv(t, w, b):
    S = t.shape[1]
    width = w.shape[0]
    tp = jnp.pad(t, ((0, 0), (width - 1, 0), (0, 0)))
    out = b
    for j in range(width):
        out = out + w[j] * tp[:, j:j + S]
    return out


def token_shift(t, mu):
    prev = jnp.pad(t, ((0, 0), (1, 0), (0, 0)))[:, :-1]
    return t + mu * (prev - t)


def linear_scan(a, b):
    def combine(l, r):
        a_l, b_l = l
        a_r, b_r = r
        return a_l * a_r, a_r * b_l + b_r
    _, h = lax.associative_scan(combine, (a, b), axis=1)
    return h


def to_chunks(t, c):
    B, S = t.shape[:2]
    t = t.reshape(B, S // c, c, *t.shape[2:])
    return jnp.moveaxis(jnp.moveaxis(t, 1, 0), 3, 2)


def from_chunks(t):
    t = jnp.moveaxis(jnp.moveaxis(t, 2, 3), 0, 1)
    return t.reshape(t.shape[0], t.shape[1] * t.shape[2], *t.shape[3:])


def rg_lru_mixer(xr, xg, conv_w, conv_b, wa, wx, ba, bx, lam):
    B, S, W = xr.shape
    u = causal_depthwise_conv(xr, conv_w, conv_b)
    ub = u.reshape(B, S, RG_BLOCKS, RG_BLK)
    r = jax.nn.sigmoid(jnp.einsum('bsnc,ncd->bsnd', ub, wa).reshape(B, S, W) + ba)
    i = jax.nn.sigmoid(jnp.einsum('bsnc,ncd->bsnd', ub, wx).reshape(B, S, W) + bx)
    log_a = -RG_C * r * jax.nn.softplus(-lam)
    a = jnp.exp(log_a)
    b = jnp.sqrt(-jnp.expm1(2.0 * log_a)) * (i * u)
    h = linear_scan(a, b)
    return h * jax.nn.gelu(xg)


def mlstm_mixer(qk, v, o, if_pre, conv_w, conv_b, i_bias, f_bias, norm_g):
    B, S, _ = v.shape
    qk = jax.nn.silu(causal_depthwise_conv(qk, conv_w, conv_b))
    q, k = jnp.split(qk, 2, axis=-1)
    q = q.reshape(B, S, ML_HEADS, ML_DH) * (ML_DH ** -0.5)
    k = k.reshape(B, S, ML_HEADS, ML_DH)
    v = v.reshape(B, S, ML_HEADS, ML_DH)
    i_pre = if_pre[..., :ML_HEADS] + i_bias
    log_f = jax.nn.log_sigmoid(if_pre[..., ML_HEADS:] + f_bias)
    causal = jnp.tril(jnp.ones((ML_CHUNK, ML_CHUNK), bool))

    def step(carry, inp):
        c_st, n_st, m = carry
        qc, kc, vc, ic, fc = inp
        bcum = jnp.cumsum(fc, axis=-1)
        d = jnp.where(causal, bcum[..., :, None] - bcum[..., None, :] + ic[..., None, :], -jnp.inf)
        inter = bcum + m[..., None]
        m_t = jnp.maximum(inter, jnp.max(d, -1))
        w_intra = jnp.exp(d - m_t[..., None])
        w_inter = jnp.exp(inter - m_t)
        s = jnp.einsum('bhid,bhjd->bhij', qc, kc) * w_intra
        num = jnp.einsum('bhij,bhje->bhie', s, vc) + w_inter[..., None] * jnp.einsum('bhid,bhde->bhie', qc, c_st)
        den = jnp.sum(s, -1) + w_inter * jnp.einsum('bhid,bhd->bhi', qc, n_st)
        h = num / jnp.maximum(jnp.abs(den), jnp.exp(-m_t))[..., None]
        b_last = bcum[..., -1]
        g = b_last[..., None] - bcum + ic
        m_new = jnp.maximum(b_last + m, jnp.max(g, -1))
        wk = jnp.exp(g - m_new[..., None])
        decay = jnp.exp(b_last + m - m_new)
        c_st = decay[..., None, None] * c_st + jnp.einsum('bhc,bhcd,bhce->bhde', wk, kc, vc)
        n_st = decay[..., None] * n_st + jnp.einsum('bhc,bhcd->bhd', wk, kc)
        return (c_st, n_st, m_new), h

    init = (jnp.zeros((B, ML_HEADS, ML_DH, ML_DH), jnp.float32),
            jnp.zeros((B, ML_HEADS, ML_DH), jnp.float32),
            jnp.zeros((B, ML_HEADS), jnp.float32))
    xs = (to_chunks(q, ML_CHUNK), to_chunks(k, ML_CHUNK), to_chunks(v, ML_CHUNK),
          to_chunks(i_pre, ML_CHUNK), to_chunks(log_f, ML_CHUNK))
    _, h = lax.scan(step, init, xs)
    h = from_chunks(h)
    return head_norm(h, norm_g, None, ML_NORM_EPS) * jax.nn.sigmoid(o)


def rwkv7_mixer(p, mu, w0, wB, a0, aB, gB, k_k, k_a, r_k, ln_g, ln_b):
    B, S, _ = p.shape
    p = token_shift(p, mu)
    r, k, v, wd, ad, gd = split_cols(p, RK_SIZES)
    w_raw = -jax.nn.softplus(-(w0 + jnp.tanh(wd) @ wB)) - 0.5
    decay = jnp.exp(-jnp.exp(w_raw))
    a = jax.nn.sigmoid(a0 + ad @ aB)
    g = jax.nn.sigmoid(gd) @ gB
    heads = lambda t: t.reshape(B, S, RK_HEADS, RK_DH)
    kk = heads(k * k_k)
    kk = kk / jnp.maximum(jnp.sqrt(jnp.sum(kk * kk, -1, keepdims=True)), 1e-12)
    k = k * (1.0 + (a - 1.0) * k_a)
    r, k, v, decay, a = heads(r), heads(k), heads(v), heads(decay), heads(a)

    def step(state, inp):
        r_t, w_t, k_t, v_t, kk_t, a_t = inp
        sa = jnp.einsum('bhij,bhj->bhi', state, -kk_t)
        state = (state * w_t[:, :, None, :] + sa[..., None] * (kk_t * a_t)[:, :, None, :]
                 + v_t[..., None] * k_t[:, :, None, :])
        return state, jnp.einsum('bhij,bhj->bhi', state, r_t)

    tm = lambda t: jnp.moveaxis(t, 1, 0)
    init = jnp.zeros((B, RK_HEADS, RK_DH, RK_DH), jnp.float32)
    _, y = lax.scan(step, init, (tm(r), tm(decay), tm(k), tm(v), tm(kk), tm(a)))
    y = head_norm(jnp.moveaxis(y, 0, 1), ln_g, ln_b, RK_NORM_EPS)
    bonus = (jnp.sum(r * k * r_k, -1, keepdims=True) * v).reshape(B, S, RK_WIDTH)
    return (y + bonus) * g


def gla_mixer(q, k, v, gd, og, gB, gb, norm_g):
    B, S, _ = v.shape
    log_alpha = jax.nn.log_sigmoid(gd @ gB + gb) / GLA_TAU
    q = to_chunks(q.reshape(B, S, GLA_HEADS, GLA_DK) * (GLA_DK ** -0.5), GLA_CHUNK)
    k = to_chunks(k.reshape(B, S, GLA_HEADS, GLA_DK), GLA_CHUNK)
    v = to_chunks(v.reshape(B, S, GLA_HEADS, GLA_DV), GLA_CHUNK)
    la = to_chunks(log_alpha.reshape(B, S, GLA_HEADS, GLA_DK), GLA_CHUNK)
    bcum = jnp.cumsum(la, axis=3)
    q_dec = q * jnp.exp(bcum)
    k_inv = k * jnp.exp(-bcum)
    causal = jnp.tril(jnp.ones((GLA_CHUNK, GLA_CHUNK), bool))
    attn = jnp.where(causal, jnp.einsum('nbhid,nbhjd->nbhij', q_dec, k_inv), 0.0)
    o = jnp.einsum('nbhij,nbhje->nbhie', attn, v)
    b_last = bcum[..., -1, :]
    kv = jnp.einsum('nbhcd,nbhce->nbhde', k * jnp.exp(b_last[..., None, :] - bcum), v)

    def step(state, inp):
        dec, kv_c = inp
        return dec[..., None] * state + kv_c, state

    init = jnp.zeros((B, GLA_HEADS, GLA_DK, GLA_DV), jnp.float32)
    _, s_prev = lax.scan(step, init, (jnp.exp(b_last), kv))
    o = o + jnp.einsum('nbhcd,nbhde->nbhce', q_dec, s_prev)
    o = from_chunks(o)
    return head_norm(o, norm_g, None, GLA_NORM_EPS) * jax.nn.silu(og)


def even_mixer(x, w_in, w_out, rg_conv_w, rg_conv_b, rg_wa, rg_wx, rg_ba, rg_bx, rg_lambda,
               ml_conv_w, ml_conv_b, ml_i_bias, ml_f_bias, ml_norm_g):
    p = (x @ w_in).astype(jnp.float32)
    rg_x, rg_g, ml_qk, ml_v, ml_o, ml_if = split_cols(p, EVEN_SIZES)
    y_rg = rg_lru_mixer(rg_x, rg_g, rg_conv_w, rg_conv_b, rg_wa, rg_wx, rg_ba, rg_bx, rg_lambda)
    y_ml = mlstm_mixer(ml_qk, ml_v, ml_o, ml_if, ml_conv_w, ml_conv_b, ml_i_bias, ml_f_bias, ml_norm_g)
    return jnp.concatenate([y_rg, y_ml], axis=-1).astype(x.dtype) @ w_out


def odd_mixer(x, w_in, w_out, rk_mu, rk_w0, rk_wB, rk_a0, rk_aB, rk_gB, rk_k_k, rk_k_a, rk_r_k,
              rk_ln_g, rk_ln_b, gla_gB, gla_gb, gla_norm_g):
    p = (x @ w_in).astype(jnp.float32)
    p_rk, p_gla = p[..., :RK_IN], p[..., RK_IN:]
    y_rk = rwkv7_mixer(p_rk, rk_mu, rk_w0, rk_wB, rk_a0, rk_aB, rk_gB, rk_k_k, rk_k_a, rk_r_k, rk_ln_g, rk_ln_b)
    gq, gk, gv, ggd, gog = split_cols(p_gla, GLA_SIZES)
    y_gla = gla_mixer(gq, gk, gv, ggd, gog, gla_gB, gla_gb, gla_norm_g)
    return jnp.concatenate([y_rk, y_gla], axis=-1).astype(x.dtype) @ w_out


def setup_inputs(seed: int = 0) -> dict:
    key = jax.random.key(seed)
    ks = iter(jax.random.split(key, 64))
    nrm = lambda shape, scale: scale * jax.random.normal(next(ks), shape, jnp.float32)
    unif = lambda shape, lo, hi: jax.random.uniform(next(ks), shape, jnp.float32, lo, hi)
    NE, NO = N_EVEN, N_ODD
    beta = DEEPNORM_BETA
    s_rg = unif((NE, RG_WIDTH), 0.9, 0.999) ** (1.0 / RG_C)
    return {
        "x": nrm((BATCH, SEQ, D_MODEL), 1.0),
        "ffn1_wi": nrm((DEPTH, D_MODEL, 2 * D_FF), D_MODEL ** -0.5),
        "ffn1_wo": nrm((DEPTH, D_FF, D_MODEL), beta * D_FF ** -0.5),
        "ffn2_wi": nrm((DEPTH, D_MODEL, 2 * D_FF), D_MODEL ** -0.5),
        "ffn2_wo": nrm((DEPTH, D_FF, D_MODEL), beta * D_FF ** -0.5),
        "ln_g": 1.0 + nrm((DEPTH, 3, D_MODEL), 0.02),
        "ln_b": nrm((DEPTH, 3, D_MODEL), 0.02),
        "ev_w_in": nrm((NE, D_MODEL, EVEN_IN), D_MODEL ** -0.5),
        "ev_w_out": nrm((NE, MIX_WIDTH, D_MODEL), beta * MIX_WIDTH ** -0.5),
        "rg_conv_w": nrm((NE, CONV_WIDTH, RG_WIDTH), CONV_WIDTH ** -0.5),
        "rg_conv_b": nrm((NE, RG_WIDTH), 0.02),
        "rg_wa": nrm((NE, RG_BLOCKS, RG_BLK, RG_BLK), RG_BLK ** -0.5),
        "rg_wx": nrm((NE, RG_BLOCKS, RG_BLK, RG_BLK), RG_BLK ** -0.5),
        "rg_ba": nrm((NE, RG_WIDTH), 0.02),
        "rg_bx": nrm((NE, RG_WIDTH), 0.02),
        "rg_lambda": jnp.log(s_rg) - jnp.log1p(-s_rg),
        "ml_conv_w": nrm((NE, CONV_WIDTH, 2 * HALF), CONV_WIDTH ** -0.5),
        "ml_conv_b": nrm((NE, 2 * HALF), 0.02),
        "ml_i_bias": nrm((NE, ML_HEADS), 0.1),
        "ml_f_bias": jnp.linspace(3.0, 6.0, ML_HEADS)[None, :] + nrm((NE, ML_HEADS), 0.1),
        "ml_norm_g": 1.0 + nrm((NE, HALF), 0.02),
        "od_w_in": nrm((NO, D_MODEL, ODD_IN), D_MODEL ** -0.5),
        "od_w_out": nrm((NO, MIX_WIDTH, D_MODEL), beta * MIX_WIDTH ** -0.5),
        "rk_mu": unif((NO, RK_IN), 0.0, 1.0),
        "rk_w0": jnp.linspace(-6.5, -1.5, RK_WIDTH)[None, :] + nrm((NO, RK_WIDTH), 0.1),
        "rk_wB": nrm((NO, RK_DECAY_LORA, RK_WIDTH), 0.1 * RK_DECAY_LORA ** -0.5),
        "rk_a0": nrm((NO, RK_WIDTH), 0.1),
        "rk_aB": nrm((NO, RK_A_LORA, RK_WIDTH), 0.1 * RK_A_LORA ** -0.5),
        "rk_gB": nrm((NO, RK_G_LORA, RK_WIDTH), RK_G_LORA ** -0.5),
        "rk_k_k": 0.85 + nrm((NO, RK_WIDTH), 0.02),
        "rk_k_a": 1.0 + nrm((NO, RK_WIDTH), 0.02),
        "rk_r_k": nrm((NO, RK_HEADS, RK_DH), 0.1),
        "rk_ln_g": 1.0 + nrm((NO, RK_WIDTH), 0.02),
        "rk_ln_b": nrm((NO, RK_WIDTH), 0.02),
        "gla_gB": nrm((NO, GLA_GATE_LORA, GLA_HEADS * GLA_DK), GLA_GATE_LORA ** -0.5),
        "gla_gb": nrm((NO, GLA_HEADS * GLA_DK), 0.1),
        "gla_norm_g": 1.0 + nrm((NO, HALF), 0.02),
    }


def reference(x, ffn1_wi, ffn1_wo, ffn2_wi, ffn2_wo, ln_g, ln_b,
              ev_w_in, ev_w_out, rg_conv_w, rg_conv_b, rg_wa, rg_wx, rg_ba, rg_bx, rg_lambda,
              ml_conv_w, ml_conv_b, ml_i_bias, ml_f_bias, ml_norm_g,
              od_w_in, od_w_out, rk_mu, rk_w0, rk_wB, rk_a0, rk_aB, rk_gB, rk_k_k, rk_k_a, rk_r_k,
              rk_ln_g, rk_ln_b, gla_gB, gla_gb, gla_norm_g):
    alpha = DEEPNORM_ALPHA
    for l in range(DEPTH):
        x = layer_norm(alpha * x + 0.5 * swiglu_ffn(x, ffn1_wi[l], ffn1_wo[l]), ln_g[l, 0], ln_b[l, 0])
        if l % 2 == 0:
            e = l // 2
            mix = even_mixer(x, ev_w_in[e], ev_w_out[e], rg_conv_w[e], rg_conv_b[e], rg_wa[e], rg_wx[e],
                             rg_ba[e], rg_bx[e], rg_lambda[e], ml_conv_w[e], ml_conv_b[e],
                             ml_i_bias[e], ml_f_bias[e], ml_norm_g[e])
        else:
            o = l // 2
            mix = odd_mixer(x, od_w_in[o], od_w_out[o], rk_mu[o], rk_w0[o], rk_wB[o], rk_a0[o], rk_aB[o],
                            rk_gB[o], rk_k_k[o], rk_k_a[o], rk_r_k[o], rk_ln_g[o], rk_ln_b[o],
                            gla_gB[o], gla_gb[o], gla_norm_g[o])
        x = layer_norm(alpha * x + mix, ln_g[l, 1], ln_b[l, 1])
        x = layer_norm(alpha * x + 0.5 * swiglu_ffn(x, ffn2_wi[l], ffn2_wo[l]), ln_g[l, 2], ln_b[l, 2])
    return x
```

```python
import contextlib
import math
import numpy as np
import ml_dtypes
import concourse.bass as bass
import concourse.mybir as mybir
from concourse.bass_utils import run_bass_kernel_spmd

F32 = mybir.dt.float32
BF16 = mybir.dt.bfloat16
AF = mybir.ActivationFunctionType
ALU = mybir.AluOpType
AX = mybir.AxisListType

D = 1024
DEPTH = 4
DFF = 2816
NCORES = 8
ALPHA = (2.0 * DEPTH) ** 0.25
LN_EPS = 1e-5
EVEN_IN = 3080
ODD_IN = 3216
ENGS = ["pe", "act", "dve", "pool", "sp"]


class Tok:
    __slots__ = ("name", "lw", "rd", "ds", "x")

    def __init__(self, name):
        self.name = name
        self.x = name.startswith("p") and not name.startswith("prm")
        self.lw = None
        self.rd = {}
        self.ds = None


class Op:
    __slots__ = ("eng", "fn", "waits", "signal", "sigval", "dma", "ds", "idx")

    def __init__(self, eng, fn, dma):
        self.eng = eng
        self.fn = fn
        self.dma = dma
        self.waits = []
        self.signal = False
        self.sigval = None
        self.ds = None


class Prog:
    def __init__(self, nc):
        self.nc = nc
        self.ges = contextlib.ExitStack()
        self.esem = {e: self.ges.enter_context(nc.semaphore("se_" + e)) for e in ENGS}
        self.ecnt = {e: 0 for e in ENGS}
        self.dfree = []
        self.dall = []
        self.nphase = 0
        self.ninst = 0

    def begin(self, name):
        self.pname = name
        self.ops = {e: [] for e in ENGS}
        self.toks = []
        self.pes = contextlib.ExitStack()
        self.nt = 0

    def tile(self, shape, dtype=F32, name=None):
        self.nt += 1
        nm = "%s_%s_%d" % (self.pname, name or "t", self.nt)
        return self.pes.enter_context(self.nc.sbuf_tensor(nm, list(shape), dtype))

    def psum(self, shape, dtype=F32, name=None):
        self.nt += 1
        nm = "%s_%s_%d" % (self.pname, name or "p", self.nt)
        return self.pes.enter_context(self.nc.psum_tensor(nm, list(shape), dtype))

    def tok(self, name="t"):
        t = Tok(name)
        self.toks.append(t)
        return t

    def toks_n(self, n, name="t"):
        return [self.tok("%s%d" % (name, i)) for i in range(n)]

    def _dsem(self, t):
        if t.ds is None:
            if self.dfree:
                t.ds = self.dfree.pop()
            else:
                s = self.ges.enter_context(self.nc.semaphore("sd_%d" % len(self.dall)))
                t.ds = [s, 0]
                self.dall.append(t.ds)
        return t.ds

    @staticmethod
    def _need(p, o, raw):
        if p.dma or o.dma:
            return True
        if p.eng != o.eng:
            return True
        if p.eng == "pe":
            return False
        return raw

    def op(self, eng, fn, r=(), w=(), dma=False, sig=None):
        o = Op(eng, fn, dma)
        deps = []
        w = list(w) + [t for t in r if t.x and not any(t is q for q in w)]
        for t in r:
            if t.lw is not None and self._need(t.lw, o, True):
                deps.append(t.lw)
        for t in w:
            if t.lw is not None and self._need(t.lw, o, False):
                deps.append(t.lw)
            for q in t.rd.values():
                if q is not o and self._need(q, o, False):
                    deps.append(q)
        seen = set()
        for p in deps:
            if id(p) in seen:
                continue
            seen.add(id(p))
            if p.dma:
                o.waits.append(("sem", p.ds[0], p.ds[1]))
            else:
                p.signal = True
                o.waits.append(("op", p))
        if dma:
            st = sig if sig is not None else (w[0] if (w and w[0] is not None) else r[0])
            o.ds = self._dsem(st)
            o.ds[1] += 16
        for t in r:
            key = ("dma", id(o.ds)) if dma else eng
            t.rd[key] = o
        for t in w:
            t.lw = o
            t.rd = {}
        self.ops[eng].append(o)
        return o

    def pe(self, fn, r=(), w=()):
        return self.op("pe", fn, r, w)

    def act(self, fn, r=(), w=()):
        return self.op("act", fn, r, w)

    def dve(self, fn, r=(), w=()):
        return self.op("dve", fn, r, w)

    def pool(self, fn, r=(), w=()):
        return self.op("pool", fn, r, w)

    def dma(self, eng, out, in_, r=(), w=(), sig=None):
        return self.op(eng, lambda e: e.dma_start(out=out, in_=in_), r, w, dma=True, sig=sig)

    def end(self):
        nc = self.nc
        last = {}
        for e in ENGS:
            if self.ops[e]:
                lo = None
                for o in reversed(self.ops[e]):
                    if not o.dma:
                        lo = o
                        break
                if lo is not None:
                    lo.signal = True
                    last[e] = lo
        for e in ENGS:
            for o in self.ops[e]:
                if o.signal and not o.dma:
                    self.ecnt[e] += 1
                    o.sigval = self.ecnt[e]
        used_ds = []
        for t in self.toks:
            if t.ds is not None and not any(t.ds is u for u in used_ds):
                used_ds.append(t.ds)
        final_waits = [(self.esem[e], last[e].sigval) for e in last]
        final_waits += [(ds[0], ds[1]) for ds in used_ds]
        prog = self

        def emit(engname, e):
            waited = {}
            for o in prog.ops[engname]:
                for wv in o.waits:
                    if wv[0] == "op":
                        sem, val = prog.esem[wv[1].eng], wv[1].sigval
                    else:
                        sem, val = wv[1], wv[2]
                    k = id(sem)
                    if waited.get(k, -1) >= val:
                        continue
                    e.wait_ge(sem, val)
                    waited[k] = val
                ins = o.fn(e)
                prog.ninst += 1
                if o.dma:
                    ins.then_inc(o.ds[0], 16)
                elif o.signal:
                    ins.then_inc(prog.esem[engname], 1)
            for sem, val in final_waits:
                if sem is prog.esem[engname]:
                    continue
                if waited.get(id(sem), -1) >= val:
                    continue
                e.wait_ge(sem, val)

        with nc.Block() as block:
            @block.tensor
            def _(e):
                emit("pe", e)

            @block.scalar
            def _(e):
                emit("act", e)

            @block.vector
            def _(e):
                emit("dve", e)

            @block.gpsimd
            def _(e):
                emit("pool", e)

            @block.sync
            def _(e):
                emit("sp", e)
        for ds in used_ds:
            self.dfree.append(ds)
        self.pes.close()
        self.nphase += 1


def make_consts():
    c = {}
    c["ident"] = np.eye(128, dtype=np.float32)
    j = np.arange(128)[:, None]
    i = np.arange(128)[None, :]
    c["mask_le"] = (j <= i).astype(np.float32)
    c["mask_lt"] = (j < i).astype(np.float32)
    c["ones"] = np.ones((128, 128), np.float32)
    c["blk64"] = ((j // 64) == (i // 64)).astype(np.float32)
    c["mask_gt"] = (j > i).astype(np.float32)
    c["blkind"] = ((j // 64) == i).astype(np.float32)[:, :128]
    c["zeros"] = np.zeros((128, 128), np.float32)
    return np.concatenate([c[k] for k in ["ident", "mask_le", "mask_lt", "ones", "blk64", "mask_gt", "blkind", "zeros"]], axis=1)


C_IDENT, C_LE, C_LT, C_ONES, C_BLK64, C_GT, C_BLKIND, C_ZEROS = 0, 128, 256, 384, 512, 640, 768, 896
NCONST = 1024


class Ctx:
    pass


def ffn_phase(P, cx, name, x_in, x_out, wi_d, wo_d, g_d, b_d, T):
    nc = P.nc
    P.begin(name)
    TT = 256
    NS = TT // 128
    ntile = T // TT
    NG = DFF // 128
    wi = P.tile([128, 8, 2 * DFF], BF16, "wi")
    wo = P.tile([128, NG, D], BF16, "wo")
    gb = P.tile([128, 2, D], F32, "gb")
    cst = P.tile([128, NCONST], F32, "cst")
    xt = [P.tile([128, NS, D], F32, "xt") for _ in range(2)]
    xT = [P.tile([128, 8, TT], BF16, "xT") for _ in range(2)]
    hT = P.tile([128, NG, TT], BF16, "hT")
    sg = [P.tile([128, TT], F32, "sg") for _ in range(2)]
    z = [P.tile([128, D], F32, "z") for _ in range(2)]
    stat = [P.tile([128, 16], F32, "stat") for _ in range(2)]
    p_tr = [P.psum([128, 512], F32, "ptr") for _ in range(2)]
    p_gu = [P.psum([128, 512], F32, "pgu") for _ in range(2)]
    p_y = [P.psum([128, 512], F32, "py") for _ in range(4)]
    t_wi, t_wo, t_gb, t_cst = P.tok("wi"), P.tok("wo"), P.tok("gb"), P.tok("cst")
    t_xt, t_xT = P.toks_n(2, "xt"), P.toks_n(2, "xT")
    t_hT = P.toks_n(NG, "hT")
    t_sg, t_z, t_stat = P.toks_n(2, "sg"), P.toks_n(2, "z"), P.toks_n(2, "stat")
    t_ptr, t_pgu, t_py = P.toks_n(2, "ptr"), P.toks_n(2, "pgu"), P.toks_n(4, "py")

    P.dma("sp", cst[:], cx.consts, w=[t_cst])
    P.dma("sp", gb[:, 0, :], g_d.partition_broadcast(128), w=[t_gb])
    P.dma("sp", gb[:, 1, :], b_d.partition_broadcast(128), w=[t_gb])
    wi_v = wi_d.rearrange("(ko ki) n -> ki ko n", ki=128)
    wo_v = wo_d.rearrange("(ko ki) n -> ki ko n", ki=128)

    def load_x(i):
        sl = i % 2
        P.dma("sp", xt[sl][:], x_in[i * TT:(i + 1) * TT, :].rearrange("(s p) d -> p s d", p=128), w=[t_xt[sl]])

    load_x(0)
    for ko in range(8):
        P.dma("pool", wi[:, ko, :], wi_v[:, ko, :], w=[t_wi])
    for k0 in range(0, NG, 2):
        P.dma("pool", wo[:, k0:k0 + 2, :], wo_v[:, k0:k0 + 2, :], w=[t_wo])
    ident = cst[:, C_IDENT:C_IDENT + 128]
    c_y = 0.5 / ALPHA
    eps2 = LN_EPS / (ALPHA * ALPHA)
    ntr = 0
    ngu = 0
    nyb = 0
    for i in range(ntile):
        sl = i % 2
        if i + 1 < ntile:
            load_x(i + 1)
        for s in range(NS):
            for g4 in range(2):
                pb = ntr % 2
                ntr += 1
                for q in range(4):
                    kc = g4 * 4 + q
                    P.pe(lambda e, pb=pb, q=q, s=s, kc=kc, sl=sl: e.transpose(
                        p_tr[pb][:, q * 128:(q + 1) * 128], xt[sl][:, s, kc * 128:(kc + 1) * 128], ident),
                        r=[t_xt[sl], t_cst], w=[t_ptr[pb]])
                src = p_tr[pb][:].rearrange("p (a b) -> p a b", a=4)
                dst = xT[sl][:, g4 * 4:(g4 + 1) * 4, s * 128:(s + 1) * 128]
                if (s + g4) % 2 == 0:
                    P.act(lambda e, dst=dst, src=src: e.copy(dst, src), r=[t_ptr[pb]], w=[t_xT[sl]])
                else:
                    P.dve(lambda e, dst=dst, src=src: e.tensor_copy(dst, src), r=[t_ptr[pb]], w=[t_xT[sl]])
        for g in range(NG):
            pb = ngu % 2
            ngu += 1
            for half, col0 in ((0, g * 128), (1, DFF + g * 128)):
                for kc in range(8):
                    P.pe(lambda e, pb=pb, half=half, col0=col0, kc=kc, sl=sl: e.matmul(
                        p_gu[pb][:, half * TT:(half + 1) * TT], wi[:, kc, col0:col0 + 128], xT[sl][:, kc, :],
                        start=(kc == 0), stop=(kc == 7)),
                        r=[t_wi, t_xT[sl]], w=[t_pgu[pb]])
            P.act(lambda e, pb=pb: e.activation(sg[pb][:], p_gu[pb][:, 0:TT], AF.Silu),
                  r=[t_pgu[pb]], w=[t_sg[pb]])
            P.dve(lambda e, pb=pb, g=g: e.tensor_tensor(hT[:, g, :], p_gu[pb][:, TT:2 * TT], sg[pb][:], ALU.mult),
                  r=[t_pgu[pb], t_sg[pb]], w=[t_hT[g]])
        for s in range(NS):
            zs = nyb % 2
            banks = [(2 * (nyb % 2)), (2 * (nyb % 2) + 1)]
            nyb += 1
            for half in range(2):
                pb = banks[half]
                for g in range(NG):
                    P.pe(lambda e, pb=pb, g=g, s=s, half=half: e.matmul(
                        p_y[pb][:], hT[:, g, s * 128:(s + 1) * 128], wo[:, g, half * 512:(half + 1) * 512],
                        start=(g == 0), stop=(g == NG - 1)),
                        r=[t_wo, t_hT[g]], w=[t_py[pb]])
            for half in range(2):
                pb = banks[half]
                P.dve(lambda e, pb=pb, zs=zs, half=half, s=s, sl=sl: e.scalar_tensor_tensor(
                    z[zs][:, half * 512:(half + 1) * 512], p_y[pb][:], c_y, xt[sl][:, s, half * 512:(half + 1) * 512],
                    ALU.mult, ALU.add),
                    r=[t_py[pb], t_xt[sl]], w=[t_z[zs]])
            ln_apply(P, z[zs], t_z[zs], stat[zs], t_stat[zs], gb, t_gb, eps2)
            r0 = i * TT + s * 128
            P.dma("sp", x_out[r0:r0 + 128, :], z[zs][:], r=[t_z[zs]], sig=t_z[zs])
    P.end()


def ln_apply(P, z, t_z, st, t_st, gb, t_gb, eps):
    for h in range(2):
        P.dve(lambda e, h=h: e.bn_stats(st[:, h * 6:(h + 1) * 6], z[:, h * 512:(h + 1) * 512]),
              r=[t_z], w=[t_st])
    P.dve(lambda e: e.bn_aggr(st[:, 12:14], st[:, 0:12]), r=[t_st], w=[t_st])
    P.act(lambda e: e.activation(st[:, 14:15], st[:, 13:14], AF.Sqrt, bias=eps_ap(P, eps)), r=[t_st], w=[t_st])
    P.dve(lambda e: e.reciprocal(st[:, 15:16], st[:, 14:15]), r=[t_st], w=[t_st])
    P.dve(lambda e: e.tensor_scalar(z[:], z[:], st[:, 12:13], st[:, 15:16], ALU.subtract, ALU.mult),
          r=[t_z, t_st], w=[t_z])
    P.pool(lambda e: e.tensor_tensor(z[:], z[:], gb[:, 0, :], ALU.mult), r=[t_z, t_gb], w=[t_z])
    P.pool(lambda e: e.tensor_tensor(z[:], z[:], gb[:, 1, :], ALU.add), r=[t_z, t_gb], w=[t_z])


def eps_ap(P, eps):
    return eps


def build_program(T, layers=DEPTH, only=None):
    nc = bass.Bass("TRN2", target_bir_lowering=False)
    cx = Ctx()
    cx.T = T
    dt = nc.dram_tensor
    cx.x = dt("x", [T, D], F32, kind="ExternalInput").ap()
    cx.consts = dt("consts", [128, NCONST], F32, kind="ExternalInput").ap()
    cx.ffn1_wi = dt("ffn1_wi", [DEPTH, D, 2 * DFF], F32, kind="ExternalInput").ap()
    cx.ffn1_wo = dt("ffn1_wo", [DEPTH, DFF, D], F32, kind="ExternalInput").ap()
    cx.ffn2_wi = dt("ffn2_wi", [DEPTH, D, 2 * DFF], F32, kind="ExternalInput").ap()
    cx.ffn2_wo = dt("ffn2_wo", [DEPTH, DFF, D], F32, kind="ExternalInput").ap()
    cx.ln_g = dt("ln_g", [DEPTH, 3, D], F32, kind="ExternalInput").ap()
    cx.ln_b = dt("ln_b", [DEPTH, 3, D], F32, kind="ExternalInput").ap()
    cx.out = dt("out", [T, D], F32, kind="ExternalOutput").ap()
    cx.xa = dt("xa", [T, D], F32).ap()
    cx.xb = dt("xb", [T, D], F32).ap()
    P = Prog(nc)
    cur = cx.x
    if only == "ffn":
        ffn_phase(P, cx, "f1", cur, cx.out, cx.ffn1_wi[0], cx.ffn1_wo[0], cx.ln_g[0, 0], cx.ln_b[0, 0], T)
    P.ges.close()
    cx.P = P
    return nc, cx


def colvec(ap1d):
    return ap1d.rearrange("(p o) -> p o", o=1)


class XLoader:
    def __init__(self, P, x_in, TT, cst, t_cst, want_T=True):
        self.P, self.x_in, self.TT = P, x_in, TT
        self.NS = TT // 128
        self.cst, self.t_cst = cst, t_cst
        self.xt = [P.tile([128, self.NS, D], F32, "xt") for _ in range(2)]
        self.t_xt = P.toks_n(2, "xt")
        self.want_T = want_T
        if want_T:
            self.xT = [P.tile([128, 8, TT], BF16, "xT") for _ in range(2)]
            self.t_xT = P.toks_n(2, "xT")
            self.p_tr = [P.psum([128, 512], F32, "ptr") for _ in range(2)]
            self.t_ptr = P.toks_n(2, "ptr")
        self.ntr = 0

    def load(self, i):
        sl = i % 2
        TT = self.TT
        self.P.dma("sp", self.xt[sl][:], self.x_in[i * TT:(i + 1) * TT, :].rearrange("(s p) d -> p s d", p=128),
                   w=[self.t_xt[sl]])

    def transpose(self, i):
        P = self.P
        sl = i % 2
        ident = self.cst[:, C_IDENT:C_IDENT + 128]
        for s in range(self.NS):
            for g4 in range(2):
                pb = self.ntr % 2
                self.ntr += 1
                for q in range(4):
                    kc = g4 * 4 + q
                    P.pe(lambda e, pb=pb, q=q, s=s, kc=kc, sl=sl: e.transpose(
                        self.p_tr[pb][:, q * 128:(q + 1) * 128], self.xt[sl][:, s, kc * 128:(kc + 1) * 128], ident),
                        r=[self.t_xt[sl], self.t_cst], w=[self.t_ptr[pb]])
                src = self.p_tr[pb][:].rearrange("p (a b) -> p a b", a=4)
                dst = self.xT[sl][:, g4 * 4:(g4 + 1) * 4, s * 128:(s + 1) * 128]
                if (s + g4) % 2 == 0:
                    P.act(lambda e, dst=dst, src=src: e.copy(dst, src), r=[self.t_ptr[pb]], w=[self.t_xT[sl]])
                else:
                    P.dve(lambda e, dst=dst, src=src: e.tensor_copy(dst, src), r=[self.t_ptr[pb]], w=[self.t_xT[sl]])


def proj_phase(P, cx, name, x_in, w_d, ncol, fgroups, tgroups, pf, pt, T):
    P.begin(name)
    TT = 256
    ntile = T // TT
    w = P.tile([128, 8, ncol], BF16, "w")
    cst = P.tile([128, NCONST], F32, "cst")
    t_w, t_cst = P.tok("w"), P.tok("cst")
    P.dma("sp", cst[:], cx.consts, w=[t_cst])
    xl = XLoader(P, x_in, TT, cst, t_cst)
    xl.load(0)
    w_v = w_d.rearrange("(ko ki) n -> ki ko n", ki=128)
    for ko in range(8):
        P.dma("pool", w[:, ko, :], w_v[:, ko, :], w=[t_w])
    NPS = 4
    pp = [P.psum([128, 512], F32, "pp") for _ in range(NPS)]
    t_pp = P.toks_n(NPS, "pp")
    NST = 4
    stg = [P.tile([128, 512], F32, "stg") for _ in range(NST)]
    t_stg = P.toks_n(NST, "stg")
    n = 0
    for i in range(ntile):
        sl = i % 2
        if i + 1 < ntile:
            xl.load(i + 1)
        xl.transpose(i)
        xT, t_xT = xl.xT[sl], xl.t_xT[sl]
        for (col0, m, row0) in fgroups:
            pb, sb = n % NPS, n % NST
            n += 1
            for kc in range(8):
                P.pe(lambda e, pb=pb, kc=kc, col0=col0, m=m, xT=xT: e.matmul(
                    pp[pb][0:m, 0:TT], w[:, kc, col0:col0 + m], xT[:, kc, :], start=(kc == 0), stop=(kc == 7)),
                    r=[t_w, t_xT], w=[t_pp[pb]])
            if n % 2 == 0:
                P.act(lambda e, pb=pb, sb=sb, m=m: e.copy(stg[sb][0:m, 0:TT], pp[pb][0:m, 0:TT]),
                      r=[t_pp[pb]], w=[t_stg[sb]])
            else:
                P.dve(lambda e, pb=pb, sb=sb, m=m: e.tensor_copy(stg[sb][0:m, 0:TT], pp[pb][0:m, 0:TT]),
                      r=[t_pp[pb]], w=[t_stg[sb]])
            P.dma("sp", pf[row0:row0 + m, i * TT:(i + 1) * TT], stg[sb][0:m, 0:TT], r=[t_stg[sb]], sig=t_stg[sb])
        for s in range(TT // 128):
            for (col0, nn, dst0) in tgroups:
                pb, sb = n % NPS, n % NST
                n += 1
                for kc in range(8):
                    P.pe(lambda e, pb=pb, kc=kc, col0=col0, nn=nn, s=s, xT=xT: e.matmul(
                        pp[pb][:, 0:nn], xT[:, kc, s * 128:(s + 1) * 128], w[:, kc, col0:col0 + nn],
                        start=(kc == 0), stop=(kc == 7)),
                        r=[t_w, t_xT], w=[t_pp[pb]])
                if n % 2 == 0:
                    P.act(lambda e, pb=pb, sb=sb, nn=nn: e.copy(stg[sb][:, 0:nn], pp[pb][:, 0:nn]),
                          r=[t_pp[pb]], w=[t_stg[sb]])
                else:
                    P.dve(lambda e, pb=pb, sb=sb, nn=nn: e.tensor_copy(stg[sb][:, 0:nn], pp[pb][:, 0:nn]),
                          r=[t_pp[pb]], w=[t_stg[sb]])
                r0 = i * TT + s * 128
                P.dma("sp", pt[r0:r0 + 128, dst0:dst0 + nn], stg[sb][:, 0:nn], r=[t_stg[sb]], sig=t_stg[sb])
    P.end()


def outproj_phase(P, cx, name, x_in, x_out, w_d, yf, nf, yt, ntk, g_d, b_d, T):
    P.begin(name)
    TT = 256
    NS = TT // 128
    ntile = T // TT
    w = P.tile([128, 8, D], BF16, "w")
    gb = P.tile([128, 2, D], F32, "gb")
    cst = P.tile([128, NCONST], F32, "cst")
    idb = P.tile([128, 128], BF16, "idb")
    t_w, t_gb, t_cst, t_idb = P.tok("w"), P.tok("gb"), P.tok("cst"), P.tok("idb")
    P.dma("sp", cst[:], cx.consts, w=[t_cst])
    P.dma("sp", gb[:, 0, :], g_d.partition_broadcast(128), w=[t_gb])
    P.dma("sp", gb[:, 1, :], b_d.partition_broadcast(128), w=[t_gb])
    P.act(lambda e: e.copy(idb[:], cst[:, C_IDENT:C_IDENT + 128]), r=[t_cst], w=[t_idb])
    xl = XLoader(P, x_in, TT, cst, t_cst, want_T=False)
    w_v = w_d.rearrange("(ko ki) n -> ki ko n", ki=128)
    for ko in range(8):
        P.dma("pool", w[:, ko, :], w_v[:, ko, :], w=[t_w])
    nfc = nf // 128
    ntc = ntk // 128
    yT = [P.tile([128, 8, TT], BF16, "yT") for _ in range(2)]
    t_yT = P.toks_n(2, "yT")
    ytk = [P.tile([128, NS, max(ntk, 128)], BF16, "ytk") for _ in range(2)]
    t_ytk = P.toks_n(2, "ytk")
    p_tr = [P.psum([128, 512], BF16, "ptr") for _ in range(2)]
    t_ptr = P.toks_n(2, "ptr")
    p_y = [P.psum([128, 512], F32, "py") for _ in range(4)]
    t_py = P.toks_n(4, "py")
    z = [P.tile([128, D], F32, "z") for _ in range(2)]
    stat = [P.tile([128, 16], F32, "stat") for _ in range(2)]
    t_z, t_stat = P.toks_n(2, "z"), P.toks_n(2, "stat")
    c_y = 1.0 / ALPHA
    eps2 = LN_EPS / (ALPHA * ALPHA)

    def load(i):
        sl = i % 2
        xl.load(i)
        if nfc:
            P.dma("sp", yT[sl][:, 0:nfc, :], yf[:, i * TT:(i + 1) * TT].rearrange("(a p) t -> p a t", p=128),
                  w=[t_yT[sl]])
        P.dma("sp", ytk[sl][:, :, 0:ntk], yt[i * TT:(i + 1) * TT, :].rearrange("(s p) c -> p s c", p=128),
              w=[t_ytk[sl]])

    load(0)
    ntr = 0
    nyb = 0
    for i in range(ntile):
        sl = i % 2
        if i + 1 < ntile:
            load(i + 1)
        for s in range(NS):
            for g4 in range(ntc // 4):
                pb = ntr % 2
                ntr += 1
                for q in range(4):
                    kc = g4 * 4 + q
                    P.pe(lambda e, pb=pb, q=q, s=s, kc=kc, sl=sl: e.transpose(
                        p_tr[pb][:, q * 128:(q + 1) * 128], ytk[sl][:, s, kc * 128:(kc + 1) * 128], idb[:]),
                        r=[t_ytk[sl], t_idb], w=[t_ptr[pb]])
                src = p_tr[pb][:].rearrange("p (a b) -> p a b", a=4)
                dst = yT[sl][:, nfc + g4 * 4:nfc + (g4 + 1) * 4, s * 128:(s + 1) * 128]
                if g4 % 2 == 0:
                    P.act(lambda e, dst=dst, src=src: e.copy(dst, src), r=[t_ptr[pb]], w=[t_yT[sl]])
                else:
                    P.dve(lambda e, dst=dst, src=src: e.tensor_copy(dst, src), r=[t_ptr[pb]], w=[t_yT[sl]])
        for s in range(NS):
            zs = nyb % 2
            banks = [(2 * (nyb % 2)), (2 * (nyb % 2) + 1)]
            nyb += 1
            for half in range(2):
                pb = banks[half]
                for kc in range(8):
                    P.pe(lambda e, pb=pb, kc=kc, s=s, half=half, sl=sl: e.matmul(
                        p_y[pb][:], yT[sl][:, kc, s * 128:(s + 1) * 128], w[:, kc, half * 512:(half + 1) * 512],
                        start=(kc == 0), stop=(kc == 7)),
                        r=[t_w, t_yT[sl]], w=[t_py[pb]])
            for half in range(2):
                pb = banks[half]
                P.dve(lambda e, pb=pb, zs=zs, half=half, s=s, sl=sl: e.scalar_tensor_tensor(
                    z[zs][:, half * 512:(half + 1) * 512], p_y[pb][:], c_y,
                    xl.xt[sl][:, s, half * 512:(half + 1) * 512], ALU.mult, ALU.add),
                    r=[t_py[pb], xl.t_xt[sl]], w=[t_z[zs]])
            ln_apply(P, z[zs], t_z[zs], stat[zs], t_stat[zs], gb, t_gb, eps2)
            r0 = i * TT + s * 128
            P.dma("sp", x_out[r0:r0 + 128, :], z[zs][:], r=[t_z[zs]], sig=t_z[zs])
    P.end()


def head_norm_tm(P, src, t_src, nh, dh, eps, st, t_st, engs=("dve",)):
    for h in range(nh):
        P.dve(lambda e, h=h: e.bn_stats(st[:, h * 6:(h + 1) * 6], src[:, h * dh:(h + 1) * dh]), r=[t_src], w=[t_st])
    mv0 = 6 * nh
    for h in range(nh):
        P.dve(lambda e, h=h: e.bn_aggr(st[:, mv0 + 2 * h:mv0 + 2 * h + 2], st[:, h * 6:(h + 1) * 6]),
              r=[t_st], w=[t_st])
    mv = st[:, mv0:mv0 + 2 * nh].rearrange("p (h two) -> p h two", two=2)
    sd0 = mv0 + 2 * nh
    P.act(lambda e: e.activation(st[:, sd0:sd0 + nh], mv[:, :, 1], AF.Sqrt, bias=eps), r=[t_st], w=[t_st])
    P.dve(lambda e: e.reciprocal(st[:, sd0 + nh:sd0 + 2 * nh], st[:, sd0:sd0 + nh]), r=[t_st], w=[t_st])
    for h in range(nh):
        P.dve(lambda e, h=h: e.tensor_scalar(src[:, h * dh:(h + 1) * dh], src[:, h * dh:(h + 1) * dh],
                                             st[:, mv0 + 2 * h:mv0 + 2 * h + 1],
                                             st[:, sd0 + nh + h:sd0 + nh + h + 1], ALU.subtract, ALU.mult),
              r=[t_src, t_st], w=[t_src])


GELU_C = 2.0 * math.sqrt(2.0 / math.pi)


def conv4(P, xp, t_xp, u, t_u, cw, t_cw, cb_col, n):
    P.act(lambda e: e.activation(u[:, 0:n], xp[:, 3:3 + n], AF.Identity, bias=cb_col, scale=cw[:, 3:4]),
          r=[t_xp, t_cw], w=[t_u])
    for j in range(3):
        P.dve(lambda e, j=j: e.scalar_tensor_tensor(u[:, 0:n], xp[:, j:j + n], cw[:, j:j + 1], u[:, 0:n],
                                                    ALU.mult, ALU.add), r=[t_xp, t_cw, t_u], w=[t_u])


def load_hist(P, xp, t_xp, src_rows, t0, n, seg_first, hist=3):
    if seg_first:
        P.dve(lambda e: e.memset(xp[:, 0:hist], 0.0), w=[t_xp])
        P.dma("sp", xp[:, hist:hist + n], src_rows[:, t0:t0 + n], w=[t_xp])
    else:
        P.dma("sp", xp[:, 0:hist + n], src_rows[:, t0 - hist:t0 + n], w=[t_xp])


def rglru_phase(P, cx, name, e_idx, pf, yf, T, S):
    P.begin(name)
    SEG = min(1024, S)
    nseg = S // SEG
    prm = P.tile([128, 4, 16], F32, "prm")
    t_prm = P.tok("prm")
    wbd = [P.tile([128, 2, 128], BF16, "wbd") for _ in range(4)]
    t_wbd = P.toks_n(4, "wbd")
    for c in range(4):
        for j in range(4):
            P.dma("sp", prm[:, c, j:j + 1], colvec(cx.rg_conv_w[e_idx, j, c * 128:(c + 1) * 128]), w=[t_prm])
        for k, v in ((4, cx.rg_conv_b), (5, cx.rg_ba), (6, cx.rg_bx), (7, cx.rg_lambda)):
            P.dma("sp", prm[:, c, k:k + 1], colvec(v[e_idx, c * 128:(c + 1) * 128]), w=[t_prm])
        P.dve(lambda e, c=c: e.memset(wbd[c][:], 0.0), w=[t_wbd[c]])
        for k, v in ((0, cx.rg_wa), (1, cx.rg_wx)):
            for b in range(2):
                P.dma("pool", wbd[c][b * 64:(b + 1) * 64, k, b * 64:(b + 1) * 64], v[e_idx, 2 * c + b], w=[t_wbd[c]])
    for c in range(4):
        P.act(lambda e, c=c: e.activation(prm[:, c, 10:11], prm[:, c, 7:8], AF.Exp, scale=-1.0), r=[t_prm], w=[t_prm])
        P.act(lambda e, c=c: e.activation(prm[:, c, 10:11], prm[:, c, 10:11], AF.Ln, bias=1.0), r=[t_prm], w=[t_prm])
        P.dve(lambda e, c=c: e.tensor_scalar(prm[:, c, 8:9], prm[:, c, 10:11], -8.0, None, ALU.mult), r=[t_prm], w=[t_prm])
        P.dve(lambda e, c=c: e.tensor_scalar(prm[:, c, 9:10], prm[:, c, 10:11], -16.0, None, ALU.mult), r=[t_prm], w=[t_prm])
    NB = 2
    mk = lambda nm, dt=F32, w=SEG: [P.tile([128, w], dt, nm) for _ in range(NB)]
    xp, u, ub, rr, ii, aa, bb, hh, xg, g1, yo = (mk("xp", F32, SEG + 3), mk("u"), mk("ub", BF16), mk("rr"), mk("ii"),
                                                 mk("aa"), mk("bb"), mk("hh"), mk("xg"), mk("g1"), mk("yo", BF16))
    tk = {k: P.toks_n(NB, k) for k in ["xp", "u", "ub", "rr", "ii", "aa", "bb", "hh", "xg", "g1", "yo"]}
    hprev = P.tile([128, 8], F32, "hprev")
    t_hp = P.tok("hp")
    pg = [P.psum([128, 512], F32, "pg") for _ in range(4)]
    t_pg = P.toks_n(4, "pg")
    npg = 0
    it = 0
    for sq in range(T // S):
        for c in range(4):
            for sg_i in range(nseg):
                b = it % NB
                it += 1
                t0 = sq * S + sg_i * SEG
                load_hist(P, xp[b], tk["xp"][b], pf[c * 128:(c + 1) * 128, :], t0, SEG, sg_i == 0)
                P.dma("sp", xg[b][:], pf[512 + c * 128:512 + (c + 1) * 128, t0:t0 + SEG], w=[tk["xg"][b]])
                conv4(P, xp[b], tk["xp"][b], u[b], tk["u"][b], prm[:, c, 0:4], t_prm, prm[:, c, 4:5], SEG)
                P.act(lambda e, b=b: e.copy(ub[b][:], u[b][:]), r=[tk["u"][b]], w=[tk["ub"][b]])
                for sp in range(SEG // 512 if SEG >= 512 else 1):
                    w_ = min(512, SEG)
                    cs = slice(sp * w_, (sp + 1) * w_)
                    for k, dst, tdst, bcol in ((0, rr, "rr", 5), (1, ii, "ii", 6)):
                        pb = npg % 4
                        npg += 1
                        P.pe(lambda e, pb=pb, k=k, c=c, b=b, cs=cs, w_=w_: e.matmul(
                            pg[pb][:, 0:w_], wbd[c][:, k, :], ub[b][:, cs], start=True, stop=True),
                            r=[t_wbd[c], tk["ub"][b]], w=[t_pg[pb]])
                        P.act(lambda e, pb=pb, dst=dst, b=b, cs=cs, w_=w_, c=c, bcol=bcol: e.activation(
                            dst[b][:, cs], pg[pb][:, 0:w_], AF.Sigmoid, bias=prm[:, c, bcol:bcol + 1]),
                            r=[t_pg[pb], t_prm], w=[tk[tdst][b]])
                P.act(lambda e, b=b, c=c: e.activation(aa[b][:], rr[b][:], AF.Exp, scale=prm[:, c, 8:9]),
                      r=[tk["rr"][b], t_prm], w=[tk["aa"][b]])
                P.act(lambda e, b=b, c=c: e.activation(bb[b][:], rr[b][:], AF.Exp, scale=prm[:, c, 9:10]),
                      r=[tk["rr"][b], t_prm], w=[tk["bb"][b]])
                P.act(lambda e, b=b: e.activation(bb[b][:], bb[b][:], AF.Sqrt, bias=1.0, scale=-1.0),
                      r=[tk["bb"][b]], w=[tk["bb"][b]])
                P.pool(lambda e, b=b: e.tensor_tensor(ii[b][:], ii[b][:], u[b][:], ALU.mult),
                       r=[tk["ii"][b], tk["u"][b]], w=[tk["ii"][b]])
                P.dve(lambda e, b=b: e.tensor_tensor(bb[b][:], bb[b][:], ii[b][:], ALU.mult),
                      r=[tk["bb"][b], tk["ii"][b]], w=[tk["bb"][b]])
                if sg_i == 0:
                    P.dve(lambda e, b=b: e.tensor_tensor_scan(hh[b][:], aa[b][:], bb[b][:], 0.0, ALU.mult, ALU.add),
                          r=[tk["aa"][b], tk["bb"][b]], w=[tk["hh"][b]])
                else:
                    P.dve(lambda e, b=b: e.tensor_tensor_scan(hh[b][:], aa[b][:], bb[b][:], hprev[:, 0:1],
                                                              ALU.mult, ALU.add),
                          r=[tk["aa"][b], tk["bb"][b], t_hp], w=[tk["hh"][b]])
                P.dve(lambda e, b=b: e.tensor_copy(hprev[:, 0:1], hh[b][:, SEG - 1:SEG]), r=[tk["hh"][b]], w=[t_hp])
                P.pool(lambda e, b=b: e.tensor_tensor(g1[b][:], xg[b][:], xg[b][:], ALU.mult),
                       r=[tk["xg"][b]], w=[tk["g1"][b]])
                P.pool(lambda e, b=b: e.tensor_scalar(g1[b][:], g1[b][:], 0.044715, 1.0, ALU.mult, ALU.add),
                       r=[tk["g1"][b]], w=[tk["g1"][b]])
                P.pool(lambda e, b=b: e.tensor_tensor(g1[b][:], g1[b][:], xg[b][:], ALU.mult),
                       r=[tk["g1"][b], tk["xg"][b]], w=[tk["g1"][b]])
                P.act(lambda e, b=b: e.activation(g1[b][:], g1[b][:], AF.Sigmoid, scale=GELU_C),
                      r=[tk["g1"][b]], w=[tk["g1"][b]])
                P.pool(lambda e, b=b: e.tensor_tensor(g1[b][:], g1[b][:], xg[b][:], ALU.mult),
                       r=[tk["g1"][b], tk["xg"][b]], w=[tk["g1"][b]])
                P.dve(lambda e, b=b: e.tensor_tensor(yo[b][:], g1[b][:], hh[b][:], ALU.mult),
                      r=[tk["g1"][b], tk["hh"][b]], w=[tk["yo"][b]])
                P.dma("sp", yf[c * 128:(c + 1) * 128, t0:t0 + SEG], yo[b][:], r=[tk["yo"][b]], sig=tk["yo"][b])
    P.end()


def mlstm_phase(P, cx, name, e_idx, pf, pt, yt, T, S):
    P.begin(name)
    H, DH = 4, 128
    SEG = min(1024, S)
    nseg = S // SEG
    NBK = SEG // 128
    cst = P.tile([128, NCONST], F32, "cst")
    t_cst = P.tok("cst")
    P.dma("sp", cst[:], cx.consts, w=[t_cst])
    idb = P.tile([128, 128], BF16, "idb")
    t_idb = P.tok("idb")
    P.act(lambda e: e.copy(idb[:], cst[:, C_IDENT:C_IDENT + 128]), r=[t_cst], w=[t_idb])
    mask_le = cst[:, C_LE:C_LE + 128]
    ones = cst[:, C_ONES:C_ONES + 128]
    prm = P.tile([128, 8, 8], F32, "prm")
    t_prm = P.tok("prm")
    for qh in range(8):
        for j in range(4):
            P.dma("sp", prm[:, qh, j:j + 1], colvec(cx.ml_conv_w[e_idx, j, qh * 128:(qh + 1) * 128]), w=[t_prm])
        P.dma("sp", prm[:, qh, 4:5], colvec(cx.ml_conv_b[e_idx, qh * 128:(qh + 1) * 128]), w=[t_prm])
    gbias = P.tile([128, 16], F32, "gbias")
    t_gbias = P.tok("gbias")
    P.dma("sp", gbias[:, 0:4], cx.ml_i_bias[e_idx].partition_broadcast(128), w=[t_gbias])
    P.dma("sp", gbias[:, 4:8], cx.ml_f_bias[e_idx].partition_broadcast(128), w=[t_gbias])
    P.dve(lambda e: e.tensor_scalar(gbias[:, 8:12], gbias[:, 4:8], -1.0, None, ALU.mult), r=[t_gbias], w=[t_gbias])
    ng = P.tile([128, 512], F32, "ng")
    t_ng = P.tok("ng")
    P.dma("sp", ng[:], cx.ml_norm_g[e_idx].partition_broadcast(128), w=[t_ng])
    QK = [P.tile([128, SEG], BF16, "QK") for _ in range(8)]
    t_QK = P.toks_n(8, "QK")
    xp = [P.tile([128, SEG + 3], F32, "xp") for _ in range(2)]
    uu = [P.tile([128, SEG], F32, "uu") for _ in range(2)]
    t_xp, t_uu = P.toks_n(2, "xp"), P.toks_n(2, "uu")
    ift = P.tile([128, NBK, 8], F32, "ift")
    gI = P.tile([128, NBK, 4], F32, "gI")
    gL = P.tile([128, NBK, 4], F32, "gL")
    gW = P.tile([128, NBK, 4], F32, "gW")
    gWb = P.tile([128, NBK, 4], BF16, "gWb")
    gE = P.tile([128, NBK, 4], F32, "gE")
    gD = P.tile([128, NBK, 4], F32, "gD")
    t_g = P.tok("gates")
    pG = P.psum([128, 512], F32, "pG")
    t_pG = P.tok("pG")
    Cf = [P.tile([128, 132], F32, "Cf") for _ in range(H)]
    Cb = [P.tile([128, 132], BF16, "Cb") for _ in range(H)]
    t_Cf, t_Cb = P.toks_n(H, "Cf"), P.toks_n(H, "Cb")
    vt = [P.tile([128, 512], F32, "vt") for _ in range(2)]
    ot = [P.tile([128, 512], F32, "ot") for _ in range(2)]
    t_vt, t_ot = P.toks_n(2, "vt"), P.toks_n(2, "ot")
    Vp = [P.tile([128, 512], BF16, "Vp") for _ in range(2)]
    t_Vp = P.toks_n(2, "Vp")
    Sm = [P.tile([128, 128], BF16, "Sm") for _ in range(2)]
    Kt = [P.tile([128, 128], BF16, "Kt") for _ in range(2)]
    t_Sm, t_Kt = P.toks_n(2, "Sm"), P.toks_n(2, "Kt")
    pS = [P.psum([128, 512], F32, "pS") for _ in range(2)]
    t_pS = P.toks_n(2, "pS")
    pK = P.psum([128, 512], BF16, "pK")
    t_pK = P.tok("pK")
    pN = P.psum([128, 512], F32, "pN")
    pD = P.psum([128, 512], F32, "pD")
    t_pN, t_pD = P.tok("pN"), P.tok("pD")
    pU = [P.psum([128, 512], F32, "pU") for _ in range(2)]
    t_pU = P.toks_n(2, "pU")
    hh = [P.tile([128, 512], F32, "hh") for _ in range(2)]
    t_hh = P.toks_n(2, "hh")
    st = [P.tile([128, 64], F32, "st") for _ in range(2)]
    t_st = P.toks_n(2, "st")
    dn = [P.tile([128, 16], F32, "dn") for _ in range(2)]
    t_dn = P.toks_n(2, "dn")
    yo = [P.tile([128, 512], BF16, "yo") for _ in range(2)]
    t_yo = P.toks_n(2, "yo")
    nit = 0
    nsk = 0
    nblk = 0
    for sq in range(T // S):
        for h in range(H):
            P.dve(lambda e, h=h: e.memset(Cf[h][:], 0.0), w=[t_Cf[h]])
            P.dve(lambda e, h=h: e.memset(Cb[h][:], 0.0), w=[t_Cb[h]])
        for sg_i in range(nseg):
            t0 = sq * S + sg_i * SEG
            for qh in range(8):
                b = nit % 2
                nit += 1
                load_hist(P, xp[b], t_xp[b], pf[1024 + qh * 128:1024 + (qh + 1) * 128, :], t0, SEG, sg_i == 0)
                conv4(P, xp[b], t_xp[b], uu[b], t_uu[b], prm[:, qh, 0:4], t_prm, prm[:, qh, 4:5], SEG)
                if qh < 4:
                    P.act(lambda e, b=b: e.activation(uu[b][:], uu[b][:], AF.Silu), r=[t_uu[b]], w=[t_uu[b]])
                    P.dve(lambda e, b=b, qh=qh: e.tensor_scalar(QK[qh][:], uu[b][:], DH ** -0.5, None, ALU.mult),
                          r=[t_uu[b]], w=[t_QK[qh]])
                else:
                    P.act(lambda e, b=b, qh=qh: e.activation(QK[qh][:], uu[b][:], AF.Silu), r=[t_uu[b]], w=[t_QK[qh]])
            P.dma("sp", ift[:], pt[t0:t0 + SEG, 1024:1032].rearrange("(c p) g -> p c g", p=128), w=[t_g])
            for h in range(H):
                P.dve(lambda e, h=h: e.tensor_scalar(gI[:, :, h], ift[:, :, h], gbias[:, h:h + 1], None, ALU.add),
                      r=[t_g, t_gbias], w=[t_g])
                P.act(lambda e, h=h: e.activation(gL[:, :, h], ift[:, :, 4 + h], AF.Exp, bias=gbias[:, 8 + h:9 + h],
                                                  scale=-1.0), r=[t_g, t_gbias], w=[t_g])
            P.act(lambda e: e.activation(gL[:], gL[:], AF.Ln, bias=1.0), r=[t_g], w=[t_g])
            gLf = gL[:].rearrange("p c h -> p (c h)")
            P.pe(lambda e: e.matmul(pG[:, 0:NBK * 4], mask_le, gLf, start=True, stop=True), r=[t_g, t_cst], w=[t_pG])
            P.pe(lambda e: e.matmul(pG[:, 256:256 + NBK * 4], ones, gLf, start=True, stop=True),
                 r=[t_g, t_cst], w=[t_pG])
            cum = pG[:, 0:NBK * 4].rearrange("p (c h) -> p c h", h=4)
            tot = pG[:, 256:256 + NBK * 4].rearrange("p (c h) -> p c h", h=4)
            P.dve(lambda e: e.tensor_tensor(gW[:], cum, gI[:], ALU.add), r=[t_pG, t_g], w=[t_g, t_pG])
            P.act(lambda e: e.activation(gW[:], gW[:], AF.Exp), r=[t_g], w=[t_g])
            P.dve(lambda e: e.tensor_copy(gWb[:], gW[:]), r=[t_g], w=[t_g])
            P.act(lambda e: e.activation(gE[:], cum, AF.Exp, scale=-1.0), r=[t_pG], w=[t_g, t_pG])
            P.act(lambda e: e.activation(gD[:], tot, AF.Exp, scale=-1.0), r=[t_pG], w=[t_g, t_pG])
            for c in range(NBK):
                bb = nblk % 2
                nblk += 1
                r0 = t0 + c * 128
                blk = slice(c * 128, (c + 1) * 128)
                P.dma("sp", vt[bb][:], pt[r0:r0 + 128, 0:512], w=[t_vt[bb]])
                P.dma("sp", ot[bb][:], pt[r0:r0 + 128, 512:1024], w=[t_ot[bb]])
                for h in range(H):
                    P.dve(lambda e, bb=bb, h=h, c=c: e.tensor_scalar(
                        Vp[bb][:, h * 128:(h + 1) * 128], vt[bb][:, h * 128:(h + 1) * 128], gW[:, c, h:h + 1], None,
                        ALU.mult), r=[t_vt[bb], t_g], w=[t_Vp[bb]])
                for h in range(H):
                    sk = nsk % 2
                    nsk += 1
                    qT, kT = QK[h], QK[4 + h]
                    P.pe(lambda e, sk=sk, kT=kT, qT=qT, blk=blk: e.matmul(pS[sk][:, 0:128], kT[:, blk], qT[:, blk],
                                                                         start=True, stop=True),
                         r=[t_QK[h], t_QK[4 + h]], w=[t_pS[sk]])
                    P.dve(lambda e, sk=sk: e.tensor_tensor(Sm[sk][:], pS[sk][:, 0:128], mask_le, ALU.mult),
                          r=[t_pS[sk], t_cst], w=[t_Sm[sk]])
                    P.pe(lambda e, kT=kT, blk=blk: e.transpose(pK[:, 0:128], kT[:, blk], idb[:]),
                         r=[t_QK[4 + h], t_idb], w=[t_pK])
                    P.act(lambda e, sk=sk: e.copy(Kt[sk][:], pK[:, 0:128]), r=[t_pK], w=[t_Kt[sk]])
                    P.pe(lambda e, sk=sk, bb=bb, h=h: e.matmul(pN[:, h * 128:(h + 1) * 128], Sm[sk][:],
                                                               Vp[bb][:, h * 128:(h + 1) * 128], start=True, stop=False),
                         r=[t_Sm[sk], t_Vp[bb]], w=[t_pN])
                    P.pe(lambda e, qT=qT, blk=blk, h=h: e.matmul(pN[:, h * 128:(h + 1) * 128], qT[:, blk],
                                                                 Cb[h][:, 0:128], start=False, stop=True),
                         r=[t_QK[h], t_Cb[h]], w=[t_pN])
                    P.pe(lambda e, sk=sk, h=h, c=c: e.matmul(pD[:, h:h + 1], Sm[sk][:], gWb[:, c, h:h + 1],
                                                             start=True, stop=False), r=[t_Sm[sk], t_g], w=[t_pD])
                    P.pe(lambda e, qT=qT, blk=blk, h=h: e.matmul(pD[:, h:h + 1], qT[:, blk], Cb[h][:, 128:129],
                                                                 start=False, stop=True),
                         r=[t_QK[h], t_Cb[h]], w=[t_pD])
                    P.pe(lambda e, sk=sk, bb=bb, h=h: e.matmul(pU[sk][:, 0:128], Kt[sk][:],
                                                               Vp[bb][:, h * 128:(h + 1) * 128], start=True, stop=True),
                         r=[t_Kt[sk], t_Vp[bb]], w=[t_pU[sk]])
                    P.pe(lambda e, sk=sk, h=h, c=c: e.matmul(pU[sk][:, 128:129], Kt[sk][:], gWb[:, c, h:h + 1],
                                                             start=True, stop=True), r=[t_Kt[sk], t_g], w=[t_pU[sk]])
                    P.dve(lambda e, sk=sk, h=h: e.tensor_tensor(Cf[h][:, 0:129], pU[sk][:, 0:129], Cf[h][:, 0:129],
                                                                ALU.add), r=[t_pU[sk], t_Cf[h]], w=[t_Cf[h]])
                    P.pool(lambda e, h=h, c=c: e.tensor_scalar(Cf[h][:, 0:129], Cf[h][:, 0:129], gD[:, c, h:h + 1],
                                                               None, ALU.mult), r=[t_Cf[h], t_g], w=[t_Cf[h]])
                    P.act(lambda e, h=h: e.copy(Cb[h][:, 0:129], Cf[h][:, 0:129]), r=[t_Cf[h]], w=[t_Cb[h]])
                P.dve(lambda e, bb=bb, c=c: e.tensor_tensor(dn[bb][:, 0:4], pD[:, 0:4], gE[:, c, :], ALU.mult),
                      r=[t_pD, t_g], w=[t_dn[bb]])
                P.dve(lambda e, bb=bb: e.tensor_scalar(dn[bb][:, 12:16], dn[bb][:, 0:4], -1.0, 1.0, ALU.mult, ALU.max),
                      r=[t_dn[bb]], w=[t_dn[bb]])
                P.dve(lambda e, bb=bb: e.tensor_scalar(dn[bb][:, 0:4], dn[bb][:, 0:4], 1.0, None, ALU.max),
                      r=[t_dn[bb]], w=[t_dn[bb]])
                P.dve(lambda e, bb=bb: e.tensor_tensor(dn[bb][:, 0:4], dn[bb][:, 0:4], dn[bb][:, 12:16], ALU.max),
                      r=[t_dn[bb]], w=[t_dn[bb]])
                P.dve(lambda e, bb=bb: e.reciprocal(dn[bb][:, 4:8], dn[bb][:, 0:4]), r=[t_dn[bb]], w=[t_dn[bb]])
                P.dve(lambda e, bb=bb, c=c: e.tensor_tensor(dn[bb][:, 8:12], dn[bb][:, 4:8], gE[:, c, :], ALU.mult),
                      r=[t_dn[bb], t_g], w=[t_dn[bb]])
                for h in range(H):
                    if h % 2 == 0:
                        P.act(lambda e, bb=bb, h=h: e.activation(hh[bb][:, h * 128:(h + 1) * 128],
                                                                 pN[:, h * 128:(h + 1) * 128], AF.Copy,
                                                                 scale=dn[bb][:, 8 + h:9 + h]),
                              r=[t_pN, t_dn[bb]], w=[t_hh[bb], t_pN])
                    else:
                        P.dve(lambda e, bb=bb, h=h: e.tensor_scalar(hh[bb][:, h * 128:(h + 1) * 128],
                                                                    pN[:, h * 128:(h + 1) * 128],
                                                                    dn[bb][:, 8 + h:9 + h], None, ALU.mult),
                              r=[t_pN, t_dn[bb]], w=[t_hh[bb], t_pN])
                head_norm_tm(P, hh[bb], t_hh[bb], H, DH, 1e-6, st[bb], t_st[bb])
                P.pool(lambda e, bb=bb: e.tensor_tensor(hh[bb][:], hh[bb][:], ng[:], ALU.mult),
                       r=[t_hh[bb], t_ng], w=[t_hh[bb]])
                P.act(lambda e, bb=bb: e.activation(ot[bb][:], ot[bb][:], AF.Sigmoid), r=[t_ot[bb]], w=[t_ot[bb]])
                P.dve(lambda e, bb=bb: e.tensor_tensor(yo[bb][:], hh[bb][:], ot[bb][:], ALU.mult),
                      r=[t_hh[bb], t_ot[bb]], w=[t_yo[bb]])
                P.dma("sp", yt[r0:r0 + 128, 0:512], yo[bb][:], r=[t_yo[bb]], sig=t_yo[bb])
    P.end()


EVEN_F = [(c * 128, 128, c * 128) for c in range(16)]
EVEN_T = [(2048, 512, 0), (2560, 512, 512), (3072, 8, 1024)]


def build_program(T, S=None, layers=DEPTH, only=None):
    nc = bass.Bass("TRN2", target_bir_lowering=False)
    cx = Ctx()
    cx.T = T
    S = S or T // 2
    dt = nc.dram_tensor

    def inp(nm, shape):
        setattr(cx, nm, dt(nm, list(shape), F32, kind="ExternalInput").ap())

    inp("x", [T, D])
    inp("consts", [128, NCONST])
    for nm, shape in PARAM_SHAPES:
        inp(nm, shape)
    cx.out = dt("out", [T, D], F32, kind="ExternalOutput").ap()
    cx.xa = dt("xa", [T, D], F32).ap()
    cx.xb = dt("xb", [T, D], F32).ap()
    cx.pf = dt("pf", [2048, T], F32).ap()
    cx.pt = dt("pt", [T, 1536], F32).ap()
    cx.yf = dt("yf", [512, T], BF16).ap()
    cx.yt = dt("yt", [T, 1024], BF16).ap()
    P = Prog(nc)
    cx.P = P
    if only == "ffn":
        ffn_phase(P, cx, "f1", cx.x, cx.out, cx.ffn1_wi[0], cx.ffn1_wo[0], cx.ln_g[0, 0], cx.ln_b[0, 0], T)
    elif only == "even":
        even_layer_mix(P, cx, 0, 0, cx.x, cx.out, T, S)
    elif only == "odd":
        odd_layer_mix(P, cx, 1, 0, cx.x, cx.out, T, S)
    P.ges.close()
    return nc, cx


def even_layer_mix(P, cx, l, e_idx, x_in, x_out, T, S):
    n = "L%d" % l
    proj_phase(P, cx, n + "pj", x_in, cx.ev_w_in[e_idx], EVEN_IN, EVEN_F, EVEN_T, cx.pf, cx.pt, T)
    rglru_phase(P, cx, n + "rg", e_idx, cx.pf, cx.yf, T, S)
    mlstm_phase(P, cx, n + "ml", e_idx, cx.pf, cx.pt, cx.yt, T, S)
    outproj_phase(P, cx, n + "op", x_in, x_out, cx.ev_w_out[e_idx], cx.yf, 512, cx.yt[:, 0:512], 512,
                  cx.ln_g[l, 1], cx.ln_b[l, 1], T)


PARAM_SHAPES = [
    ("ffn1_wi", (4, 1024, 5632)), ("ffn1_wo", (4, 2816, 1024)), ("ffn2_wi", (4, 1024, 5632)),
    ("ffn2_wo", (4, 2816, 1024)), ("ln_g", (4, 3, 1024)), ("ln_b", (4, 3, 1024)),
    ("ev_w_in", (2, 1024, 3080)), ("ev_w_out", (2, 1024, 1024)), ("rg_conv_w", (2, 4, 512)),
    ("rg_conv_b", (2, 512)), ("rg_wa", (2, 8, 64, 64)), ("rg_wx", (2, 8, 64, 64)), ("rg_ba", (2, 512)),
    ("rg_bx", (2, 512)), ("rg_lambda", (2, 512)), ("ml_conv_w", (2, 4, 1024)), ("ml_conv_b", (2, 1024)),
    ("ml_i_bias", (2, 4)), ("ml_f_bias", (2, 4)), ("ml_norm_g", (2, 512)), ("od_w_in", (2, 1024, 3216)),
    ("od_w_out", (2, 1024, 1024)), ("rk_mu", (2, 1664)), ("rk_w0", (2, 512)), ("rk_wB", (2, 32, 512)),
    ("rk_a0", (2, 512)), ("rk_aB", (2, 32, 512)), ("rk_gB", (2, 64, 512)), ("rk_k_k", (2, 512)),
    ("rk_k_a", (2, 512)), ("rk_r_k", (2, 8, 64)), ("rk_ln_g", (2, 512)), ("rk_ln_b", (2, 512)),
    ("gla_gB", (2, 16, 256)), ("gla_gb", (2, 256)), ("gla_norm_g", (2, 512)),
]


ODD_F = ([(c * 128, 128, c * 128) for c in range(8)] +
         [(1536, 32, 1024), (1568, 32, 1056), (1600, 64, 1088)] +
         [(1664 + c * 128, 128, 1152 + c * 128) for c in range(4)] +
         [(2688, 16, 1664)])
ODD_T = [(1024, 512, 0), (2176, 512, 512), (2704, 512, 1024)]


def cumsum_blocks(P, dst, t_dst, src, t_src, ones, t_cst, n):
    for c in range(n // 128):
        blk = slice(c * 128, (c + 1) * 128)
        P.dve(lambda e, blk=blk: e.tensor_tensor_scan(dst[:, blk], ones, src[:, blk], 0.0, ALU.mult, ALU.add),
              r=[t_src, t_cst], w=[t_dst])


def gla_phase(P, cx, name, o_idx, pf, pt, yt, T, S):
    P.begin(name)
    H = 4
    SEG = min(1024, S)
    nseg = S // SEG
    NBK = SEG // 128
    cst = P.tile([128, NCONST], F32, "cst")
    t_cst = P.tok("cst")
    P.dma("sp", cst[:], cx.consts, w=[t_cst])
    idb = P.tile([128, 128], BF16, "idb")
    t_idb = P.tok("idb")
    P.act(lambda e: e.copy(idb[:], cst[:, C_IDENT:C_IDENT + 128]), r=[t_cst], w=[t_idb])
    mask_le = cst[:, C_LE:C_LE + 128]
    ones = cst[:, C_ONES:C_ONES + 128]
    gBb = P.tile([16, 256], BF16, "gBb")
    t_gBb = P.tok("gBb")
    P.dma("pool", gBb[:], cx.gla_gB[o_idx], w=[t_gBb])
    prm = P.tile([128, 2, 4], F32, "prm")
    t_prm = P.tok("prm")
    for pc in range(2):
        P.dma("sp", prm[:, pc, 0:1], colvec(cx.gla_gb[o_idx, pc * 128:(pc + 1) * 128]), w=[t_prm])
        P.dve(lambda e, pc=pc: e.tensor_scalar(prm[:, pc, 1:2], prm[:, pc, 0:1], -1.0, None, ALU.mult),
              r=[t_prm], w=[t_prm])
    ng = P.tile([128, 512], F32, "ng")
    t_ng = P.tok("ng")
    P.dma("sp", ng[:], cx.gla_norm_g[o_idx].partition_broadcast(128), w=[t_ng])
    gd = P.tile([16, SEG], F32, "gd")
    gdb = P.tile([16, SEG], BF16, "gdb")
    t_gd, t_gdb = P.tok("gd"), P.tok("gdb")
    QD = [P.tile([128, SEG], BF16, "QD") for _ in range(2)]
    KI = [P.tile([128, SEG], BF16, "KI") for _ in range(2)]
    DEC = [P.tile([128, NBK], F32, "DEC") for _ in range(2)]
    t_QD, t_KI, t_DEC = P.toks_n(2, "QD"), P.toks_n(2, "KI"), P.toks_n(2, "DEC")
    qx = [P.tile([128, SEG], F32, "qx") for _ in range(2)]
    kx = [P.tile([128, SEG], F32, "kx") for _ in range(2)]
    LL = [P.tile([128, SEG], F32, "LL") for _ in range(2)]
    cum = [P.tile([128, SEG], F32, "cum") for _ in range(2)]
    EE = [P.tile([128, SEG], F32, "EE") for _ in range(2)]
    t_qx, t_kx, t_LL, t_cum, t_EE = (P.toks_n(2, "qx"), P.toks_n(2, "kx"), P.toks_n(2, "LL"), P.toks_n(2, "cum"),
                                     P.toks_n(2, "EE"))
    pz = [P.psum([128, 512], F32, "pz") for _ in range(2)]
    t_pz = P.toks_n(2, "pz")
    Sf = [P.tile([128, 128], F32, "Sf") for _ in range(2)]
    Sb = [P.tile([128, 128], BF16, "Sb") for _ in range(2)]
    t_Sf, t_Sb = P.toks_n(2, "Sf"), P.toks_n(2, "Sb")
    Kp = [[P.tile([128, 128], BF16, "Kp") for _ in range(2)] for _ in range(2)]
    t_Kp = [P.toks_n(2, "Kp") for _ in range(2)]
    vt = [P.tile([128, 512], F32, "vt") for _ in range(2)]
    vb = [P.tile([128, 512], BF16, "vb") for _ in range(2)]
    ot = [P.tile([128, 512], F32, "ot") for _ in range(2)]
    t_vt, t_vb, t_ot = P.toks_n(2, "vt"), P.toks_n(2, "vb"), P.toks_n(2, "ot")
    Sm = [P.tile([128, 128], BF16, "Sm") for _ in range(2)]
    t_Sm = P.toks_n(2, "Sm")
    pS = [P.psum([128, 512], F32, "pS") for _ in range(2)]
    t_pS = P.toks_n(2, "pS")
    pK = P.psum([128, 512], BF16, "pK")
    t_pK = P.tok("pK")
    pO = P.psum([128, 512], F32, "pO")
    t_pO = P.tok("pO")
    pU = [P.psum([128, 512], F32, "pU") for _ in range(2)]
    t_pU = P.toks_n(2, "pU")
    oo = [P.tile([128, 512], F32, "oo") for _ in range(2)]
    t_oo = P.toks_n(2, "oo")
    st = [P.tile([128, 64], F32, "st") for _ in range(2)]
    t_st = P.toks_n(2, "st")
    yo = [P.tile([128, 512], BF16, "yo") for _ in range(2)]
    t_yo = P.toks_n(2, "yo")
    for pc in range(2):
        for ab in range(2):
            P.dve(lambda e, pc=pc, ab=ab: e.memset(Kp[pc][ab][:], 0.0), w=[t_Kp[pc][ab]])
    npz = 0
    nsk = 0
    nblk = 0
    for sq in range(T // S):
        for pc in range(2):
            P.dve(lambda e, pc=pc: e.memset(Sf[pc][:], 0.0), w=[t_Sf[pc]])
            P.dve(lambda e, pc=pc: e.memset(Sb[pc][:], 0.0), w=[t_Sb[pc]])
        for sg_i in range(nseg):
            t0 = sq * S + sg_i * SEG
            P.dma("sp", gd[:], pf[1664:1680, t0:t0 + SEG], w=[t_gd])
            P.act(lambda e: e.copy(gdb[:], gd[:]), r=[t_gd], w=[t_gdb])
            for pc in range(2):
                P.dma("sp", qx[pc][:], pf[1152 + pc * 128:1152 + (pc + 1) * 128, t0:t0 + SEG], w=[t_qx[pc]])
                P.dma("sp", kx[pc][:], pf[1408 + pc * 128:1408 + (pc + 1) * 128, t0:t0 + SEG], w=[t_kx[pc]])
                w_ = min(512, SEG)
                for sp in range(SEG // w_):
                    cs = slice(sp * w_, (sp + 1) * w_)
                    pb = npz % 2
                    npz += 1
                    P.pe(lambda e, pb=pb, pc=pc, cs=cs: e.matmul(pz[pb][:, 0:w_], gBb[:, pc * 128:(pc + 1) * 128],
                                                                gdb[:, cs], start=True, stop=True),
                         r=[t_gBb, t_gdb], w=[t_pz[pb]])
                    P.act(lambda e, pb=pb, pc=pc, cs=cs: e.activation(LL[pc][:, cs], pz[pb][:, 0:w_], AF.Exp,
                                                                       bias=prm[:, pc, 1:2], scale=-1.0),
                          r=[t_pz[pb], t_prm], w=[t_LL[pc]])
                P.act(lambda e, pc=pc: e.activation(LL[pc][:], LL[pc][:], AF.Ln, bias=1.0), r=[t_LL[pc]], w=[t_LL[pc]])
                cumsum_blocks(P, cum[pc], t_cum[pc], LL[pc], t_LL[pc], ones, t_cst, SEG)
                P.act(lambda e, pc=pc: e.activation(EE[pc][:], cum[pc][:], AF.Exp, scale=-1.0 / 16.0),
                      r=[t_cum[pc]], w=[t_EE[pc]])
                P.dve(lambda e, pc=pc: e.scalar_tensor_tensor(QD[pc][:], qx[pc][:], 0.125, EE[pc][:], ALU.mult, ALU.mult),
                      r=[t_qx[pc], t_EE[pc]], w=[t_QD[pc]])
                P.act(lambda e, pc=pc: e.activation(DEC[pc][:], cum[pc][:].rearrange("p (c t) -> p c t", t=128)[:, :, 127],
                                                    AF.Exp, scale=-1.0 / 16.0), r=[t_cum[pc]], w=[t_DEC[pc]])
                P.act(lambda e, pc=pc: e.activation(EE[pc][:], cum[pc][:], AF.Exp, scale=1.0 / 16.0),
                      r=[t_cum[pc], t_QD[pc]], w=[t_EE[pc]])
                P.dve(lambda e, pc=pc: e.tensor_tensor(KI[pc][:], kx[pc][:], EE[pc][:], ALU.mult),
                      r=[t_kx[pc], t_EE[pc]], w=[t_KI[pc]])
            for c in range(NBK):
                bb = nblk % 2
                nblk += 1
                r0 = t0 + c * 128
                blk = slice(c * 128, (c + 1) * 128)
                P.dma("sp", vt[bb][:], pt[r0:r0 + 128, 512:1024], w=[t_vt[bb]])
                P.dma("sp", ot[bb][:], pt[r0:r0 + 128, 1024:1536], w=[t_ot[bb]])
                P.act(lambda e, bb=bb: e.copy(vb[bb][:], vt[bb][:]), r=[t_vt[bb]], w=[t_vb[bb]])
                for h in range(H):
                    pc, po = h // 2, (h % 2) * 64
                    sk = nsk % 2
                    nsk += 1
                    P.pe(lambda e, sk=sk, pc=pc, po=po, blk=blk: e.matmul(
                        pS[sk][:, 0:128], KI[pc][po:po + 64, blk], QD[pc][po:po + 64, blk], start=True, stop=True),
                        r=[t_KI[pc], t_QD[pc]], w=[t_pS[sk]])
                    P.dve(lambda e, sk=sk: e.tensor_tensor(Sm[sk][:], pS[sk][:, 0:128], mask_le, ALU.mult),
                          r=[t_pS[sk], t_cst], w=[t_Sm[sk]])
                    P.pe(lambda e, sk=sk, bb=bb, h=h: e.matmul(pO[:, h * 128:(h + 1) * 128], Sm[sk][:],
                                                               vb[bb][:, h * 128:(h + 1) * 128], start=True, stop=False),
                         r=[t_Sm[sk], t_vb[bb]], w=[t_pO])
                    P.pe(lambda e, pc=pc, po=po, blk=blk, h=h: e.matmul(
                        pO[:, h * 128:(h + 1) * 128], QD[pc][po:po + 64, blk], Sb[pc][po:po + 64, :],
                        start=False, stop=True), r=[t_QD[pc], t_Sb[pc]], w=[t_pO])
                for pc in range(2):
                    P.pe(lambda e, pc=pc, blk=blk: e.transpose(pK[:, pc * 128:(pc + 1) * 128], KI[pc][:, blk], idb[:]),
                         r=[t_KI[pc], t_idb], w=[t_pK])
                    P.act(lambda e, pc=pc: e.copy(Kp[pc][0][:, 0:64], pK[:, pc * 128:pc * 128 + 64]),
                          r=[t_pK], w=[t_Kp[pc][0], t_pK])
                    P.dve(lambda e, pc=pc: e.tensor_copy(Kp[pc][1][:, 64:128], pK[:, pc * 128 + 64:pc * 128 + 128]),
                          r=[t_pK], w=[t_Kp[pc][1], t_pK])
                    for ab in range(2):
                        h = 2 * pc + ab
                        P.pe(lambda e, pc=pc, ab=ab, bb=bb, h=h: e.matmul(
                            pU[pc][:, 0:128], Kp[pc][ab][:], vb[bb][:, h * 128:(h + 1) * 128],
                            start=(ab == 0), stop=(ab == 1)), r=[t_Kp[pc][ab], t_vb[bb]], w=[t_pU[pc]])
                    P.dve(lambda e, pc=pc: e.tensor_tensor(Sf[pc][:], pU[pc][:, 0:128], Sf[pc][:], ALU.add),
                          r=[t_pU[pc], t_Sf[pc]], w=[t_Sf[pc]])
                    P.pool(lambda e, pc=pc, c=c: e.tensor_scalar(Sf[pc][:], Sf[pc][:], DEC[pc][:, c:c + 1], None, ALU.mult),
                           r=[t_Sf[pc], t_DEC[pc]], w=[t_Sf[pc]])
                    P.act(lambda e, pc=pc: e.copy(Sb[pc][:], Sf[pc][:]), r=[t_Sf[pc]], w=[t_Sb[pc]])
                P.act(lambda e, bb=bb: e.copy(oo[bb][:], pO[:]), r=[t_pO], w=[t_oo[bb]])
                head_norm_tm(P, oo[bb], t_oo[bb], H, 128, 1e-5, st[bb], t_st[bb])
                P.pool(lambda e, bb=bb: e.tensor_tensor(oo[bb][:], oo[bb][:], ng[:], ALU.mult),
                       r=[t_oo[bb], t_ng], w=[t_oo[bb]])
                P.act(lambda e, bb=bb: e.activation(ot[bb][:], ot[bb][:], AF.Silu), r=[t_ot[bb]], w=[t_ot[bb]])
                P.dve(lambda e, bb=bb: e.tensor_tensor(yo[bb][:], oo[bb][:], ot[bb][:], ALU.mult),
                      r=[t_oo[bb], t_ot[bb]], w=[t_yo[bb]])
                P.dma("sp", yt[r0:r0 + 128, 512:1024], yo[bb][:], r=[t_yo[bb]], sig=t_yo[bb])
    P.end()


RK_C = math.exp(-0.5)


def rwkv_phase(P, cx, name, o_idx, pf, pt, yt, T, S):
    P.begin(name)
    SEG = min(512, S)
    nseg = S // SEG
    NBK = SEG // 128
    w_ = min(512, SEG)
    cst = P.tile([128, NCONST], F32, "cst")
    t_cst = P.tok("cst")
    P.dma("sp", cst[:], cx.consts, w=[t_cst])
    cb = P.tile([128, 1024], BF16, "cb")
    t_cb = P.tok("cb")
    P.act(lambda e: e.copy(cb[:, 0:128], cst[:, C_IDENT:C_IDENT + 128]), r=[t_cst], w=[t_cb])
    P.act(lambda e: e.copy(cb[:, 128:256], cst[:, C_BLKIND:C_BLKIND + 128]), r=[t_cst], w=[t_cb])
    idb = cb[:, 0:128]
    blkind = cb[:, 128:130]
    mask_le = cst[:, C_LE:C_LE + 128]
    mask_lt = cst[:, C_LT:C_LT + 128]
    mask_gt = cst[:, C_GT:C_GT + 128]
    ones = cst[:, C_ONES:C_ONES + 128]
    blk64 = cst[:, C_BLK64:C_BLK64 + 128]
    m2 = P.tile([128, 256], F32, "m2")
    t_m2 = P.tok("m2")
    P.dve(lambda e: e.tensor_copy(m2[:, 0:128], mask_lt), r=[t_cst], w=[t_m2])
    P.dve(lambda e: e.tensor_copy(m2[:, 128:256], mask_le), r=[t_cst], w=[t_m2])
    prm = P.tile([128, 4, 8], F32, "prm")
    t_prm = P.tok("prm")
    rkflat = cx.rk_r_k[o_idx].rearrange("h d -> (h d)")
    for pc in range(4):
        cs = slice(pc * 128, (pc + 1) * 128)
        for k, v in ((0, cx.rk_mu[o_idx, 0:512]), (1, cx.rk_mu[o_idx, 512:1024]), (2, cx.rk_w0[o_idx]),
                     (3, cx.rk_a0[o_idx]), (4, cx.rk_k_k[o_idx]), (5, cx.rk_k_a[o_idx]), (7, rkflat)):
            P.dma("sp", prm[:, pc, k:k + 1], colvec(v[cs]), w=[t_prm])
        P.dve(lambda e, pc=pc: e.tensor_scalar(prm[:, pc, 6:7], prm[:, pc, 5:6], -1.0, 1.0, ALU.mult, ALU.add),
              r=[t_prm], w=[t_prm])
    mul = P.tile([64, 4], F32, "mul")
    t_mul = P.tok("mul")
    P.dma("sp", mul[0:32, 0:1], colvec(cx.rk_mu[o_idx, 1536:1568]), w=[t_mul])
    P.dma("sp", mul[0:32, 1:2], colvec(cx.rk_mu[o_idx, 1568:1600]), w=[t_mul])
    P.dma("sp", mul[0:64, 2:3], colvec(cx.rk_mu[o_idx, 1600:1664]), w=[t_mul])
    wBb = P.tile([32, 512], BF16, "wBb")
    aBb = P.tile([32, 512], BF16, "aBb")
    gBb = P.tile([64, 512], BF16, "gBb")
    t_lw = P.tok("loraw")
    P.dma("pool", wBb[:], cx.rk_wB[o_idx], w=[t_lw])
    P.dma("pool", aBb[:], cx.rk_aB[o_idx], w=[t_lw])
    P.dma("pool", gBb[:], cx.rk_gB[o_idx], w=[t_lw])
    bt = P.tile([128, 3, 512], F32, "bt")
    t_bt = P.tok("bt")
    P.dma("sp", bt[:, 0, :], cx.rk_mu[o_idx, 1024:1536].partition_broadcast(128), w=[t_bt])
    P.dma("sp", bt[:, 1, :], cx.rk_ln_g[o_idx].partition_broadcast(128), w=[t_bt])
    P.dma("sp", bt[:, 2, :], cx.rk_ln_b[o_idx].partition_broadcast(128), w=[t_bt])
    lx = [P.tile([64, SEG + 1], F32, "lx") for _ in range(3)]
    ld = [P.tile([64, SEG], F32, "ld") for _ in range(3)]
    t_lx, t_ld = P.toks_n(3, "lx"), P.toks_n(3, "ld")
    twd = P.tile([32, SEG], BF16, "twd")
    adb = P.tile([32, SEG], BF16, "adb")
    sgd = P.tile([64, SEG], BF16, "sgd")
    t_twd, t_adb, t_sgd = P.tok("twd"), P.tok("adb"), P.tok("sgd")
    ARt = [P.tile([128, NBK, 256], BF16, "ARt") for _ in range(4)]
    BT = [P.tile([128, SEG], BF16, "BT") for _ in range(4)]
    KTt = [P.tile([128, SEG], BF16, "KTt") for _ in range(4)]
    PRD = [P.tile([128, SEG], BF16, "PRD") for _ in range(4)]
    WC = [P.tile([128, NBK], F32, "WC") for _ in range(4)]
    t_AR, t_BT, t_KT, t_PRD, t_WC = (P.toks_n(4, "AR"), P.toks_n(4, "BT"), P.toks_n(4, "KT"), P.toks_n(4, "PRD"),
                                     P.toks_n(4, "WC"))
    names = ["rx", "kx", "rs", "ks", "lw", "av", "cum", "E1", "E2", "E3", "kk", "t1", "km", "t2"]
    tmp = {}
    tt = {}
    for nm in names:
        wd_ = SEG + 1 if nm in ("rx", "kx") else SEG
        tmp[nm] = P.tile([128, wd_], F32, nm)
        tt[nm] = P.tok(nm)
    pz = [P.psum([128, 512], F32, "pz") for _ in range(2)]
    t_pz = P.toks_n(2, "pz")
    Tf = [P.tile([128, 64], F32, "Tf") for _ in range(4)]
    Tb = [P.tile([128, 64], BF16, "Tb") for _ in range(4)]
    t_Tf, t_Tb = P.toks_n(4, "Tf"), P.toks_n(4, "Tb")
    Bp = [[P.tile([128, 128], BF16, "Bp") for _ in range(2)] for _ in range(2)]
    Kp = [[P.tile([128, 128], BF16, "Kp") for _ in range(2)] for _ in range(2)]
    t_Bp = [P.toks_n(2, "Bp") for _ in range(2)]
    t_Kp = [P.toks_n(2, "Kp") for _ in range(2)]
    for par in range(2):
        for ab in range(2):
            P.dve(lambda e, par=par, ab=ab: e.memset(Bp[par][ab][:], 0.0), w=[t_Bp[par][ab]])
            P.pool(lambda e, par=par, ab=ab: e.memset(Kp[par][ab][:], 0.0), w=[t_Kp[par][ab]])
    hb_n = 2
    Pm = [P.tile([128, 128], BF16, "Pm") for _ in range(hb_n)]
    PT = [P.tile([128, 128], BF16, "PT") for _ in range(hb_n)]
    MT = [P.tile([128, 128], BF16, "MT") for _ in range(hb_n)]
    ArbT = [P.tile([128, 128], BF16, "ArbT") for _ in range(hb_n)]
    AakT = [P.tile([128, 128], BF16, "AakT") for _ in range(hb_n)]
    ArkT = [P.tile([128, 128], BF16, "ArkT") for _ in range(hb_n)]
    t_Pm, t_PT, t_MT, t_Arb, t_Aak, t_Ark = (P.toks_n(hb_n, "Pm"), P.toks_n(hb_n, "PT"), P.toks_n(hb_n, "MT"),
                                             P.toks_n(hb_n, "Arb"), P.toks_n(hb_n, "Aak"), P.toks_n(hb_n, "Ark"))
    pA = [P.psum([128, 512], F32, "pA") for _ in range(2)]
    t_pA = P.toks_n(2, "pA")
    pI = [P.psum([128, 512], F32, "pI") for _ in range(2)]
    t_pI = P.toks_n(2, "pI")
    pC = P.psum([128, 512], F32, "pC")
    t_pC = P.tok("pC")
    pTr = P.psum([128, 512], BF16, "pTr")
    t_pTr = P.tok("pTr")
    pY = pz[1]
    t_pY = t_pz[1]
    rhs_sb = [P.tile([128, 128], BF16, "rhs") for _ in range(2)]
    u_sb = [P.tile([128, 128], BF16, "usb") for _ in range(2)]
    t_rhs, t_usb = P.toks_n(2, "rhs"), P.toks_n(2, "usb")
    vt = [P.tile([128, 512], F32, "vt") for _ in range(2)]
    vpv = [P.tile([128, 512], F32, "vpv") for _ in range(2)]
    vs = [P.tile([128, 512], F32, "vs") for _ in range(2)]
    vsb = [P.tile([128, 512], BF16, "vsb") for _ in range(2)]
    t_vt, t_vpv, t_vs, t_vsb = P.toks_n(2, "vt"), P.toks_n(2, "vpv"), P.toks_n(2, "vs"), P.toks_n(2, "vsb")
    yy = [P.tile([128, 512], F32, "yy") for _ in range(2)]
    gg = [P.tile([128, 512], F32, "gg") for _ in range(2)]
    bs = [P.tile([128, 8], F32, "bs") for _ in range(2)]
    st = [P.tile([128, 96], F32, "st") for _ in range(2)]
    yo = [P.tile([128, 512], BF16, "yo") for _ in range(2)]
    t_yy, t_gg, t_bs, t_st, t_yo = (P.toks_n(2, "yy"), P.toks_n(2, "gg"), P.toks_n(2, "bs"), P.toks_n(2, "st"),
                                    P.toks_n(2, "yo"))
    pGt = pz[0]
    t_pG = t_pz[0]
    npz = 0
    nhb = 0
    nblk = 0
    npar = 0

    def shift(dst, t_dst, x, t_x, d, t_d, mu_col, n, rows=128):
        P.dve(lambda e: e.tensor_tensor(d[0:rows, 0:n], x[0:rows, 0:n], x[0:rows, 1:n + 1], ALU.subtract),
              r=[t_x], w=[t_d])
        P.dve(lambda e: e.scalar_tensor_tensor(dst[0:rows, 0:n], d[0:rows, 0:n], mu_col, x[0:rows, 1:n + 1],
                                               ALU.mult, ALU.add), r=[t_d, t_x], w=[t_dst])

    for sq in range(T // S):
        for pc in range(4):
            P.dve(lambda e, pc=pc: e.memset(Tf[pc][:], 0.0), w=[t_Tf[pc]])
            P.dve(lambda e, pc=pc: e.memset(Tb[pc][:], 0.0), w=[t_Tb[pc]])
        for sg_i in range(nseg):
            t0 = sq * S + sg_i * SEG
            first = (sg_i == 0)
            for li, (r0_, nr) in enumerate(((1024, 32), (1056, 32), (1088, 64))):
                if first:
                    P.dve(lambda e, li=li, nr=nr: e.memset(lx[li][0:nr, 0:1], 0.0), w=[t_lx[li]])
                    P.dma("sp", lx[li][0:nr, 1:SEG + 1], pf[r0_:r0_ + nr, t0:t0 + SEG], w=[t_lx[li]])
                else:
                    P.dma("sp", lx[li][0:nr, 0:SEG + 1], pf[r0_:r0_ + nr, t0 - 1:t0 + SEG], w=[t_lx[li]])
            shift(ld[0], t_ld[0], lx[0], t_lx[0], ld[0], t_ld[0], mul[0:32, 0:1], SEG, 32)
            P.act(lambda e: e.activation(twd[:], ld[0][0:32, :], AF.Tanh), r=[t_ld[0]], w=[t_twd])
            shift(ld[1], t_ld[1], lx[1], t_lx[1], ld[1], t_ld[1], mul[0:32, 1:2], SEG, 32)
            P.act(lambda e: e.copy(adb[:], ld[1][0:32, :]), r=[t_ld[1]], w=[t_adb])
            shift(ld[2], t_ld[2], lx[2], t_lx[2], ld[2], t_ld[2], mul[0:64, 2:3], SEG, 64)
            P.act(lambda e: e.activation(sgd[:], ld[2][0:64, :], AF.Sigmoid), r=[t_ld[2]], w=[t_sgd])
            for pc in range(4):
                cs = slice(pc * 128, (pc + 1) * 128)
                for nm, r0_ in (("rx", pc * 128), ("kx", 512 + pc * 128)):
                    if first:
                        P.dve(lambda e, nm=nm: e.memset(tmp[nm][:, 0:1], 0.0), w=[tt[nm]])
                        P.dma("sp", tmp[nm][:, 1:SEG + 1], pf[r0_:r0_ + 128, t0:t0 + SEG], w=[tt[nm]])
                    else:
                        P.dma("sp", tmp[nm][:, 0:SEG + 1], pf[r0_:r0_ + 128, t0 - 1:t0 + SEG], w=[tt[nm]])
                shift(tmp["rs"], tt["rs"], tmp["rx"], tt["rx"], tmp["rs"], tt["rs"], prm[:, pc, 0:1], SEG)
                shift(tmp["ks"], tt["ks"], tmp["kx"], tt["kx"], tmp["ks"], tt["ks"], prm[:, pc, 1:2], SEG)
                for sp in range(SEG // w_):
                    cw = slice(sp * w_, (sp + 1) * w_)
                    for wt, src, t_src, dst, bcol in ((wBb, twd, t_twd, "lw", 2), (aBb, adb, t_adb, "av", 3)):
                        pb = npz % 2
                        npz += 1
                        P.pe(lambda e, pb=pb, wt=wt, src=src, cw=cw, cs=cs: e.matmul(
                            pz[pb][:, 0:w_], wt[:, cs], src[:, cw], start=True, stop=True),
                            r=[t_lw, t_src], w=[t_pz[pb]])
                        P.act(lambda e, pb=pb, dst=dst, cw=cw, bcol=bcol, pc=pc: e.activation(
                            tmp[dst][:, cw], pz[pb][:, 0:w_], AF.Sigmoid, bias=prm[:, pc, bcol:bcol + 1]),
                            r=[t_pz[pb], t_prm], w=[tt[dst]])
                cumsum_blocks(P, tmp["cum"], tt["cum"], tmp["lw"], tt["lw"], ones, t_cst, SEG)
                P.act(lambda e: e.activation(tmp["E1"][:], tmp["cum"][:], AF.Exp, scale=-RK_C),
                      r=[tt["cum"]], w=[tt["E1"]])
                P.act(lambda e: e.activation(tmp["E2"][:], tmp["cum"][:], AF.Exp, scale=RK_C),
                      r=[tt["cum"]], w=[tt["E2"]])
                P.pool(lambda e: e.tensor_tensor(tmp["cum"][:], tmp["cum"][:], tmp["lw"][:], ALU.subtract),
                       r=[tt["cum"], tt["lw"]], w=[tt["cum"]])
                P.act(lambda e: e.activation(tmp["E3"][:], tmp["cum"][:], AF.Exp, scale=-RK_C),
                      r=[tt["cum"]], w=[tt["E3"]])
                P.act(lambda e, pc=pc: e.copy(WC[pc][:], tmp["E1"][:].rearrange("p (c t) -> p c t", t=128)[:, :, 127]),
                      r=[tt["E1"]], w=[t_WC[pc]])
                P.dve(lambda e, pc=pc: e.tensor_scalar(tmp["kk"][:], tmp["ks"][:], prm[:, pc, 4:5], None, ALU.mult),
                      r=[tt["ks"], t_prm], w=[tt["kk"]])
                P.pool(lambda e: e.tensor_tensor(tmp["t1"][:], tmp["kk"][:], tmp["kk"][:], ALU.mult),
                       r=[tt["kk"]], w=[tt["t1"]])
                for sp in range(SEG // w_):
                    cw = slice(sp * w_, (sp + 1) * w_)
                    pb = npz % 2
                    npz += 1
                    P.pe(lambda e, pb=pb, cw=cw: e.matmul(pz[pb][:, 0:w_], blk64, tmp["t1"][:, cw], start=True, stop=True),
                         r=[t_cst, tt["t1"]], w=[t_pz[pb]])
                    P.act(lambda e, pb=pb, cw=cw: e.activation(tmp["t2"][:, cw], pz[pb][:, 0:w_], AF.Sqrt),
                          r=[t_pz[pb]], w=[tt["t2"]])
                P.dve(lambda e: e.tensor_scalar(tmp["t2"][:], tmp["t2"][:], 1e-12, None, ALU.max), r=[tt["t2"]], w=[tt["t2"]])
                P.dve(lambda e: e.reciprocal(tmp["t2"][:], tmp["t2"][:]), r=[tt["t2"]], w=[tt["t2"]])
                P.dve(lambda e: e.tensor_tensor(tmp["kk"][:], tmp["kk"][:], tmp["t2"][:], ALU.mult),
                      r=[tt["kk"], tt["t2"]], w=[tt["kk"]])
                P.dve(lambda e, pc=pc: e.tensor_scalar(tmp["t1"][:], tmp["av"][:], prm[:, pc, 5:6], prm[:, pc, 6:7],
                                                       ALU.mult, ALU.add), r=[tt["av"], t_prm, tt["t1"]], w=[tt["t1"]])
                P.pool(lambda e: e.tensor_tensor(tmp["km"][:], tmp["ks"][:], tmp["t1"][:], ALU.mult),
                       r=[tt["ks"], tt["t1"]], w=[tt["km"]])
                v3 = lambda ap: ap.rearrange("p (c t) -> p c t", t=128)
                P.dve(lambda e, pc=pc: e.scalar_tensor_tensor(ARt[pc][:, :, 0:128], v3(tmp["kk"][:]), -1.0,
                                                              v3(tmp["E3"][:]), ALU.mult, ALU.mult),
                      r=[tt["kk"], tt["E3"]], w=[t_AR[pc]])
                P.pool(lambda e, pc=pc: e.tensor_tensor(ARt[pc][:, :, 128:256], v3(tmp["rs"][:]), v3(tmp["E1"][:]),
                                                        ALU.mult), r=[tt["rs"], tt["E1"]], w=[t_AR[pc]])
                P.dve(lambda e: e.tensor_tensor(tmp["t2"][:], tmp["kk"][:], tmp["av"][:], ALU.mult),
                      r=[tt["kk"], tt["av"], tt["t2"]], w=[tt["t2"]])
                P.dve(lambda e, pc=pc: e.tensor_tensor(BT[pc][:], tmp["t2"][:], tmp["E2"][:], ALU.mult),
                      r=[tt["t2"], tt["E2"]], w=[t_BT[pc]])
                P.pool(lambda e, pc=pc: e.tensor_tensor(KTt[pc][:], tmp["km"][:], tmp["E2"][:], ALU.mult),
                       r=[tt["km"], tt["E2"]], w=[t_KT[pc]])
                P.dve(lambda e, pc=pc: e.scalar_tensor_tensor(PRD[pc][:], tmp["rs"][:], prm[:, pc, 7:8], tmp["km"][:],
                                                              ALU.mult, ALU.mult),
                      r=[tt["rs"], tt["km"], t_prm], w=[t_PRD[pc]])
            for c in range(NBK):
                bb = nblk % 2
                nblk += 1
                r0 = t0 + c * 128
                blk = slice(c * 128, (c + 1) * 128)
                P.dma("sp", vt[bb][:], pt[r0:r0 + 128, 0:512], w=[t_vt[bb]])
                if first and c == 0:
                    P.dve(lambda e, bb=bb: e.memset(vpv[bb][0:1, :], 0.0), w=[t_vpv[bb]])
                    P.dma("sp", vpv[bb][1:128, :], pt[r0:r0 + 127, 0:512], w=[t_vpv[bb]])
                else:
                    P.dma("sp", vpv[bb][:], pt[r0 - 1:r0 + 127, 0:512], w=[t_vpv[bb]])
                P.pool(lambda e, bb=bb: e.tensor_tensor(vpv[bb][:], vpv[bb][:], vt[bb][:], ALU.subtract),
                       r=[t_vpv[bb], t_vt[bb]], w=[t_vpv[bb]])
                P.pool(lambda e, bb=bb: e.tensor_tensor(vpv[bb][:], vpv[bb][:], bt[:, 0, :], ALU.mult),
                       r=[t_vpv[bb], t_bt], w=[t_vpv[bb]])
                P.pool(lambda e, bb=bb: e.tensor_tensor(vs[bb][:], vpv[bb][:], vt[bb][:], ALU.add),
                       r=[t_vpv[bb], t_vt[bb]], w=[t_vs[bb]])
                P.act(lambda e, bb=bb: e.copy(vsb[bb][:], vs[bb][:]), r=[t_vs[bb]], w=[t_vsb[bb]])
                P.pe(lambda e, blk=blk: e.matmul(pGt[:], sgd[:, blk], gBb[:], start=True, stop=True),
                     r=[t_sgd, t_lw], w=[t_pG])
                P.act(lambda e, bb=bb: e.copy(gg[bb][:], pGt[:]), r=[t_pG], w=[t_gg[bb]])
                for pc in range(4):
                    P.pe(lambda e, pc=pc, blk=blk: e.matmul(pC[:, 320 + 2 * pc:322 + 2 * pc], PRD[pc][:, blk], blkind,
                                                            start=True, stop=True), r=[t_PRD[pc], t_cb], w=[t_pC])
                P.dve(lambda e, bb=bb: e.tensor_copy(bs[bb][:], pC[:, 320:328]), r=[t_pC], w=[t_bs[bb]])
                for pc in range(4):
                    par = npar % 2
                    npar += 1
                    AR = ARt[pc][:, c, :]
                    hbs = []
                    for ab in range(2):
                        po = ab * 64
                        hb = nhb % hb_n
                        nhb += 1
                        hbs.append(hb)
                        pa = pA[hb]
                        pi = pI[hb]
                        P.pe(lambda e, pa=pa, pc=pc, po=po, AR=AR, blk=blk: e.matmul(
                            pa[:, 0:256], BT[pc][po:po + 64, blk], AR[po:po + 64, :], start=True, stop=True),
                            r=[t_BT[pc], t_AR[pc]], w=[t_pA[hb]])
                        P.pe(lambda e, pa=pa, pc=pc, po=po, AR=AR, blk=blk: e.matmul(
                            pa[:, 256:512], KTt[pc][po:po + 64, blk], AR[po:po + 64, :], start=True, stop=True),
                            r=[t_KT[pc], t_AR[pc]], w=[t_pA[hb]])
                        P.pe(lambda e, pi=pi, pc=pc, po=po, AR=AR, blk=blk: e.matmul(
                            pi[:, 0:128], AR[po:po + 64, 0:128], BT[pc][po:po + 64, blk], start=True, stop=True),
                            r=[t_BT[pc], t_AR[pc]], w=[t_pI[hb]])
                        P.dve(lambda e, hb=hb, pa=pa: e.tensor_tensor(PT[hb][:], pa[:, 0:128], mask_lt, ALU.mult),
                              r=[t_pA[hb], t_cst], w=[t_PT[hb]])
                        P.dve(lambda e, hb=hb, pa=pa: e.tensor_tensor(ArbT[hb][:], pa[:, 128:256], mask_le, ALU.mult),
                              r=[t_pA[hb], t_cst], w=[t_Arb[hb]])
                        P.dve(lambda e, hb=hb, pa=pa: e.tensor_tensor(AakT[hb][:], pa[:, 256:384], mask_lt, ALU.mult),
                              r=[t_pA[hb], t_cst], w=[t_Aak[hb]])
                        P.dve(lambda e, hb=hb, pa=pa: e.tensor_tensor(ArkT[hb][:], pa[:, 384:512], mask_le, ALU.mult),
                              r=[t_pA[hb], t_cst], w=[t_Ark[hb]])
                        P.dve(lambda e, hb=hb, pi=pi: e.tensor_tensor(Pm[hb][:], pi[:, 0:128], mask_gt, ALU.mult),
                              r=[t_pI[hb], t_cst], w=[t_Pm[hb]])
                        P.pool(lambda e, hb=hb: e.tensor_tensor(MT[hb][:], PT[hb][:], idb, ALU.add),
                               r=[t_PT[hb], t_cb], w=[t_MT[hb]])
                        for lvl in range(6):
                            lastl = (lvl == 5)
                            P.pe(lambda e, pi=pi, hb=hb: e.matmul(pi[:, 128:256], PT[hb][:], Pm[hb][:], start=True, stop=True),
                                 r=[t_PT[hb], t_Pm[hb]], w=[t_pI[hb]])
                            if not lastl:
                                P.pe(lambda e, pi=pi, hb=hb: e.matmul(pi[:, 256:384], Pm[hb][:], PT[hb][:], start=True,
                                                                      stop=True),
                                     r=[t_PT[hb], t_Pm[hb]], w=[t_pI[hb]])
                            P.act(lambda e, pi=pi, hb=hb: e.copy(Pm[hb][:], pi[:, 128:256]), r=[t_pI[hb]], w=[t_Pm[hb]])
                            if not lastl:
                                P.dve(lambda e, pi=pi, hb=hb: e.tensor_copy(PT[hb][:], pi[:, 256:384]),
                                      r=[t_pI[hb]], w=[t_PT[hb]])
                            P.pe(lambda e, pi=pi, hb=hb: e.matmul(pi[:, 384:512], Pm[hb][:], MT[hb][:], start=True, stop=True),
                                 r=[t_Pm[hb], t_MT[hb]], w=[t_pI[hb]])
                            P.dve(lambda e, pi=pi, hb=hb: e.tensor_tensor(MT[hb][:], pi[:, 384:512], MT[hb][:], ALU.add),
                                  r=[t_pI[hb], t_MT[hb]], w=[t_MT[hb]])
                    for ab in range(2):
                        po = ab * 64
                        hb = hbs[ab]
                        h = 2 * pc + ab
                        P.pe(lambda e, AR=AR, po=po, pc=pc, ab=ab: e.matmul(
                            pC[:, ab * 64:(ab + 1) * 64], AR[po:po + 64, 0:128], Tb[pc][po:po + 64, :],
                            start=True, stop=False), r=[t_AR[pc], t_Tb[pc]], w=[t_pC])
                        P.pe(lambda e, hb=hb, bb=bb, h=h, ab=ab: e.matmul(
                            pC[:, ab * 64:(ab + 1) * 64], AakT[hb][:], vsb[bb][:, h * 64:(h + 1) * 64],
                            start=False, stop=True), r=[t_Aak[hb], t_vsb[bb]], w=[t_pC])
                    P.act(lambda e, par=par: e.copy(rhs_sb[par][:], pC[:, 0:128]), r=[t_pC], w=[t_rhs[par]])
                    for ab in range(2):
                        hb = hbs[ab]
                        P.pe(lambda e, hb=hb, par=par, ab=ab: e.matmul(
                            pC[:, 128 + ab * 64:128 + (ab + 1) * 64], MT[hb][:], rhs_sb[par][:, ab * 64:(ab + 1) * 64],
                            start=True, stop=True), r=[t_MT[hb], t_rhs[par]], w=[t_pC])
                    P.dve(lambda e, par=par: e.tensor_copy(u_sb[par][:], pC[:, 128:256]), r=[t_pC], w=[t_usb[par]])
                    for ab in range(2):
                        po = ab * 64
                        hb = hbs[ab]
                        h = 2 * pc + ab
                        ycol = slice(h * 64, (h + 1) * 64)
                        P.pe(lambda e, AR=AR, po=po, pc=pc, ycol=ycol: e.matmul(
                            pY[:, ycol], AR[po:po + 64, 128:256], Tb[pc][po:po + 64, :], start=True, stop=False),
                            r=[t_AR[pc], t_Tb[pc]], w=[t_pY])
                        P.pe(lambda e, hb=hb, par=par, ab=ab, ycol=ycol: e.matmul(
                            pY[:, ycol], ArbT[hb][:], u_sb[par][:, ab * 64:(ab + 1) * 64], start=False, stop=False),
                            r=[t_Arb[hb], t_usb[par]], w=[t_pY])
                        P.pe(lambda e, hb=hb, bb=bb, ycol=ycol: e.matmul(
                            pY[:, ycol], ArkT[hb][:], vsb[bb][:, ycol], start=False, stop=True),
                            r=[t_Ark[hb], t_vsb[bb]], w=[t_pY])
                    P.pe(lambda e, pc=pc, blk=blk: e.transpose(pTr[:, 0:128], BT[pc][:, blk], idb),
                         r=[t_BT[pc], t_cb], w=[t_pTr])
                    P.pe(lambda e, pc=pc, blk=blk: e.transpose(pTr[:, 128:256], KTt[pc][:, blk], idb),
                         r=[t_KT[pc], t_cb], w=[t_pTr])
                    P.act(lambda e, par=par: e.copy(Bp[par][0][:, 0:64], pTr[:, 0:64]), r=[t_pTr], w=[t_Bp[par][0]])
                    P.act(lambda e, par=par: e.copy(Bp[par][1][:, 64:128], pTr[:, 64:128]), r=[t_pTr], w=[t_Bp[par][1]])
                    P.dve(lambda e, par=par: e.tensor_copy(Kp[par][0][:, 0:64], pTr[:, 128:192]),
                          r=[t_pTr], w=[t_Kp[par][0]])
                    P.dve(lambda e, par=par: e.tensor_copy(Kp[par][1][:, 64:128], pTr[:, 192:256]),
                          r=[t_pTr], w=[t_Kp[par][1]])
                    for ab in range(2):
                        h = 2 * pc + ab
                        P.pe(lambda e, par=par, ab=ab: e.matmul(pC[:, 256:320], Bp[par][ab][:],
                                                                u_sb[par][:, ab * 64:(ab + 1) * 64],
                                                                start=(ab == 0), stop=False),
                             r=[t_Bp[par][ab], t_usb[par]], w=[t_pC])
                    for ab in range(2):
                        h = 2 * pc + ab
                        P.pe(lambda e, par=par, ab=ab, bb=bb, h=h: e.matmul(pC[:, 256:320], Kp[par][ab][:],
                                                                           vsb[bb][:, h * 64:(h + 1) * 64],
                                                                           start=False, stop=(ab == 1)),
                             r=[t_Kp[par][ab], t_vsb[bb]], w=[t_pC])
                    P.dve(lambda e, pc=pc: e.tensor_tensor(Tf[pc][:], pC[:, 256:320], Tf[pc][:], ALU.add),
                          r=[t_pC, t_Tf[pc]], w=[t_Tf[pc]])
                    P.pool(lambda e, pc=pc, c=c: e.tensor_scalar(Tf[pc][:], Tf[pc][:], WC[pc][:, c:c + 1], None, ALU.mult),
                           r=[t_Tf[pc], t_WC[pc]], w=[t_Tf[pc]])
                    P.act(lambda e, pc=pc: e.copy(Tb[pc][:], Tf[pc][:]), r=[t_Tf[pc]], w=[t_Tb[pc]])
                P.act(lambda e, bb=bb: e.copy(yy[bb][:], pY[:]), r=[t_pY], w=[t_yy[bb]])
                head_norm_tm(P, yy[bb], t_yy[bb], 8, 64, 64e-5, st[bb], t_st[bb])
                P.pool(lambda e, bb=bb: e.tensor_tensor(yy[bb][:], yy[bb][:], bt[:, 1, :], ALU.mult),
                       r=[t_yy[bb], t_bt], w=[t_yy[bb]])
                P.pool(lambda e, bb=bb: e.tensor_tensor(yy[bb][:], yy[bb][:], bt[:, 2, :], ALU.add),
                       r=[t_yy[bb], t_bt], w=[t_yy[bb]])
                for h in range(8):
                    hc = slice(h * 64, (h + 1) * 64)
                    P.dve(lambda e, bb=bb, h=h, hc=hc: e.scalar_tensor_tensor(
                        yy[bb][:, hc], vs[bb][:, hc], bs[bb][:, h:h + 1], yy[bb][:, hc], ALU.mult, ALU.add),
                        r=[t_vs[bb], t_bs[bb], t_yy[bb]], w=[t_yy[bb]])
                P.dve(lambda e, bb=bb: e.tensor_tensor(yo[bb][:], yy[bb][:], gg[bb][:], ALU.mult),
                      r=[t_yy[bb], t_gg[bb]], w=[t_yo[bb]])
                P.dma("sp", yt[r0:r0 + 128, 0:512], yo[bb][:], r=[t_yo[bb]], sig=t_yo[bb])
    P.end()


def odd_layer_mix(P, cx, l, o_idx, x_in, x_out, T, S):
    n = "L%d" % l
    proj_phase(P, cx, n + "pj", x_in, cx.od_w_in[o_idx], ODD_IN, ODD_F, ODD_T, cx.pf, cx.pt, T)
    rwkv_phase(P, cx, n + "rk", o_idx, cx.pf, cx.pt, cx.yt, T, S)
    gla_phase(P, cx, n + "gl", o_idx, cx.pf, cx.pt, cx.yt, T, S)
    outproj_phase(P, cx, n + "op", x_in, x_out, cx.od_w_out[o_idx], None, 0, cx.yt, 1024,
                  cx.ln_g[l, 1], cx.ln_b[l, 1], T)


def build_full(T, S):
    nc = bass.Bass("TRN2", target_bir_lowering=False)
    cx = Ctx()
    cx.T = T
    dt = nc.dram_tensor
    cx.x = dt("x", [T, D], F32, kind="ExternalInput").ap()
    cx.consts = dt("consts", [128, NCONST], F32, kind="ExternalInput").ap()
    for nm, shape in PARAM_SHAPES:
        setattr(cx, nm, dt(nm, list(shape), F32, kind="ExternalInput").ap())
    cx.out = dt("out", [T, D], F32, kind="ExternalOutput").ap()
    cx.xa = dt("xa", [T, D], F32).ap()
    cx.xb = dt("xb", [T, D], F32).ap()
    cx.pf = dt("pf", [2048, T], F32).ap()
    cx.pt = dt("pt", [T, 1536], F32).ap()
    cx.yf = dt("yf", [512, T], BF16).ap()
    cx.yt = dt("yt", [T, 1024], BF16).ap()
    P = Prog(nc)
    cx.P = P
    cur = cx.x
    bufs = [cx.xa, cx.xb]
    nb = 0

    def nxt(last=False):
        nonlocal nb
        if last:
            return cx.out
        b = bufs[nb % 2]
        nb += 1
        return b

    for l in range(DEPTH):
        d = nxt()
        ffn_phase(P, cx, "L%df1" % l, cur, d, cx.ffn1_wi[l], cx.ffn1_wo[l], cx.ln_g[l, 0], cx.ln_b[l, 0], T)
        cur = d
        d = nxt()
        if l % 2 == 0:
            even_layer_mix(P, cx, l, l // 2, cur, d, T, S)
        else:
            odd_layer_mix(P, cx, l, l // 2, cur, d, T, S)
        cur = d
        d = nxt(last=(l == DEPTH - 1))
        ffn_phase(P, cx, "L%df2" % l, cur, d, cx.ffn2_wi[l], cx.ffn2_wo[l], cx.ln_g[l, 2], cx.ln_b[l, 2], T)
        cur = d
    P.ges.close()
    return nc, cx


def kernel(**inputs):
    x = np.ascontiguousarray(np.asarray(inputs["x"], dtype=np.float32))
    B, S, _ = x.shape
    per = B // NCORES
    T = per * S
    nc, cx = build_full(T, S)
    consts = make_consts()
    params = {nm: np.ascontiguousarray(np.asarray(inputs[nm], dtype=np.float32)) for nm, _ in PARAM_SHAPES}
    in_maps = []
    for c in range(NCORES):
        m = dict(params)
        m["x"] = x[c * per:(c + 1) * per].reshape(T, D)
        m["consts"] = consts
        in_maps.append(m)
    res = run_bass_kernel_spmd(nc, in_maps, core_ids=list(range(NCORES)))
    outs = [np.asarray(r["out"]).reshape(per, S, D) for r in res.results]
    return np.concatenate(outs, axis=0).astype(np.float32)
```

```python
import contextlib
import math
import numpy as np
import ml_dtypes
import concourse.bass as bass
import concourse.mybir as mybir
from concourse.bass_utils import run_bass_kernel_spmd

F32 = mybir.dt.float32
BF16 = mybir.dt.bfloat16
AF = mybir.ActivationFunctionType
ALU = mybir.AluOpType
AX = mybir.AxisListType

D = 1024
DEPTH = 4
DFF = 2816
NCORES = 8
ALPHA = (2.0 * DEPTH) ** 0.25
LN_EPS = 1e-5
EVEN_IN = 3080
ODD_IN = 3216
ENGS = ["pe", "act", "dve", "pool", "sp"]


class Tok:
    __slots__ = ("name", "lw", "rd", "ds", "x")

    def __init__(self, name):
        self.name = name
        self.x = name.startswith("p") and not name.startswith("prm")
        self.lw = None
        self.rd = {}
        self.ds = None


class Op:
    __slots__ = ("eng", "fn", "waits", "signal", "sigval", "dma", "ds", "idx")

    def __init__(self, eng, fn, dma):
        self.eng = eng
        self.fn = fn
        self.dma = dma
        self.waits = []
        self.signal = False
        self.sigval = None
        self.ds = None


class Prog:
    def __init__(self, nc):
        self.nc = nc
        self.ges = contextlib.ExitStack()
        self.esem = {e: self.ges.enter_context(nc.semaphore("se_" + e)) for e in ENGS}
        self.ecnt = {e: 0 for e in ENGS}
        self.dfree = []
        self.dall = []
        self.nphase = 0
        self.ninst = 0

    def begin(self, name):
        self.pname = name
        self.ops = {e: [] for e in ENGS}
        self.toks = []
        self.pes = contextlib.ExitStack()
        self.nt = 0

    def tile(self, shape, dtype=F32, name=None):
        self.nt += 1
        nm = "%s_%s_%d" % (self.pname, name or "t", self.nt)
        return self.pes.enter_context(self.nc.sbuf_tensor(nm, list(shape), dtype))

    def psum(self, shape, dtype=F32, name=None):
        self.nt += 1
        nm = "%s_%s_%d" % (self.pname, name or "p", self.nt)
        return self.pes.enter_context(self.nc.psum_tensor(nm, list(shape), dtype))

    def tok(self, name="t"):
        t = Tok(name)
        self.toks.append(t)
        return t

    def toks_n(self, n, name="t"):
        return [self.tok("%s%d" % (name, i)) for i in range(n)]

    def _dsem(self, t):
        if t.ds is None:
            if self.dfree:
                t.ds = self.dfree.pop()
            else:
                s = self.ges.enter_context(self.nc.semaphore("sd_%d" % len(self.dall)))
                t.ds = [s, 0]
                self.dall.append(t.ds)
        return t.ds

    @staticmethod
    def _need(p, o, raw):
        if p.dma or o.dma:
            return True
        if p.eng != o.eng:
            return True
        if p.eng == "pe":
            return False
        return raw

    def op(self, eng, fn, r=(), w=(), dma=False, sig=None):
        o = Op(eng, fn, dma)
        deps = []
        w = list(w) + [t for t in r if t.x and not any(t is q for q in w)]
        for t in r:
            if t.lw is not None and self._need(t.lw, o, True):
                deps.append(t.lw)
        for t in w:
            if t.lw is not None and self._need(t.lw, o, False):
                deps.append(t.lw)
            for q in t.rd.values():
                if q is not o and self._need(q, o, False):
                    deps.append(q)
        seen = set()
        for p in deps:
            if id(p) in seen:
                continue
            seen.add(id(p))
            if p.dma:
                o.waits.append(("sem", p.ds[0], p.ds[1]))
            else:
                p.signal = True
                o.waits.append(("op", p))
        if dma:
            st = sig if sig is not None else (w[0] if (w and w[0] is not None) else r[0])
            o.ds = self._dsem(st)
            o.ds[1] += 16
        for t in r:
            key = ("dma", id(o.ds)) if dma else eng
            t.rd[key] = o
        for t in w:
            t.lw = o
            t.rd = {}
        self.ops[eng].append(o)
        return o

    def pe(self, fn, r=(), w=()):
        return self.op("pe", fn, r, w)

    def act(self, fn, r=(), w=()):
        return self.op("act", fn, r, w)

    def dve(self, fn, r=(), w=()):
        return self.op("dve", fn, r, w)

    def pool(self, fn, r=(), w=()):
        return self.op("pool", fn, r, w)

    def dma(self, eng, out, in_, r=(), w=(), sig=None):
        return self.op(eng, lambda e: e.dma_start(out=out, in_=in_), r, w, dma=True, sig=sig)

    def end(self):
        nc = self.nc
        last = {}
        for e in ENGS:
            if self.ops[e]:
                lo = None
                for o in reversed(self.ops[e]):
                    if not o.dma:
                        lo = o
                        break
                if lo is not None:
                    lo.signal = True
                    last[e] = lo
        for e in ENGS:
            for o in self.ops[e]:
                if o.signal and not o.dma:
                    self.ecnt[e] += 1
                    o.sigval = self.ecnt[e]
        used_ds = []
        for t in self.toks:
            if t.ds is not None and not any(t.ds is u for u in used_ds):
                used_ds.append(t.ds)
        final_waits = [(self.esem[e], last[e].sigval) for e in last]
        final_waits += [(ds[0], ds[1]) for ds in used_ds]
        prog = self

        def emit(engname, e):
            waited = {}
            for o in prog.ops[engname]:
                for wv in o.waits:
                    if wv[0] == "op":
                        sem, val = prog.esem[wv[1].eng], wv[1].sigval
                    else:
                        sem, val = wv[1], wv[2]
                    k = id(sem)
                    if waited.get(k, -1) >= val:
                        continue
                    e.wait_ge(sem, val)
                    waited[k] = val
                ins = o.fn(e)
                prog.ninst += 1
                if o.dma:
                    ins.then_inc(o.ds[0], 16)
                elif o.signal:
                    ins.then_inc(prog.esem[engname], 1)
            for sem, val in final_waits:
                if sem is prog.esem[engname]:
                    continue
                if waited.get(id(sem), -1) >= val:
                    continue
                e.wait_ge(sem, val)

        with nc.Block() as block:
            @block.tensor
            def _(e):
                emit("pe", e)

            @block.scalar
            def _(e):
                emit("act", e)

            @block.vector
            def _(e):
                emit("dve", e)

            @block.gpsimd
            def _(e):
                emit("pool", e)

            @block.sync
            def _(e):
                emit("sp", e)
        for ds in used_ds:
            self.dfree.append(ds)
        self.pes.close()
        self.nphase += 1


def make_consts():
    c = {}
    c["ident"] = np.eye(128, dtype=np.float32)
    j = np.arange(128)[:, None]
    i = np.arange(128)[None, :]
    c["mask_le"] = (j <= i).astype(np.float32)
    c["mask_lt"] = (j < i).astype(np.float32)
    c["ones"] = np.ones((128, 128), np.float32)
    c["blk64"] = ((j // 64) == (i // 64)).astype(np.float32)
    c["mask_gt"] = (j > i).astype(np.float32)
    c["blkind"] = ((j // 64) == i).astype(np.float32)[:, :128]
    c["zeros"] = np.zeros((128, 128), np.float32)
    return np.concatenate([c[k] for k in ["ident", "mask_le", "mask_lt", "ones", "blk64", "mask_gt", "blkind", "zeros"]], axis=1)


C_IDENT, C_LE, C_LT, C_ONES, C_BLK64, C_GT, C_BLKIND, C_ZEROS = 0, 128, 256, 384, 512, 640, 768, 896
NCONST = 1024


class Ctx:
    pass


def ffn_phase(P, cx, name, x_in, x_out, wi_d, wo_d, g_d, b_d, T):
    P.begin(name)
    TT = 512 if T % 512 == 0 else 256
    NS = TT // 128
    ntile = T // TT
    NG = DFF // 128
    wi = P.tile([128, 8, 2 * DFF], BF16, "wi")
    wo = P.tile([128, NG, D], BF16, "wo")
    gb = P.tile([128, 2, D], F32, "gb")
    idt = P.tile([128, 128], F32, "idt")
    hT = P.tile([128, NG, TT], BF16, "hT")
    sg = [P.tile([128, TT], F32, "sg") for _ in range(2)]
    stat = [P.tile([128, 16], F32, "stat") for _ in range(2)]
    p_gu = [P.psum([128, 512], F32, "pgu") for _ in range(4)]
    p_y = [P.psum([128, 512], F32, "py") for _ in range(2)]
    t_wi, t_wo, t_gb, t_idt = P.tok("wi"), P.tok("wo"), P.tok("gb"), P.tok("idt")
    t_hT = P.toks_n(NG, "hT")
    t_sg, t_stat = P.toks_n(2, "sg"), P.toks_n(2, "stat")
    t_pgu, t_py = P.toks_n(4, "pgu"), P.toks_n(2, "py")
    P.dma("sp", idt[:], cx.consts[:, C_IDENT:C_IDENT + 128], w=[t_idt])
    P.dma("sp", gb[:, 0, :], g_d.partition_broadcast(128), w=[t_gb])
    P.dma("sp", gb[:, 1, :], b_d.partition_broadcast(128), w=[t_gb])
    xl = XLoader(P, x_in, TT, idt, t_idt, want_T=True, nbx=1, nbT=1, ident=idt[:])
    xl.load(0)
    wi_v = wi_d.rearrange("(ko ki) n -> ki ko n", ki=128)
    wo_v = wo_d.rearrange("(ko ki) n -> ki ko n", ki=128)
    for ko in range(8):
        P.dma("pool", wi[:, ko, :], wi_v[:, ko, :], w=[t_wi])
    for k0 in range(0, NG, 2):
        P.dma("pool", wo[:, k0:k0 + 2, :], wo_v[:, k0:k0 + 2, :], w=[t_wo])
    c_y = 0.5 / ALPHA
    eps2 = LN_EPS / (ALPHA * ALPHA)
    ngu = 0
    nst = 0
    xt, xT, t_xT = xl.xt[0], xl.xT[0], xl.t_xT[0]
    for i in range(ntile):
        xl.transpose(i)
        for g in range(NG):
            pb = ngu % 2
            ngu += 1
            for half, col0 in ((0, g * 128), (1, DFF + g * 128)):
                bank = 2 * pb + half
                for kc in range(8):
                    P.pe(lambda e, bank=bank, col0=col0, kc=kc: e.matmul(
                        p_gu[bank][:, 0:TT], wi[:, kc, col0:col0 + 128], xT[:, kc, :],
                        start=(kc == 0), stop=(kc == 7)),
                        r=[t_wi, t_xT], w=[t_pgu[bank]])
            P.act(lambda e, pb=pb: e.activation(sg[pb][:], p_gu[2 * pb][:, 0:TT], AF.Silu),
                  r=[t_pgu[2 * pb]], w=[t_sg[pb]])
            P.dve(lambda e, pb=pb, g=g: e.tensor_tensor(hT[:, g, :], p_gu[2 * pb + 1][:, 0:TT], sg[pb][:], ALU.mult),
                  r=[t_pgu[2 * pb + 1], t_sg[pb]], w=[t_hT[g]])
        for s in range(NS):
            for half in range(2):
                for g in range(NG):
                    P.pe(lambda e, g=g, s=s, half=half: e.matmul(
                        p_y[half][:], hT[:, g, s * 128:(s + 1) * 128], wo[:, g, half * 512:(half + 1) * 512],
                        start=(g == 0), stop=(g == NG - 1)),
                        r=[t_wo, t_hT[g]], w=[t_py[half]])
            zt = xt[:, s, :]
            t_z = xl.t_xt[0][s]
            for half in range(2):
                P.dve(lambda e, zt=zt, half=half: e.scalar_tensor_tensor(
                    zt[:, half * 512:(half + 1) * 512], p_y[half][:], c_y, zt[:, half * 512:(half + 1) * 512],
                    ALU.mult, ALU.add), r=[t_py[half], t_z], w=[t_z])
            ss = nst % 2
            nst += 1
            ln_apply(P, zt, t_z, stat[ss], t_stat[ss], gb, t_gb, eps2)
            r0 = i * TT + s * 128
            P.dma("sp", x_out[r0:r0 + 128, :], zt, r=[t_z], sig=t_z)
        if i + 1 < ntile:
            xl.load(i + 1)
    P.end()


def ln_apply(P, z, t_z, st, t_st, gb, t_gb, eps):
    for h in range(2):
        P.dve(lambda e, h=h: e.bn_stats(st[:, h * 6:(h + 1) * 6], z[:, h * 512:(h + 1) * 512]),
              r=[t_z], w=[t_st])
    P.dve(lambda e: e.bn_aggr(st[:, 12:14], st[:, 0:12]), r=[t_st], w=[t_st])
    P.act(lambda e: e.activation(st[:, 14:15], st[:, 13:14], AF.Sqrt, bias=eps_ap(P, eps)), r=[t_st], w=[t_st])
    P.dve(lambda e: e.reciprocal(st[:, 15:16], st[:, 14:15]), r=[t_st], w=[t_st])
    zz = z[:, 0:D]
    P.dve(lambda e: e.tensor_scalar(zz, zz, st[:, 12:13], st[:, 15:16], ALU.subtract, ALU.mult),
          r=[t_z, t_st], w=[t_z])
    P.pool(lambda e: e.tensor_tensor(zz, zz, gb[:, 0, :], ALU.mult), r=[t_z, t_gb], w=[t_z])
    P.pool(lambda e: e.tensor_tensor(zz, zz, gb[:, 1, :], ALU.add), r=[t_z, t_gb], w=[t_z])


def eps_ap(P, eps):
    return eps


def build_program(T, layers=DEPTH, only=None):
    nc = bass.Bass("TRN2", target_bir_lowering=False)
    cx = Ctx()
    cx.T = T
    dt = nc.dram_tensor
    cx.x = dt("x", [T, D], F32, kind="ExternalInput").ap()
    cx.consts = dt("consts", [128, NCONST], F32, kind="ExternalInput").ap()
    cx.ffn1_wi = dt("ffn1_wi", [DEPTH, D, 2 * DFF], F32, kind="ExternalInput").ap()
    cx.ffn1_wo = dt("ffn1_wo", [DEPTH, DFF, D], F32, kind="ExternalInput").ap()
    cx.ffn2_wi = dt("ffn2_wi", [DEPTH, D, 2 * DFF], F32, kind="ExternalInput").ap()
    cx.ffn2_wo = dt("ffn2_wo", [DEPTH, DFF, D], F32, kind="ExternalInput").ap()
    cx.ln_g = dt("ln_g", [DEPTH, 3, D], F32, kind="ExternalInput").ap()
    cx.ln_b = dt("ln_b", [DEPTH, 3, D], F32, kind="ExternalInput").ap()
    cx.out = dt("out", [T, D], F32, kind="ExternalOutput").ap()
    cx.xa = dt("xa", [T, D], F32).ap()
    cx.xb = dt("xb", [T, D], F32).ap()
    P = Prog(nc)
    cur = cx.x
    if only == "ffn":
        ffn_phase(P, cx, "f1", cur, cx.out, cx.ffn1_wi[0], cx.ffn1_wo[0], cx.ln_g[0, 0], cx.ln_b[0, 0], T)
    P.ges.close()
    cx.P = P
    return nc, cx


def colvec(ap1d):
    return ap1d.rearrange("(p o) -> p o", o=1)


class XLoader:
    def __init__(self, P, x_in, TT, cst, t_cst, want_T=True, nbx=2, nbT=2, ident=None):
        self.P, self.x_in, self.TT = P, x_in, TT
        self.NS = TT // 128
        self.cst, self.t_cst = cst, t_cst
        self.nbx, self.nbT = nbx, nbT
        self.ident = ident if ident is not None else cst[:, C_IDENT:C_IDENT + 128]
        self.xt = [P.tile([128, self.NS, D], F32, "xt") for _ in range(nbx)]
        self.t_xt = [P.toks_n(self.NS, "xt") for _ in range(nbx)]
        self.want_T = want_T
        if want_T:
            self.xT = [P.tile([128, 8, TT], BF16, "xT") for _ in range(nbT)]
            self.t_xT = P.toks_n(nbT, "xT")
            self.p_tr = [P.psum([128, 512], F32, "ptr") for _ in range(2)]
            self.t_ptr = P.toks_n(2, "ptr")
        self.ntr = 0

    def load(self, i):
        sl = i % self.nbx
        TT = self.TT
        for s in range(self.NS):
            r0 = i * TT + s * 128
            self.P.dma("sp", self.xt[sl][:, s, :], self.x_in[r0:r0 + 128, :], w=[self.t_xt[sl][s]])

    def transpose(self, i):
        P = self.P
        sl = i % self.nbx
        slT = i % self.nbT
        ident = self.ident
        for s in range(self.NS):
            for g4 in range(2):
                pb = self.ntr % 2
                self.ntr += 1
                for q in range(4):
                    kc = g4 * 4 + q
                    P.pe(lambda e, pb=pb, q=q, s=s, kc=kc, sl=sl: e.transpose(
                        self.p_tr[pb][:, q * 128:(q + 1) * 128], self.xt[sl][:, s, kc * 128:(kc + 1) * 128], ident),
                        r=[self.t_xt[sl][s], self.t_cst], w=[self.t_ptr[pb]])
                src = self.p_tr[pb][:].rearrange("p (a b) -> p a b", a=4)
                dst = self.xT[slT][:, g4 * 4:(g4 + 1) * 4, s * 128:(s + 1) * 128]
                if (s + g4) % 2 == 0:
                    P.act(lambda e, dst=dst, src=src: e.copy(dst, src), r=[self.t_ptr[pb]], w=[self.t_xT[slT]])
                else:
                    P.dve(lambda e, dst=dst, src=src: e.tensor_copy(dst, src), r=[self.t_ptr[pb]], w=[self.t_xT[slT]])


def proj_phase(P, cx, name, x_in, w_d, ncol, fgroups, tgroups, pf, pt, T):
    P.begin(name)
    TT = 512 if T % 512 == 0 else 256
    ntile = T // TT
    w = P.tile([128, 8, ncol], BF16, "w")
    cst = P.tile([128, NCONST], F32, "cst")
    t_w, t_cst = P.tok("w"), P.tok("cst")
    P.dma("sp", cst[:], cx.consts, w=[t_cst])
    xl = XLoader(P, x_in, TT, cst, t_cst)
    xl.load(0)
    w_v = w_d.rearrange("(ko ki) n -> ki ko n", ki=128)
    for ko in range(8):
        P.dma("pool", w[:, ko, :], w_v[:, ko, :], w=[t_w])
    NPS = 4
    pp = [P.psum([128, 512], F32, "pp") for _ in range(NPS)]
    t_pp = P.toks_n(NPS, "pp")
    NST = 4
    stg = [P.tile([128, 512], F32, "stg") for _ in range(NST)]
    t_stg = P.toks_n(NST, "stg")
    n = 0
    for i in range(ntile):
        sl = i % 2
        if i + 1 < ntile:
            xl.load(i + 1)
        xl.transpose(i)
        xT, t_xT = xl.xT[i % xl.nbT], xl.t_xT[i % xl.nbT]
        for (col0, m, row0) in fgroups:
            pb, sb = n % NPS, n % NST
            n += 1
            for kc in range(8):
                P.pe(lambda e, pb=pb, kc=kc, col0=col0, m=m, xT=xT: e.matmul(
                    pp[pb][0:m, 0:TT], w[:, kc, col0:col0 + m], xT[:, kc, :], start=(kc == 0), stop=(kc == 7)),
                    r=[t_w, t_xT], w=[t_pp[pb]])
            if n % 2 == 0:
                P.act(lambda e, pb=pb, sb=sb, m=m: e.copy(stg[sb][0:m, 0:TT], pp[pb][0:m, 0:TT]),
                      r=[t_pp[pb]], w=[t_stg[sb]])
            else:
                P.dve(lambda e, pb=pb, sb=sb, m=m: e.tensor_copy(stg[sb][0:m, 0:TT], pp[pb][0:m, 0:TT]),
                      r=[t_pp[pb]], w=[t_stg[sb]])
            P.dma("sp", pf[row0:row0 + m, i * TT:(i + 1) * TT], stg[sb][0:m, 0:TT], r=[t_stg[sb]], sig=t_stg[sb])
        for s in range(TT // 128):
            for (col0, nn, dst0) in tgroups:
                pb, sb = n % NPS, n % NST
                n += 1
                for kc in range(8):
                    P.pe(lambda e, pb=pb, kc=kc, col0=col0, nn=nn, s=s, xT=xT: e.matmul(
                        pp[pb][:, 0:nn], xT[:, kc, s * 128:(s + 1) * 128], w[:, kc, col0:col0 + nn],
                        start=(kc == 0), stop=(kc == 7)),
                        r=[t_w, t_xT], w=[t_pp[pb]])
                if n % 2 == 0:
                    P.act(lambda e, pb=pb, sb=sb, nn=nn: e.copy(stg[sb][:, 0:nn], pp[pb][:, 0:nn]),
                          r=[t_pp[pb]], w=[t_stg[sb]])
                else:
                    P.dve(lambda e, pb=pb, sb=sb, nn=nn: e.tensor_copy(stg[sb][:, 0:nn], pp[pb][:, 0:nn]),
                          r=[t_pp[pb]], w=[t_stg[sb]])
                r0 = i * TT + s * 128
                P.dma("sp", pt[r0:r0 + 128, dst0:dst0 + nn], stg[sb][:, 0:nn], r=[t_stg[sb]], sig=t_stg[sb])
    P.end()


def outproj_phase(P, cx, name, x_in, x_out, w_d, yf, nf, yt, ntk, g_d, b_d, T):
    P.begin(name)
    TT = 512 if T % 512 == 0 else 256
    NS = TT // 128
    ntile = T // TT
    w = P.tile([128, 8, D], BF16, "w")
    gb = P.tile([128, 2, D], F32, "gb")
    cst = P.tile([128, NCONST], F32, "cst")
    idb = P.tile([128, 128], BF16, "idb")
    t_w, t_gb, t_cst, t_idb = P.tok("w"), P.tok("gb"), P.tok("cst"), P.tok("idb")
    P.dma("sp", cst[:], cx.consts, w=[t_cst])
    P.dma("sp", gb[:, 0, :], g_d.partition_broadcast(128), w=[t_gb])
    P.dma("sp", gb[:, 1, :], b_d.partition_broadcast(128), w=[t_gb])
    P.act(lambda e: e.copy(idb[:], cst[:, C_IDENT:C_IDENT + 128]), r=[t_cst], w=[t_idb])
    xl = XLoader(P, x_in, TT, cst, t_cst, want_T=False)
    w_v = w_d.rearrange("(ko ki) n -> ki ko n", ki=128)
    for ko in range(8):
        P.dma("pool", w[:, ko, :], w_v[:, ko, :], w=[t_w])
    nfc = nf // 128
    ntc = ntk // 128
    yT = [P.tile([128, 8, TT], BF16, "yT") for _ in range(2)]
    t_yT = P.toks_n(2, "yT")
    ytk = [P.tile([128, NS, max(ntk, 128)], BF16, "ytk") for _ in range(2)]
    t_ytk = P.toks_n(2, "ytk")
    p_tr = [P.psum([128, 512], BF16, "ptr") for _ in range(2)]
    t_ptr = P.toks_n(2, "ptr")
    p_y = [P.psum([128, 512], F32, "py") for _ in range(4)]
    t_py = P.toks_n(4, "py")
    z = [P.tile([128, D], F32, "z") for _ in range(2)]
    stat = [P.tile([128, 16], F32, "stat") for _ in range(2)]
    t_z, t_stat = P.toks_n(2, "z"), P.toks_n(2, "stat")
    c_y = 1.0 / ALPHA
    eps2 = LN_EPS / (ALPHA * ALPHA)

    def load(i):
        sl = i % 2
        xl.load(i)
        if nfc:
            P.dma("sp", yT[sl][:, 0:nfc, :], yf[:, i * TT:(i + 1) * TT].rearrange("(a p) t -> p a t", p=128),
                  w=[t_yT[sl]])
        P.dma("sp", ytk[sl][:, :, 0:ntk], yt[i * TT:(i + 1) * TT, :].rearrange("(s p) c -> p s c", p=128),
              w=[t_ytk[sl]])

    load(0)
    ntr = 0
    nyb = 0
    for i in range(ntile):
        sl = i % 2
        if i + 1 < ntile:
            load(i + 1)
        for s in range(NS):
            for g4 in range(ntc // 4):
                pb = ntr % 2
                ntr += 1
                for q in range(4):
                    kc = g4 * 4 + q
                    P.pe(lambda e, pb=pb, q=q, s=s, kc=kc, sl=sl: e.transpose(
                        p_tr[pb][:, q * 128:(q + 1) * 128], ytk[sl][:, s, kc * 128:(kc + 1) * 128], idb[:]),
                        r=[t_ytk[sl], t_idb], w=[t_ptr[pb]])
                src = p_tr[pb][:].rearrange("p (a b) -> p a b", a=4)
                dst = yT[sl][:, nfc + g4 * 4:nfc + (g4 + 1) * 4, s * 128:(s + 1) * 128]
                if g4 % 2 == 0:
                    P.act(lambda e, dst=dst, src=src: e.copy(dst, src), r=[t_ptr[pb]], w=[t_yT[sl]])
                else:
                    P.dve(lambda e, dst=dst, src=src: e.tensor_copy(dst, src), r=[t_ptr[pb]], w=[t_yT[sl]])
        for s in range(NS):
            zs = nyb % 2
            banks = [(2 * (nyb % 2)), (2 * (nyb % 2) + 1)]
            nyb += 1
            for half in range(2):
                pb = banks[half]
                for kc in range(8):
                    P.pe(lambda e, pb=pb, kc=kc, s=s, half=half, sl=sl: e.matmul(
                        p_y[pb][:], yT[sl][:, kc, s * 128:(s + 1) * 128], w[:, kc, half * 512:(half + 1) * 512],
                        start=(kc == 0), stop=(kc == 7)),
                        r=[t_w, t_yT[sl]], w=[t_py[pb]])
            for half in range(2):
                pb = banks[half]
                P.dve(lambda e, pb=pb, zs=zs, half=half, s=s, sl=sl: e.scalar_tensor_tensor(
                    z[zs][:, half * 512:(half + 1) * 512], p_y[pb][:], c_y,
                    xl.xt[sl][:, s, half * 512:(half + 1) * 512], ALU.mult, ALU.add),
                    r=[t_py[pb], xl.t_xt[sl][s]], w=[t_z[zs]])
            ln_apply(P, z[zs], t_z[zs], stat[zs], t_stat[zs], gb, t_gb, eps2)
            r0 = i * TT + s * 128
            P.dma("sp", x_out[r0:r0 + 128, :], z[zs][:], r=[t_z[zs]], sig=t_z[zs])
    P.end()


def head_norm_tm(P, src, t_src, nh, dh, eps, st, t_st, engs=("dve",)):
    for h in range(nh):
        P.dve(lambda e, h=h: e.bn_stats(st[:, h * 6:(h + 1) * 6], src[:, h * dh:(h + 1) * dh]), r=[t_src], w=[t_st])
    mv0 = 6 * nh
    for h in range(nh):
        P.dve(lambda e, h=h: e.bn_aggr(st[:, mv0 + 2 * h:mv0 + 2 * h + 2], st[:, h * 6:(h + 1) * 6]),
              r=[t_st], w=[t_st])
    mv = st[:, mv0:mv0 + 2 * nh].rearrange("p (h two) -> p h two", two=2)
    sd0 = mv0 + 2 * nh
    P.act(lambda e: e.activation(st[:, sd0:sd0 + nh], mv[:, :, 1], AF.Sqrt, bias=eps), r=[t_st], w=[t_st])
    P.dve(lambda e: e.reciprocal(st[:, sd0 + nh:sd0 + 2 * nh], st[:, sd0:sd0 + nh]), r=[t_st], w=[t_st])
    for h in range(nh):
        P.dve(lambda e, h=h: e.tensor_scalar(src[:, h * dh:(h + 1) * dh], src[:, h * dh:(h + 1) * dh],
                                             st[:, mv0 + 2 * h:mv0 + 2 * h + 1],
                                             st[:, sd0 + nh + h:sd0 + nh + h + 1], ALU.subtract, ALU.mult),
              r=[t_src, t_st], w=[t_src])


GELU_C = 2.0 * math.sqrt(2.0 / math.pi)


def conv4(P, xp, t_xp, u, t_u, cw, t_cw, cb_col, n):
    P.act(lambda e: e.activation(u[:, 0:n], xp[:, 3:3 + n], AF.Identity, bias=cb_col, scale=cw[:, 3:4]),
          r=[t_xp, t_cw], w=[t_u])
    for j in range(3):
        P.dve(lambda e, j=j: e.scalar_tensor_tensor(u[:, 0:n], xp[:, j:j + n], cw[:, j:j + 1], u[:, 0:n],
                                                    ALU.mult, ALU.add), r=[t_xp, t_cw, t_u], w=[t_u])


def load_hist(P, xp, t_xp, src_rows, t0, n, seg_first, hist=3):
    if seg_first:
        P.dve(lambda e: e.memset(xp[:, 0:hist], 0.0), w=[t_xp])
        P.dma("sp", xp[:, hist:hist + n], src_rows[:, t0:t0 + n], w=[t_xp])
    else:
        P.dma("sp", xp[:, 0:hist + n], src_rows[:, t0 - hist:t0 + n], w=[t_xp])


def rglru_phase(P, cx, name, e_idx, pf, yf, T, S):
    P.begin(name)
    SEG = min(1024, S)
    nseg = S // SEG
    prm = P.tile([128, 4, 16], F32, "prm")
    t_prm = P.tok("prm")
    wbd = [P.tile([128, 2, 128], BF16, "wbd") for _ in range(4)]
    t_wbd = P.toks_n(4, "wbd")
    for c in range(4):
        for j in range(4):
            P.dma("sp", prm[:, c, j:j + 1], colvec(cx.rg_conv_w[e_idx, j, c * 128:(c + 1) * 128]), w=[t_prm])
        for k, v in ((4, cx.rg_conv_b), (5, cx.rg_ba), (6, cx.rg_bx), (7, cx.rg_lambda)):
            P.dma("sp", prm[:, c, k:k + 1], colvec(v[e_idx, c * 128:(c + 1) * 128]), w=[t_prm])
        P.dve(lambda e, c=c: e.memset(wbd[c][:], 0.0), w=[t_wbd[c]])
        for k, v in ((0, cx.rg_wa), (1, cx.rg_wx)):
            for b in range(2):
                P.dma("pool", wbd[c][b * 64:(b + 1) * 64, k, b * 64:(b + 1) * 64], v[e_idx, 2 * c + b], w=[t_wbd[c]])
    for c in range(4):
        P.act(lambda e, c=c: e.activation(prm[:, c, 10:11], prm[:, c, 7:8], AF.Exp, scale=-1.0), r=[t_prm], w=[t_prm])
        P.act(lambda e, c=c: e.activation(prm[:, c, 10:11], prm[:, c, 10:11], AF.Ln, bias=1.0), r=[t_prm], w=[t_prm])
        P.dve(lambda e, c=c: e.tensor_scalar(prm[:, c, 8:9], prm[:, c, 10:11], -8.0, None, ALU.mult), r=[t_prm], w=[t_prm])
        P.dve(lambda e, c=c: e.tensor_scalar(prm[:, c, 9:10], prm[:, c, 10:11], -16.0, None, ALU.mult), r=[t_prm], w=[t_prm])
    NB = 2
    mk = lambda nm, dt=F32, w=SEG: [P.tile([128, w], dt, nm) for _ in range(NB)]
    xp, u, ub, rr, ii, aa, bb, hh, xg, g1, yo = (mk("xp", F32, SEG + 3), mk("u"), mk("ub", BF16), mk("rr"), mk("ii"),
                                                 mk("aa"), mk("bb"), mk("hh"), mk("xg"), mk("g1"), mk("yo", BF16))
    tk = {k: P.toks_n(NB, k) for k in ["xp", "u", "ub", "rr", "ii", "aa", "bb", "hh", "xg", "g1", "yo"]}
    hprev = P.tile([128, 8], F32, "hprev")
    t_hp = P.tok("hp")
    pg = [P.psum([128, 512], F32, "pg") for _ in range(4)]
    t_pg = P.toks_n(4, "pg")
    npg = 0
    it = 0
    for sq in range(T // S):
        for c in range(4):
            for sg_i in range(nseg):
                b = it % NB
                it += 1
                t0 = sq * S + sg_i * SEG
                load_hist(P, xp[b], tk["xp"][b], pf[c * 128:(c + 1) * 128, :], t0, SEG, sg_i == 0)
                P.dma("sp", xg[b][:], pf[512 + c * 128:512 + (c + 1) * 128, t0:t0 + SEG], w=[tk["xg"][b]])
                conv4(P, xp[b], tk["xp"][b], u[b], tk["u"][b], prm[:, c, 0:4], t_prm, prm[:, c, 4:5], SEG)
                P.act(lambda e, b=b: e.copy(ub[b][:], u[b][:]), r=[tk["u"][b]], w=[tk["ub"][b]])
                for sp in range(SEG // 512 if SEG >= 512 else 1):
                    w_ = min(512, SEG)
                    cs = slice(sp * w_, (sp + 1) * w_)
                    for k, dst, tdst, bcol in ((0, rr, "rr", 5), (1, ii, "ii", 6)):
                        pb = npg % 4
                        npg += 1
                        P.pe(lambda e, pb=pb, k=k, c=c, b=b, cs=cs, w_=w_: e.matmul(
                            pg[pb][:, 0:w_], wbd[c][:, k, :], ub[b][:, cs], start=True, stop=True),
                            r=[t_wbd[c], tk["ub"][b]], w=[t_pg[pb]])
                        P.act(lambda e, pb=pb, dst=dst, b=b, cs=cs, w_=w_, c=c, bcol=bcol: e.activation(
                            dst[b][:, cs], pg[pb][:, 0:w_], AF.Sigmoid, bias=prm[:, c, bcol:bcol + 1]),
                            r=[t_pg[pb], t_prm], w=[tk[tdst][b]])
                P.act(lambda e, b=b, c=c: e.activation(aa[b][:], rr[b][:], AF.Exp, scale=prm[:, c, 8:9]),
                      r=[tk["rr"][b], t_prm], w=[tk["aa"][b]])
                P.act(lambda e, b=b, c=c: e.activation(bb[b][:], rr[b][:], AF.Exp, scale=prm[:, c, 9:10]),
                      r=[tk["rr"][b], t_prm], w=[tk["bb"][b]])
                P.act(lambda e, b=b: e.activation(bb[b][:], bb[b][:], AF.Sqrt, bias=1.0, scale=-1.0),
                      r=[tk["bb"][b]], w=[tk["bb"][b]])
                P.pool(lambda e, b=b: e.tensor_tensor(ii[b][:], ii[b][:], u[b][:], ALU.mult),
                       r=[tk["ii"][b], tk["u"][b]], w=[tk["ii"][b]])
                P.dve(lambda e, b=b: e.tensor_tensor(bb[b][:], bb[b][:], ii[b][:], ALU.mult),
                      r=[tk["bb"][b], tk["ii"][b]], w=[tk["bb"][b]])
                if sg_i == 0:
                    P.dve(lambda e, b=b: e.tensor_tensor_scan(hh[b][:], aa[b][:], bb[b][:], 0.0, ALU.mult, ALU.add),
                          r=[tk["aa"][b], tk["bb"][b]], w=[tk["hh"][b]])
                else:
                    P.dve(lambda e, b=b: e.tensor_tensor_scan(hh[b][:], aa[b][:], bb[b][:], hprev[:, 0:1],
                                                              ALU.mult, ALU.add),
                          r=[tk["aa"][b], tk["bb"][b], t_hp], w=[tk["hh"][b]])
                P.dve(lambda e, b=b: e.tensor_copy(hprev[:, 0:1], hh[b][:, SEG - 1:SEG]), r=[tk["hh"][b]], w=[t_hp])
                P.pool(lambda e, b=b: e.tensor_tensor(g1[b][:], xg[b][:], xg[b][:], ALU.mult),
                       r=[tk["xg"][b]], w=[tk["g1"][b]])
                P.pool(lambda e, b=b: e.tensor_scalar(g1[b][:], g1[b][:], 0.044715, 1.0, ALU.mult, ALU.add),
                       r=[tk["g1"][b]], w=[tk["g1"][b]])
                P.pool(lambda e, b=b: e.tensor_tensor(g1[b][:], g1[b][:], xg[b][:], ALU.mult),
                       r=[tk["g1"][b], tk["xg"][b]], w=[tk["g1"][b]])
                P.act(lambda e, b=b: e.activation(g1[b][:], g1[b][:], AF.Sigmoid, scale=GELU_C),
                      r=[tk["g1"][b]], w=[tk["g1"][b]])
                P.pool(lambda e, b=b: e.tensor_tensor(g1[b][:], g1[b][:], xg[b][:], ALU.mult),
                       r=[tk["g1"][b], tk["xg"][b]], w=[tk["g1"][b]])
                P.dve(lambda e, b=b: e.tensor_tensor(yo[b][:], g1[b][:], hh[b][:], ALU.mult),
                      r=[tk["g1"][b], tk["hh"][b]], w=[tk["yo"][b]])
                P.dma("sp", yf[c * 128:(c + 1) * 128, t0:t0 + SEG], yo[b][:], r=[tk["yo"][b]], sig=tk["yo"][b])
    P.end()


def mlstm_phase(P, cx, name, e_idx, pf, pt, yt, T, S):
    P.begin(name)
    H, DH = 4, 128
    SEG = min(1024, S)
    nseg = S // SEG
    NBK = SEG // 128
    cst = P.tile([128, NCONST], F32, "cst")
    t_cst = P.tok("cst")
    P.dma("sp", cst[:], cx.consts, w=[t_cst])
    idb = P.tile([128, 128], BF16, "idb")
    t_idb = P.tok("idb")
    P.act(lambda e: e.copy(idb[:], cst[:, C_IDENT:C_IDENT + 128]), r=[t_cst], w=[t_idb])
    mask_le = cst[:, C_LE:C_LE + 128]
    ones = cst[:, C_ONES:C_ONES + 128]
    prm = P.tile([128, 8, 8], F32, "prm")
    t_prm = P.tok("prm")
    for qh in range(8):
        for j in range(4):
            P.dma("sp", prm[:, qh, j:j + 1], colvec(cx.ml_conv_w[e_idx, j, qh * 128:(qh + 1) * 128]), w=[t_prm])
        P.dma("sp", prm[:, qh, 4:5], colvec(cx.ml_conv_b[e_idx, qh * 128:(qh + 1) * 128]), w=[t_prm])
    gbias = P.tile([128, 16], F32, "gbias")
    t_gbias = P.tok("gbias")
    P.dma("sp", gbias[:, 0:4], cx.ml_i_bias[e_idx].partition_broadcast(128), w=[t_gbias])
    P.dma("sp", gbias[:, 4:8], cx.ml_f_bias[e_idx].partition_broadcast(128), w=[t_gbias])
    P.dve(lambda e: e.tensor_scalar(gbias[:, 8:12], gbias[:, 4:8], -1.0, None, ALU.mult), r=[t_gbias], w=[t_gbias])
    ng = P.tile([128, 512], F32, "ng")
    t_ng = P.tok("ng")
    P.dma("sp", ng[:], cx.ml_norm_g[e_idx].partition_broadcast(128), w=[t_ng])
    QK = [P.tile([128, SEG], BF16, "QK") for _ in range(8)]
    t_QK = P.toks_n(8, "QK")
    xp = [P.tile([128, SEG + 3], F32, "xp") for _ in range(2)]
    uu = [P.tile([128, SEG], F32, "uu") for _ in range(2)]
    t_xp, t_uu = P.toks_n(2, "xp"), P.toks_n(2, "uu")
    ift = P.tile([128, NBK, 8], F32, "ift")
    gI = P.tile([128, NBK, 4], F32, "gI")
    gL = P.tile([128, NBK, 4], F32, "gL")
    gW = P.tile([128, NBK, 4], F32, "gW")
    gWb = P.tile([128, NBK, 4], BF16, "gWb")
    gE = P.tile([128, NBK, 4], F32, "gE")
    gD = P.tile([128, NBK, 4], F32, "gD")
    t_g = P.tok("gates")
    pG = P.psum([128, 512], F32, "pG")
    t_pG = P.tok("pG")
    Cf = [P.tile([128, 132], F32, "Cf") for _ in range(H)]
    Cb = [P.tile([128, 132], BF16, "Cb") for _ in range(H)]
    t_Cf, t_Cb = P.toks_n(H, "Cf"), P.toks_n(H, "Cb")
    vt = [P.tile([128, 512], F32, "vt") for _ in range(2)]
    ot = [P.tile([128, 512], F32, "ot") for _ in range(2)]
    t_vt, t_ot = P.toks_n(2, "vt"), P.toks_n(2, "ot")
    Vp = [P.tile([128, 512], BF16, "Vp") for _ in range(2)]
    t_Vp = P.toks_n(2, "Vp")
    Sm = [P.tile([128, 128], BF16, "Sm") for _ in range(2)]
    Kt = [P.tile([128, 128], BF16, "Kt") for _ in range(2)]
    t_Sm, t_Kt = P.toks_n(2, "Sm"), P.toks_n(2, "Kt")
    pS = [P.psum([128, 512], F32, "pS") for _ in range(2)]
    t_pS = P.toks_n(2, "pS")
    pK = P.psum([128, 512], BF16, "pK")
    t_pK = P.tok("pK")
    pN = P.psum([128, 512], F32, "pN")
    pD = P.psum([128, 512], F32, "pD")
    t_pN, t_pD = P.tok("pN"), P.tok("pD")
    pU = [P.psum([128, 512], F32, "pU") for _ in range(2)]
    t_pU = P.toks_n(2, "pU")
    hh = [P.tile([128, 512], F32, "hh") for _ in range(2)]
    t_hh = P.toks_n(2, "hh")
    st = [P.tile([128, 64], F32, "st") for _ in range(2)]
    t_st = P.toks_n(2, "st")
    dn = [P.tile([128, 16], F32, "dn") for _ in range(2)]
    t_dn = P.toks_n(2, "dn")
    yo = [P.tile([128, 512], BF16, "yo") for _ in range(2)]
    t_yo = P.toks_n(2, "yo")
    nit = 0
    nsk = 0
    nblk = 0
    for sq in range(T // S):
        for h in range(H):
            P.dve(lambda e, h=h: e.memset(Cf[h][:], 0.0), w=[t_Cf[h]])
            P.dve(lambda e, h=h: e.memset(Cb[h][:], 0.0), w=[t_Cb[h]])
        for sg_i in range(nseg):
            t0 = sq * S + sg_i * SEG
            for qh in range(8):
                b = nit % 2
                nit += 1
                load_hist(P, xp[b], t_xp[b], pf[1024 + qh * 128:1024 + (qh + 1) * 128, :], t0, SEG, sg_i == 0)
                conv4(P, xp[b], t_xp[b], uu[b], t_uu[b], prm[:, qh, 0:4], t_prm, prm[:, qh, 4:5], SEG)
                if qh < 4:
                    P.act(lambda e, b=b: e.activation(uu[b][:], uu[b][:], AF.Silu), r=[t_uu[b]], w=[t_uu[b]])
                    P.dve(lambda e, b=b, qh=qh: e.tensor_scalar(QK[qh][:], uu[b][:], DH ** -0.5, None, ALU.mult),
                          r=[t_uu[b]], w=[t_QK[qh]])
                else:
                    P.act(lambda e, b=b, qh=qh: e.activation(QK[qh][:], uu[b][:], AF.Silu), r=[t_uu[b]], w=[t_QK[qh]])
            P.dma("sp", ift[:], pt[t0:t0 + SEG, 1024:1032].rearrange("(c p) g -> p c g", p=128), w=[t_g])
            for h in range(H):
                P.dve(lambda e, h=h: e.tensor_scalar(gI[:, :, h], ift[:, :, h], gbias[:, h:h + 1], None, ALU.add),
                      r=[t_g, t_gbias], w=[t_g])
                P.act(lambda e, h=h: e.activation(gL[:, :, h], ift[:, :, 4 + h], AF.Exp, bias=gbias[:, 8 + h:9 + h],
                                                  scale=-1.0), r=[t_g, t_gbias], w=[t_g])
            P.act(lambda e: e.activation(gL[:], gL[:], AF.Ln, bias=1.0), r=[t_g], w=[t_g])
            gLf = gL[:].rearrange("p c h -> p (c h)")
            P.pe(lambda e: e.matmul(pG[:, 0:NBK * 4], mask_le, gLf, start=True, stop=True), r=[t_g, t_cst], w=[t_pG])
            P.pe(lambda e: e.matmul(pG[:, 256:256 + NBK * 4], ones, gLf, start=True, stop=True),
                 r=[t_g, t_cst], w=[t_pG])
            cum = pG[:, 0:NBK * 4].rearrange("p (c h) -> p c h", h=4)
            tot = pG[:, 256:256 + NBK * 4].rearrange("p (c h) -> p c h", h=4)
            P.dve(lambda e: e.tensor_tensor(gW[:], cum, gI[:], ALU.add), r=[t_pG, t_g], w=[t_g, t_pG])
            P.act(lambda e: e.activation(gW[:], gW[:], AF.Exp), r=[t_g], w=[t_g])
            P.dve(lambda e: e.tensor_copy(gWb[:], gW[:]), r=[t_g], w=[t_g])
            P.act(lambda e: e.activation(gE[:], cum, AF.Exp, scale=-1.0), r=[t_pG], w=[t_g, t_pG])
            P.act(lambda e: e.activation(gD[:], tot, AF.Exp, scale=-1.0), r=[t_pG], w=[t_g, t_pG])
            for c in range(NBK):
                bb = nblk % 2
                nblk += 1
                r0 = t0 + c * 128
                blk = slice(c * 128, (c + 1) * 128)
                P.dma("sp", vt[bb][:], pt[r0:r0 + 128, 0:512], w=[t_vt[bb]])
                P.dma("sp", ot[bb][:], pt[r0:r0 + 128, 512:1024], w=[t_ot[bb]])
                for h in range(H):
                    P.dve(lambda e, bb=bb, h=h, c=c: e.tensor_scalar(
                        Vp[bb][:, h * 128:(h + 1) * 128], vt[bb][:, h * 128:(h + 1) * 128], gW[:, c, h:h + 1], None,
                        ALU.mult), r=[t_vt[bb], t_g], w=[t_Vp[bb]])
                for h in range(H):
                    sk = nsk % 2
                    nsk += 1
                    qT, kT = QK[h], QK[4 + h]
                    P.pe(lambda e, sk=sk, kT=kT, qT=qT, blk=blk: e.matmul(pS[sk][:, 0:128], kT[:, blk], qT[:, blk],
                                                                         start=True, stop=True),
                         r=[t_QK[h], t_QK[4 + h]], w=[t_pS[sk]])
                    P.dve(lambda e, sk=sk: e.tensor_tensor(Sm[sk][:], pS[sk][:, 0:128], mask_le, ALU.mult),
                          r=[t_pS[sk], t_cst], w=[t_Sm[sk]])
                    P.pe(lambda e, kT=kT, blk=blk: e.transpose(pK[:, 0:128], kT[:, blk], idb[:]),
                         r=[t_QK[4 + h], t_idb], w=[t_pK])
                    P.act(lambda e, sk=sk: e.copy(Kt[sk][:], pK[:, 0:128]), r=[t_pK], w=[t_Kt[sk]])
                    P.pe(lambda e, sk=sk, bb=bb, h=h: e.matmul(pN[:, h * 128:(h + 1) * 128], Sm[sk][:],
                                                               Vp[bb][:, h * 128:(h + 1) * 128], start=True, stop=False),
                         r=[t_Sm[sk], t_Vp[bb]], w=[t_pN])
                    P.pe(lambda e, qT=qT, blk=blk, h=h: e.matmul(pN[:, h * 128:(h + 1) * 128], qT[:, blk],
                                                                 Cb[h][:, 0:128], start=False, stop=True),
                         r=[t_QK[h], t_Cb[h]], w=[t_pN])
                    P.pe(lambda e, sk=sk, h=h, c=c: e.matmul(pD[:, h:h + 1], Sm[sk][:], gWb[:, c, h:h + 1],
                                                             start=True, stop=False), r=[t_Sm[sk], t_g], w=[t_pD])
                    P.pe(lambda e, qT=qT, blk=blk, h=h: e.matmul(pD[:, h:h + 1], qT[:, blk], Cb[h][:, 128:129],
                                                                 start=False, stop=True),
                         r=[t_QK[h], t_Cb[h]], w=[t_pD])
                    P.pe(lambda e, sk=sk, bb=bb, h=h: e.matmul(pU[sk][:, 0:128], Kt[sk][:],
                                                               Vp[bb][:, h * 128:(h + 1) * 128], start=True, stop=True),
                         r=[t_Kt[sk], t_Vp[bb]], w=[t_pU[sk]])
                    P.pe(lambda e, sk=sk, h=h, c=c: e.matmul(pU[sk][:, 128:129], Kt[sk][:], gWb[:, c, h:h + 1],
                                                             start=True, stop=True), r=[t_Kt[sk], t_g], w=[t_pU[sk]])
                    P.dve(lambda e, sk=sk, h=h: e.tensor_tensor(Cf[h][:, 0:129], pU[sk][:, 0:129], Cf[h][:, 0:129],
                                                                ALU.add), r=[t_pU[sk], t_Cf[h]], w=[t_Cf[h]])
                    P.pool(lambda e, h=h, c=c: e.tensor_scalar(Cf[h][:, 0:129], Cf[h][:, 0:129], gD[:, c, h:h + 1],
                                                               None, ALU.mult), r=[t_Cf[h], t_g], w=[t_Cf[h]])
                    P.act(lambda e, h=h: e.copy(Cb[h][:, 0:129], Cf[h][:, 0:129]), r=[t_Cf[h]], w=[t_Cb[h]])
                P.dve(lambda e, bb=bb, c=c: e.tensor_tensor(dn[bb][:, 0:4], pD[:, 0:4], gE[:, c, :], ALU.mult),
                      r=[t_pD, t_g], w=[t_dn[bb]])
                P.dve(lambda e, bb=bb: e.tensor_scalar(dn[bb][:, 12:16], dn[bb][:, 0:4], -1.0, 1.0, ALU.mult, ALU.max),
                      r=[t_dn[bb]], w=[t_dn[bb]])
                P.dve(lambda e, bb=bb: e.tensor_scalar(dn[bb][:, 0:4], dn[bb][:, 0:4], 1.0, None, ALU.max),
                      r=[t_dn[bb]], w=[t_dn[bb]])
                P.dve(lambda e, bb=bb: e.tensor_tensor(dn[bb][:, 0:4], dn[bb][:, 0:4], dn[bb][:, 12:16], ALU.max),
                      r=[t_dn[bb]], w=[t_dn[bb]])
                P.dve(lambda e, bb=bb: e.reciprocal(dn[bb][:, 4:8], dn[bb][:, 0:4]), r=[t_dn[bb]], w=[t_dn[bb]])
                P.dve(lambda e, bb=bb, c=c: e.tensor_tensor(dn[bb][:, 8:12], dn[bb][:, 4:8], gE[:, c, :], ALU.mult),
                      r=[t_dn[bb], t_g], w=[t_dn[bb]])
                for h in range(H):
                    if h % 2 == 0:
                        P.act(lambda e, bb=bb, h=h: e.activation(hh[bb][:, h * 128:(h + 1) * 128],
                                                                 pN[:, h * 128:(h + 1) * 128], AF.Copy,
                                                                 scale=dn[bb][:, 8 + h:9 + h]),
                              r=[t_pN, t_dn[bb]], w=[t_hh[bb], t_pN])
                    else:
                        P.dve(lambda e, bb=bb, h=h: e.tensor_scalar(hh[bb][:, h * 128:(h + 1) * 128],
                                                                    pN[:, h * 128:(h + 1) * 128],
                                                                    dn[bb][:, 8 + h:9 + h], None, ALU.mult),
                              r=[t_pN, t_dn[bb]], w=[t_hh[bb], t_pN])
                head_norm_tm(P, hh[bb], t_hh[bb], H, DH, 1e-6, st[bb], t_st[bb])
                P.pool(lambda e, bb=bb: e.tensor_tensor(hh[bb][:], hh[bb][:], ng[:], ALU.mult),
                       r=[t_hh[bb], t_ng], w=[t_hh[bb]])
                P.act(lambda e, bb=bb: e.activation(ot[bb][:], ot[bb][:], AF.Sigmoid), r=[t_ot[bb]], w=[t_ot[bb]])
                P.dve(lambda e, bb=bb: e.tensor_tensor(yo[bb][:], hh[bb][:], ot[bb][:], ALU.mult),
                      r=[t_hh[bb], t_ot[bb]], w=[t_yo[bb]])
                P.dma("sp", yt[r0:r0 + 128, 0:512], yo[bb][:], r=[t_yo[bb]], sig=t_yo[bb])
    P.end()


EVEN_F = [(c * 128, 128, c * 128) for c in range(16)]
EVEN_T = [(2048, 512, 0), (2560, 512, 512), (3072, 8, 1024)]


def build_program(T, S=None, layers=DEPTH, only=None):
    nc = bass.Bass("TRN2", target_bir_lowering=False)
    cx = Ctx()
    cx.T = T
    S = S or T // 2
    dt = nc.dram_tensor

    def inp(nm, shape):
        setattr(cx, nm, dt(nm, list(shape), F32, kind="ExternalInput").ap())

    inp("x", [T, D])
    inp("consts", [128, NCONST])
    for nm, shape in PARAM_SHAPES:
        inp(nm, shape)
    cx.out = dt("out", [T, D], F32, kind="ExternalOutput").ap()
    cx.xa = dt("xa", [T, D], F32).ap()
    cx.xb = dt("xb", [T, D], F32).ap()
    cx.pf = dt("pf", [2048, T], F32).ap()
    cx.pt = dt("pt", [T, 1536], F32).ap()
    cx.yf = dt("yf", [512, T], BF16).ap()
    cx.yt = dt("yt", [T, 1024], BF16).ap()
    P = Prog(nc)
    cx.P = P
    if only == "ffn":
        ffn_phase(P, cx, "f1", cx.x, cx.out, cx.ffn1_wi[0], cx.ffn1_wo[0], cx.ln_g[0, 0], cx.ln_b[0, 0], T)
    elif only == "even":
        even_layer_mix(P, cx, 0, 0, cx.x, cx.out, T, S)
    elif only == "odd":
        odd_layer_mix(P, cx, 1, 0, cx.x, cx.out, T, S)
    P.ges.close()
    return nc, cx


def even_layer_mix(P, cx, l, e_idx, x_in, x_out, T, S):
    n = "L%d" % l
    proj_phase(P, cx, n + "pj", x_in, cx.ev_w_in[e_idx], EVEN_IN, EVEN_F, EVEN_T, cx.pf, cx.pt, T)
    rglru_phase(P, cx, n + "rg", e_idx, cx.pf, cx.yf, T, S)
    mlstm_phase(P, cx, n + "ml", e_idx, cx.pf, cx.pt, cx.yt, T, S)
    outproj_phase(P, cx, n + "op", x_in, x_out, cx.ev_w_out[e_idx], cx.yf, 512, cx.yt[:, 0:512], 512,
                  cx.ln_g[l, 1], cx.ln_b[l, 1], T)


PARAM_SHAPES = [
    ("ffn1_wi", (4, 1024, 5632)), ("ffn1_wo", (4, 2816, 1024)), ("ffn2_wi", (4, 1024, 5632)),
    ("ffn2_wo", (4, 2816, 1024)), ("ln_g", (4, 3, 1024)), ("ln_b", (4, 3, 1024)),
    ("ev_w_in", (2, 1024, 3080)), ("ev_w_out", (2, 1024, 1024)), ("rg_conv_w", (2, 4, 512)),
    ("rg_conv_b", (2, 512)), ("rg_wa", (2, 8, 64, 64)), ("rg_wx", (2, 8, 64, 64)), ("rg_ba", (2, 512)),
    ("rg_bx", (2, 512)), ("rg_lambda", (2, 512)), ("ml_conv_w", (2, 4, 1024)), ("ml_conv_b", (2, 1024)),
    ("ml_i_bias", (2, 4)), ("ml_f_bias", (2, 4)), ("ml_norm_g", (2, 512)), ("od_w_in", (2, 1024, 3216)),
    ("od_w_out", (2, 1024, 1024)), ("rk_mu", (2, 1664)), ("rk_w0", (2, 512)), ("rk_wB", (2, 32, 512)),
    ("rk_a0", (2, 512)), ("rk_aB", (2, 32, 512)), ("rk_gB", (2, 64, 512)), ("rk_k_k", (2, 512)),
    ("rk_k_a", (2, 512)), ("rk_r_k", (2, 8, 64)), ("rk_ln_g", (2, 512)), ("rk_ln_b", (2, 512)),
    ("gla_gB", (2, 16, 256)), ("gla_gb", (2, 256)), ("gla_norm_g", (2, 512)),
]


ODD_F = ([(c * 128, 128, c * 128) for c in range(8)] +
         [(1536, 32, 1024), (1568, 32, 1056), (1600, 64, 1088)] +
         [(1664 + c * 128, 128, 1152 + c * 128) for c in range(4)] +
         [(2688, 16, 1664)])
ODD_T = [(1024, 512, 0), (2176, 512, 512), (2704, 512, 1024)]


def cumsum_blocks(P, dst, t_dst, src, t_src, ones, t_cst, n):
    for c in range(n // 128):
        blk = slice(c * 128, (c + 1) * 128)
        P.dve(lambda e, blk=blk: e.tensor_tensor_scan(dst[:, blk], ones, src[:, blk], 0.0, ALU.mult, ALU.add),
              r=[t_src, t_cst], w=[t_dst])


def gla_phase(P, cx, name, o_idx, pf, pt, yt, T, S):
    P.begin(name)
    H = 4
    SEG = min(1024, S)
    nseg = S // SEG
    NBK = SEG // 128
    cst = P.tile([128, NCONST], F32, "cst")
    t_cst = P.tok("cst")
    P.dma("sp", cst[:], cx.consts, w=[t_cst])
    idb = P.tile([128, 128], BF16, "idb")
    t_idb = P.tok("idb")
    P.act(lambda e: e.copy(idb[:], cst[:, C_IDENT:C_IDENT + 128]), r=[t_cst], w=[t_idb])
    mask_le = cst[:, C_LE:C_LE + 128]
    ones = cst[:, C_ONES:C_ONES + 128]
    gBb = P.tile([16, 256], BF16, "gBb")
    t_gBb = P.tok("gBb")
    P.dma("pool", gBb[:], cx.gla_gB[o_idx], w=[t_gBb])
    prm = P.tile([128, 2, 4], F32, "prm")
    t_prm = P.tok("prm")
    for pc in range(2):
        P.dma("sp", prm[:, pc, 0:1], colvec(cx.gla_gb[o_idx, pc * 128:(pc + 1) * 128]), w=[t_prm])
        P.dve(lambda e, pc=pc: e.tensor_scalar(prm[:, pc, 1:2], prm[:, pc, 0:1], -1.0, None, ALU.mult),
              r=[t_prm], w=[t_prm])
    ng = P.tile([128, 512], F32, "ng")
    t_ng = P.tok("ng")
    P.dma("sp", ng[:], cx.gla_norm_g[o_idx].partition_broadcast(128), w=[t_ng])
    gd = P.tile([16, SEG], F32, "gd")
    gdb = P.tile([16, SEG], BF16, "gdb")
    t_gd, t_gdb = P.tok("gd"), P.tok("gdb")
    QD = [P.tile([128, SEG], BF16, "QD") for _ in range(2)]
    KI = [P.tile([128, SEG], BF16, "KI") for _ in range(2)]
    DEC = [P.tile([128, NBK], F32, "DEC") for _ in range(2)]
    t_QD, t_KI, t_DEC = P.toks_n(2, "QD"), P.toks_n(2, "KI"), P.toks_n(2, "DEC")
    qx = [P.tile([128, SEG], F32, "qx") for _ in range(2)]
    kx = [P.tile([128, SEG], F32, "kx") for _ in range(2)]
    LL = [P.tile([128, SEG], F32, "LL") for _ in range(2)]
    cum = [P.tile([128, SEG], F32, "cum") for _ in range(2)]
    EE = [P.tile([128, SEG], F32, "EE") for _ in range(2)]
    t_qx, t_kx, t_LL, t_cum, t_EE = (P.toks_n(2, "qx"), P.toks_n(2, "kx"), P.toks_n(2, "LL"), P.toks_n(2, "cum"),
                                     P.toks_n(2, "EE"))
    pz = [P.psum([128, 512], F32, "pz") for _ in range(2)]
    t_pz = P.toks_n(2, "pz")
    Sf = [P.tile([128, 128], F32, "Sf") for _ in range(2)]
    Sb = [P.tile([128, 128], BF16, "Sb") for _ in range(2)]
    t_Sf, t_Sb = P.toks_n(2, "Sf"), P.toks_n(2, "Sb")
    Kp = [[P.tile([128, 128], BF16, "Kp") for _ in range(2)] for _ in range(2)]
    t_Kp = [P.toks_n(2, "Kp") for _ in range(2)]
    vt = [P.tile([128, 512], F32, "vt") for _ in range(2)]
    vb = [P.tile([128, 512], BF16, "vb") for _ in range(2)]
    ot = [P.tile([128, 512], F32, "ot") for _ in range(2)]
    t_vt, t_vb, t_ot = P.toks_n(2, "vt"), P.toks_n(2, "vb"), P.toks_n(2, "ot")
    Sm = [P.tile([128, 128], BF16, "Sm") for _ in range(2)]
    t_Sm = P.toks_n(2, "Sm")
    pS = [P.psum([128, 512], F32, "pS") for _ in range(2)]
    t_pS = P.toks_n(2, "pS")
    pK = P.psum([128, 512], BF16, "pK")
    t_pK = P.tok("pK")
    pO = P.psum([128, 512], F32, "pO")
    t_pO = P.tok("pO")
    pU = [P.psum([128, 512], F32, "pU") for _ in range(2)]
    t_pU = P.toks_n(2, "pU")
    oo = [P.tile([128, 512], F32, "oo") for _ in range(2)]
    t_oo = P.toks_n(2, "oo")
    st = [P.tile([128, 64], F32, "st") for _ in range(2)]
    t_st = P.toks_n(2, "st")
    yo = [P.tile([128, 512], BF16, "yo") for _ in range(2)]
    t_yo = P.toks_n(2, "yo")
    for pc in range(2):
        for ab in range(2):
            P.dve(lambda e, pc=pc, ab=ab: e.memset(Kp[pc][ab][:], 0.0), w=[t_Kp[pc][ab]])
    npz = 0
    nsk = 0
    nblk = 0
    for sq in range(T // S):
        for pc in range(2):
            P.dve(lambda e, pc=pc: e.memset(Sf[pc][:], 0.0), w=[t_Sf[pc]])
            P.dve(lambda e, pc=pc: e.memset(Sb[pc][:], 0.0), w=[t_Sb[pc]])
        for sg_i in range(nseg):
            t0 = sq * S + sg_i * SEG
            P.dma("sp", gd[:], pf[1664:1680, t0:t0 + SEG], w=[t_gd])
            P.act(lambda e: e.copy(gdb[:], gd[:]), r=[t_gd], w=[t_gdb])
            for pc in range(2):
                P.dma("sp", qx[pc][:], pf[1152 + pc * 128:1152 + (pc + 1) * 128, t0:t0 + SEG], w=[t_qx[pc]])
                P.dma("sp", kx[pc][:], pf[1408 + pc * 128:1408 + (pc + 1) * 128, t0:t0 + SEG], w=[t_kx[pc]])
                w_ = min(512, SEG)
                for sp in range(SEG // w_):
                    cs = slice(sp * w_, (sp + 1) * w_)
                    pb = npz % 2
                    npz += 1
                    P.pe(lambda e, pb=pb, pc=pc, cs=cs: e.matmul(pz[pb][:, 0:w_], gBb[:, pc * 128:(pc + 1) * 128],
                                                                gdb[:, cs], start=True, stop=True),
                         r=[t_gBb, t_gdb], w=[t_pz[pb]])
                    P.act(lambda e, pb=pb, pc=pc, cs=cs: e.activation(LL[pc][:, cs], pz[pb][:, 0:w_], AF.Exp,
                                                                       bias=prm[:, pc, 1:2], scale=-1.0),
                          r=[t_pz[pb], t_prm], w=[t_LL[pc]])
                P.act(lambda e, pc=pc: e.activation(LL[pc][:], LL[pc][:], AF.Ln, bias=1.0), r=[t_LL[pc]], w=[t_LL[pc]])
                cumsum_blocks(P, cum[pc], t_cum[pc], LL[pc], t_LL[pc], ones, t_cst, SEG)
                P.act(lambda e, pc=pc: e.activation(EE[pc][:], cum[pc][:], AF.Exp, scale=-1.0 / 16.0),
                      r=[t_cum[pc]], w=[t_EE[pc]])
                P.dve(lambda e, pc=pc: e.scalar_tensor_tensor(QD[pc][:], qx[pc][:], 0.125, EE[pc][:], ALU.mult, ALU.mult),
                      r=[t_qx[pc], t_EE[pc]], w=[t_QD[pc]])
                P.act(lambda e, pc=pc: e.activation(DEC[pc][:], cum[pc][:].rearrange("p (c t) -> p c t", t=128)[:, :, 127],
                                                    AF.Exp, scale=-1.0 / 16.0), r=[t_cum[pc]], w=[t_DEC[pc]])
                P.act(lambda e, pc=pc: e.activation(EE[pc][:], cum[pc][:], AF.Exp, scale=1.0 / 16.0),
                      r=[t_cum[pc], t_QD[pc]], w=[t_EE[pc]])
                P.dve(lambda e, pc=pc: e.tensor_tensor(KI[pc][:], kx[pc][:], EE[pc][:], ALU.mult),
                      r=[t_kx[pc], t_EE[pc]], w=[t_KI[pc]])
            for c in range(NBK):
                bb = nblk % 2
                nblk += 1
                r0 = t0 + c * 128
                blk = slice(c * 128, (c + 1) * 128)
                P.dma("sp", vt[bb][:], pt[r0:r0 + 128, 512:1024], w=[t_vt[bb]])
                P.dma("sp", ot[bb][:], pt[r0:r0 + 128, 1024:1536], w=[t_ot[bb]])
                P.act(lambda e, bb=bb: e.copy(vb[bb][:], vt[bb][:]), r=[t_vt[bb]], w=[t_vb[bb]])
                for h in range(H):
                    pc, po = h // 2, (h % 2) * 64
                    sk = nsk % 2
                    nsk += 1
                    P.pe(lambda e, sk=sk, pc=pc, po=po, blk=blk: e.matmul(
                        pS[sk][:, 0:128], KI[pc][po:po + 64, blk], QD[pc][po:po + 64, blk], start=True, stop=True),
                        r=[t_KI[pc], t_QD[pc]], w=[t_pS[sk]])
                    P.dve(lambda e, sk=sk: e.tensor_tensor(Sm[sk][:], pS[sk][:, 0:128], mask_le, ALU.mult),
                          r=[t_pS[sk], t_cst], w=[t_Sm[sk]])
                    P.pe(lambda e, sk=sk, bb=bb, h=h: e.matmul(pO[:, h * 128:(h + 1) * 128], Sm[sk][:],
                                                               vb[bb][:, h * 128:(h + 1) * 128], start=True, stop=False),
                         r=[t_Sm[sk], t_vb[bb]], w=[t_pO])
                    P.pe(lambda e, pc=pc, po=po, blk=blk, h=h: e.matmul(
                        pO[:, h * 128:(h + 1) * 128], QD[pc][po:po + 64, blk], Sb[pc][po:po + 64, :],
                        start=False, stop=True), r=[t_QD[pc], t_Sb[pc]], w=[t_pO])
                for pc in range(2):
                    P.pe(lambda e, pc=pc, blk=blk: e.transpose(pK[:, pc * 128:(pc + 1) * 128], KI[pc][:, blk], idb[:]),
                         r=[t_KI[pc], t_idb], w=[t_pK])
                    P.act(lambda e, pc=pc: e.copy(Kp[pc][0][:, 0:64], pK[:, pc * 128:pc * 128 + 64]),
                          r=[t_pK], w=[t_Kp[pc][0], t_pK])
                    P.dve(lambda e, pc=pc: e.tensor_copy(Kp[pc][1][:, 64:128], pK[:, pc * 128 + 64:pc * 128 + 128]),
                          r=[t_pK], w=[t_Kp[pc][1], t_pK])
                    for ab in range(2):
                        h = 2 * pc + ab
                        P.pe(lambda e, pc=pc, ab=ab, bb=bb, h=h: e.matmul(
                            pU[pc][:, 0:128], Kp[pc][ab][:], vb[bb][:, h * 128:(h + 1) * 128],
                            start=(ab == 0), stop=(ab == 1)), r=[t_Kp[pc][ab], t_vb[bb]], w=[t_pU[pc]])
                    P.dve(lambda e, pc=pc: e.tensor_tensor(Sf[pc][:], pU[pc][:, 0:128], Sf[pc][:], ALU.add),
                          r=[t_pU[pc], t_Sf[pc]], w=[t_Sf[pc]])
                    P.pool(lambda e, pc=pc, c=c: e.tensor_scalar(Sf[pc][:], Sf[pc][:], DEC[pc][:, c:c + 1], None, ALU.mult),
                           r=[t_Sf[pc], t_DEC[pc]], w=[t_Sf[pc]])
                    P.act(lambda e, pc=pc: e.copy(Sb[pc][:], Sf[pc][:]), r=[t_Sf[pc]], w=[t_Sb[pc]])
                P.act(lambda e, bb=bb: e.copy(oo[bb][:], pO[:]), r=[t_pO], w=[t_oo[bb]])
                head_norm_tm(P, oo[bb], t_oo[bb], H, 128, 1e-5, st[bb], t_st[bb])
                P.pool(lambda e, bb=bb: e.tensor_tensor(oo[bb][:], oo[bb][:], ng[:], ALU.mult),
                       r=[t_oo[bb], t_ng], w=[t_oo[bb]])
                P.act(lambda e, bb=bb: e.activation(ot[bb][:], ot[bb][:], AF.Silu), r=[t_ot[bb]], w=[t_ot[bb]])
                P.dve(lambda e, bb=bb: e.tensor_tensor(yo[bb][:], oo[bb][:], ot[bb][:], ALU.mult),
                      r=[t_oo[bb], t_ot[bb]], w=[t_yo[bb]])
                P.dma("sp", yt[r0:r0 + 128, 512:1024], yo[bb][:], r=[t_yo[bb]], sig=t_yo[bb])
    P.end()


RK_C = math.exp(-0.5)


def rwkv_phase(P, cx, name, o_idx, pf, pt, yt, T, S):
    P.begin(name)
    SEG = min(512, S)
    nseg = S // SEG
    NBK = SEG // 128
    w_ = min(512, SEG)
    cst = P.tile([128, NCONST], F32, "cst")
    t_cst = P.tok("cst")
    P.dma("sp", cst[:], cx.consts, w=[t_cst])
    cb = P.tile([128, 1024], BF16, "cb")
    t_cb = P.tok("cb")
    P.act(lambda e: e.copy(cb[:, 0:128], cst[:, C_IDENT:C_IDENT + 128]), r=[t_cst], w=[t_cb])
    P.act(lambda e: e.copy(cb[:, 128:256], cst[:, C_BLKIND:C_BLKIND + 128]), r=[t_cst], w=[t_cb])
    idb = cb[:, 0:128]
    blkind = cb[:, 128:130]
    mask_le = cst[:, C_LE:C_LE + 128]
    mask_lt = cst[:, C_LT:C_LT + 128]
    mask_gt = cst[:, C_GT:C_GT + 128]
    ones = cst[:, C_ONES:C_ONES + 128]
    blk64 = cst[:, C_BLK64:C_BLK64 + 128]
    m2 = P.tile([128, 256], F32, "m2")
    t_m2 = P.tok("m2")
    P.dve(lambda e: e.tensor_copy(m2[:, 0:128], mask_lt), r=[t_cst], w=[t_m2])
    P.dve(lambda e: e.tensor_copy(m2[:, 128:256], mask_le), r=[t_cst], w=[t_m2])
    prm = P.tile([128, 4, 8], F32, "prm")
    t_prm = P.tok("prm")
    rkflat = cx.rk_r_k[o_idx].rearrange("h d -> (h d)")
    for pc in range(4):
        cs = slice(pc * 128, (pc + 1) * 128)
        for k, v in ((0, cx.rk_mu[o_idx, 0:512]), (1, cx.rk_mu[o_idx, 512:1024]), (2, cx.rk_w0[o_idx]),
                     (3, cx.rk_a0[o_idx]), (4, cx.rk_k_k[o_idx]), (5, cx.rk_k_a[o_idx]), (7, rkflat)):
            P.dma("sp", prm[:, pc, k:k + 1], colvec(v[cs]), w=[t_prm])
        P.dve(lambda e, pc=pc: e.tensor_scalar(prm[:, pc, 6:7], prm[:, pc, 5:6], -1.0, 1.0, ALU.mult, ALU.add),
              r=[t_prm], w=[t_prm])
    mul = P.tile([64, 4], F32, "mul")
    t_mul = P.tok("mul")
    P.dma("sp", mul[0:32, 0:1], colvec(cx.rk_mu[o_idx, 1536:1568]), w=[t_mul])
    P.dma("sp", mul[0:32, 1:2], colvec(cx.rk_mu[o_idx, 1568:1600]), w=[t_mul])
    P.dma("sp", mul[0:64, 2:3], colvec(cx.rk_mu[o_idx, 1600:1664]), w=[t_mul])
    wBb = P.tile([32, 512], BF16, "wBb")
    aBb = P.tile([32, 512], BF16, "aBb")
    gBb = P.tile([64, 512], BF16, "gBb")
    t_lw = P.tok("loraw")
    P.dma("pool", wBb[:], cx.rk_wB[o_idx], w=[t_lw])
    P.dma("pool", aBb[:], cx.rk_aB[o_idx], w=[t_lw])
    P.dma("pool", gBb[:], cx.rk_gB[o_idx], w=[t_lw])
    bt = P.tile([128, 3, 512], F32, "bt")
    t_bt = P.tok("bt")
    P.dma("sp", bt[:, 0, :], cx.rk_mu[o_idx, 1024:1536].partition_broadcast(128), w=[t_bt])
    P.dma("sp", bt[:, 1, :], cx.rk_ln_g[o_idx].partition_broadcast(128), w=[t_bt])
    P.dma("sp", bt[:, 2, :], cx.rk_ln_b[o_idx].partition_broadcast(128), w=[t_bt])
    lx = [P.tile([64, SEG + 1], F32, "lx") for _ in range(3)]
    ld = [P.tile([64, SEG], F32, "ld") for _ in range(3)]
    t_lx, t_ld = P.toks_n(3, "lx"), P.toks_n(3, "ld")
    twd = P.tile([32, SEG], BF16, "twd")
    adb = P.tile([32, SEG], BF16, "adb")
    sgd = P.tile([64, SEG], BF16, "sgd")
    t_twd, t_adb, t_sgd = P.tok("twd"), P.tok("adb"), P.tok("sgd")
    ARt = [P.tile([128, NBK, 256], BF16, "ARt") for _ in range(4)]
    BT = [P.tile([128, SEG], BF16, "BT") for _ in range(4)]
    KTt = [P.tile([128, SEG], BF16, "KTt") for _ in range(4)]
    PRD = [P.tile([128, SEG], BF16, "PRD") for _ in range(4)]
    WC = [P.tile([128, NBK], F32, "WC") for _ in range(4)]
    t_AR, t_BT, t_KT, t_PRD, t_WC = (P.toks_n(4, "AR"), P.toks_n(4, "BT"), P.toks_n(4, "KT"), P.toks_n(4, "PRD"),
                                     P.toks_n(4, "WC"))
    names = ["rx", "kx", "rs", "ks", "lw", "av", "cum", "E1", "E2", "E3", "kk", "t1", "km", "t2"]
    tmp = {}
    tt = {}
    for nm in names:
        wd_ = SEG + 1 if nm in ("rx", "kx") else SEG
        tmp[nm] = P.tile([128, wd_], F32, nm)
        tt[nm] = P.tok(nm)
    pz = [P.psum([128, 512], F32, "pz") for _ in range(2)]
    t_pz = P.toks_n(2, "pz")
    Tf = [P.tile([128, 64], F32, "Tf") for _ in range(4)]
    Tb = [P.tile([128, 64], BF16, "Tb") for _ in range(4)]
    t_Tf, t_Tb = P.toks_n(4, "Tf"), P.toks_n(4, "Tb")
    Bp = [[P.tile([128, 128], BF16, "Bp") for _ in range(2)] for _ in range(2)]
    Kp = [[P.tile([128, 128], BF16, "Kp") for _ in range(2)] for _ in range(2)]
    t_Bp = [P.toks_n(2, "Bp") for _ in range(2)]
    t_Kp = [P.toks_n(2, "Kp") for _ in range(2)]
    for par in range(2):
        for ab in range(2):
            P.dve(lambda e, par=par, ab=ab: e.memset(Bp[par][ab][:], 0.0), w=[t_Bp[par][ab]])
            P.pool(lambda e, par=par, ab=ab: e.memset(Kp[par][ab][:], 0.0), w=[t_Kp[par][ab]])
    hb_n = 8
    Pm = [P.tile([128, 128], BF16, "Pm") for _ in range(hb_n)]
    PT = [P.tile([128, 128], BF16, "PT") for _ in range(hb_n)]
    MT = [P.tile([128, 128], BF16, "MT") for _ in range(hb_n)]
    ArbT = [P.tile([128, 128], BF16, "ArbT") for _ in range(hb_n)]
    AakT = [P.tile([128, 128], BF16, "AakT") for _ in range(hb_n)]
    ArkT = [P.tile([128, 128], BF16, "ArkT") for _ in range(hb_n)]
    t_Pm, t_PT, t_MT, t_Arb, t_Aak, t_Ark = (P.toks_n(hb_n, "Pm"), P.toks_n(hb_n, "PT"), P.toks_n(hb_n, "MT"),
                                             P.toks_n(hb_n, "Arb"), P.toks_n(hb_n, "Aak"), P.toks_n(hb_n, "Ark"))
    pG4 = [P.psum([128, 512], F32, "pG4") for _ in range(4)]
    t_G4 = P.toks_n(4, "pG4")
    pC = P.psum([128, 512], F32, "pC")
    t_pC = P.tok("pC")
    pTr = P.psum([128, 512], BF16, "pTr")
    t_pTr = P.tok("pTr")
    pY = pz[1]
    t_pY = t_pz[1]
    rhs_sb = [P.tile([128, 128], BF16, "rhs") for _ in range(2)]
    u_sb = [P.tile([128, 128], BF16, "usb") for _ in range(2)]
    t_rhs, t_usb = P.toks_n(2, "rhs"), P.toks_n(2, "usb")
    vt = [P.tile([128, 512], F32, "vt") for _ in range(2)]
    vpv = [P.tile([128, 512], F32, "vpv") for _ in range(2)]
    vs = [P.tile([128, 512], F32, "vs") for _ in range(2)]
    vsb = [P.tile([128, 512], BF16, "vsb") for _ in range(2)]
    t_vt, t_vpv, t_vs, t_vsb = P.toks_n(2, "vt"), P.toks_n(2, "vpv"), P.toks_n(2, "vs"), P.toks_n(2, "vsb")
    yy = [P.tile([128, 512], F32, "yy") for _ in range(2)]
    gg = [P.tile([128, 512], F32, "gg") for _ in range(2)]
    bs = [P.tile([128, 8], F32, "bs") for _ in range(2)]
    st = [P.tile([128, 96], F32, "st") for _ in range(2)]
    yo = [P.tile([128, 512], BF16, "yo") for _ in range(2)]
    t_yy, t_gg, t_bs, t_st, t_yo = (P.toks_n(2, "yy"), P.toks_n(2, "gg"), P.toks_n(2, "bs"), P.toks_n(2, "st"),
                                    P.toks_n(2, "yo"))
    pGt = pz[0]
    t_pG = t_pz[0]
    npz = 0
    nhb = 0
    nblk = 0
    npar = 0
    ngrp = 0

    def shift(dst, t_dst, x, t_x, d, t_d, mu_col, n, rows=128):
        P.dve(lambda e: e.tensor_tensor(d[0:rows, 0:n], x[0:rows, 0:n], x[0:rows, 1:n + 1], ALU.subtract),
              r=[t_x], w=[t_d])
        P.dve(lambda e: e.scalar_tensor_tensor(dst[0:rows, 0:n], d[0:rows, 0:n], mu_col, x[0:rows, 1:n + 1],
                                               ALU.mult, ALU.add), r=[t_d, t_x], w=[t_dst])

    for sq in range(T // S):
        for pc in range(4):
            P.dve(lambda e, pc=pc: e.memset(Tf[pc][:], 0.0), w=[t_Tf[pc]])
            P.dve(lambda e, pc=pc: e.memset(Tb[pc][:], 0.0), w=[t_Tb[pc]])
        for sg_i in range(nseg):
            t0 = sq * S + sg_i * SEG
            first = (sg_i == 0)
            for li, (r0_, nr) in enumerate(((1024, 32), (1056, 32), (1088, 64))):
                if first:
                    P.dve(lambda e, li=li, nr=nr: e.memset(lx[li][0:nr, 0:1], 0.0), w=[t_lx[li]])
                    P.dma("sp", lx[li][0:nr, 1:SEG + 1], pf[r0_:r0_ + nr, t0:t0 + SEG], w=[t_lx[li]])
                else:
                    P.dma("sp", lx[li][0:nr, 0:SEG + 1], pf[r0_:r0_ + nr, t0 - 1:t0 + SEG], w=[t_lx[li]])
            shift(ld[0], t_ld[0], lx[0], t_lx[0], ld[0], t_ld[0], mul[0:32, 0:1], SEG, 32)
            P.act(lambda e: e.activation(twd[:], ld[0][0:32, :], AF.Tanh), r=[t_ld[0]], w=[t_twd])
            shift(ld[1], t_ld[1], lx[1], t_lx[1], ld[1], t_ld[1], mul[0:32, 1:2], SEG, 32)
            P.act(lambda e: e.copy(adb[:], ld[1][0:32, :]), r=[t_ld[1]], w=[t_adb])
            shift(ld[2], t_ld[2], lx[2], t_lx[2], ld[2], t_ld[2], mul[0:64, 2:3], SEG, 64)
            P.act(lambda e: e.activation(sgd[:], ld[2][0:64, :], AF.Sigmoid), r=[t_ld[2]], w=[t_sgd])
            for pc in range(4):
                cs = slice(pc * 128, (pc + 1) * 128)
                for nm, r0_ in (("rx", pc * 128), ("kx", 512 + pc * 128)):
                    if first:
                        P.dve(lambda e, nm=nm: e.memset(tmp[nm][:, 0:1], 0.0), w=[tt[nm]])
                        P.dma("sp", tmp[nm][:, 1:SEG + 1], pf[r0_:r0_ + 128, t0:t0 + SEG], w=[tt[nm]])
                    else:
                        P.dma("sp", tmp[nm][:, 0:SEG + 1], pf[r0_:r0_ + 128, t0 - 1:t0 + SEG], w=[tt[nm]])
                shift(tmp["rs"], tt["rs"], tmp["rx"], tt["rx"], tmp["rs"], tt["rs"], prm[:, pc, 0:1], SEG)
                shift(tmp["ks"], tt["ks"], tmp["kx"], tt["kx"], tmp["ks"], tt["ks"], prm[:, pc, 1:2], SEG)
                for sp in range(SEG // w_):
                    cw = slice(sp * w_, (sp + 1) * w_)
                    for wt, src, t_src, dst, bcol in ((wBb, twd, t_twd, "lw", 2), (aBb, adb, t_adb, "av", 3)):
                        pb = npz % 2
                        npz += 1
                        P.pe(lambda e, pb=pb, wt=wt, src=src, cw=cw, cs=cs: e.matmul(
                            pz[pb][:, 0:w_], wt[:, cs], src[:, cw], start=True, stop=True),
                            r=[t_lw, t_src], w=[t_pz[pb]])
                        P.act(lambda e, pb=pb, dst=dst, cw=cw, bcol=bcol, pc=pc: e.activation(
                            tmp[dst][:, cw], pz[pb][:, 0:w_], AF.Sigmoid, bias=prm[:, pc, bcol:bcol + 1]),
                            r=[t_pz[pb], t_prm], w=[tt[dst]])
                cumsum_blocks(P, tmp["cum"], tt["cum"], tmp["lw"], tt["lw"], ones, t_cst, SEG)
                P.act(lambda e: e.activation(tmp["E1"][:], tmp["cum"][:], AF.Exp, scale=-RK_C),
                      r=[tt["cum"]], w=[tt["E1"]])
                P.act(lambda e: e.activation(tmp["E2"][:], tmp["cum"][:], AF.Exp, scale=RK_C),
                      r=[tt["cum"]], w=[tt["E2"]])
                P.pool(lambda e: e.tensor_tensor(tmp["cum"][:], tmp["cum"][:], tmp["lw"][:], ALU.subtract),
                       r=[tt["cum"], tt["lw"]], w=[tt["cum"]])
                P.act(lambda e: e.activation(tmp["E3"][:], tmp["cum"][:], AF.Exp, scale=-RK_C),
                      r=[tt["cum"]], w=[tt["E3"]])
                P.act(lambda e, pc=pc: e.copy(WC[pc][:], tmp["E1"][:].rearrange("p (c t) -> p c t", t=128)[:, :, 127]),
                      r=[tt["E1"]], w=[t_WC[pc]])
                P.dve(lambda e, pc=pc: e.tensor_scalar(tmp["kk"][:], tmp["ks"][:], prm[:, pc, 4:5], None, ALU.mult),
                      r=[tt["ks"], t_prm], w=[tt["kk"]])
                P.pool(lambda e: e.tensor_tensor(tmp["t1"][:], tmp["kk"][:], tmp["kk"][:], ALU.mult),
                       r=[tt["kk"]], w=[tt["t1"]])
                for sp in range(SEG // w_):
                    cw = slice(sp * w_, (sp + 1) * w_)
                    pb = npz % 2
                    npz += 1
                    P.pe(lambda e, pb=pb, cw=cw: e.matmul(pz[pb][:, 0:w_], blk64, tmp["t1"][:, cw], start=True, stop=True),
                         r=[t_cst, tt["t1"]], w=[t_pz[pb]])
                    P.act(lambda e, pb=pb, cw=cw: e.activation(tmp["t2"][:, cw], pz[pb][:, 0:w_], AF.Sqrt),
                          r=[t_pz[pb]], w=[tt["t2"]])
                P.dve(lambda e: e.tensor_scalar(tmp["t2"][:], tmp["t2"][:], 1e-12, None, ALU.max), r=[tt["t2"]], w=[tt["t2"]])
                P.dve(lambda e: e.reciprocal(tmp["t2"][:], tmp["t2"][:]), r=[tt["t2"]], w=[tt["t2"]])
                P.dve(lambda e: e.tensor_tensor(tmp["kk"][:], tmp["kk"][:], tmp["t2"][:], ALU.mult),
                      r=[tt["kk"], tt["t2"]], w=[tt["kk"]])
                P.dve(lambda e, pc=pc: e.tensor_scalar(tmp["t1"][:], tmp["av"][:], prm[:, pc, 5:6], prm[:, pc, 6:7],
                                                       ALU.mult, ALU.add), r=[tt["av"], t_prm, tt["t1"]], w=[tt["t1"]])
                P.pool(lambda e: e.tensor_tensor(tmp["km"][:], tmp["ks"][:], tmp["t1"][:], ALU.mult),
                       r=[tt["ks"], tt["t1"]], w=[tt["km"]])
                v3 = lambda ap: ap.rearrange("p (c t) -> p c t", t=128)
                P.dve(lambda e, pc=pc: e.scalar_tensor_tensor(ARt[pc][:, :, 0:128], v3(tmp["kk"][:]), -1.0,
                                                              v3(tmp["E3"][:]), ALU.mult, ALU.mult),
                      r=[tt["kk"], tt["E3"]], w=[t_AR[pc]])
                P.pool(lambda e, pc=pc: e.tensor_tensor(ARt[pc][:, :, 128:256], v3(tmp["rs"][:]), v3(tmp["E1"][:]),
                                                        ALU.mult), r=[tt["rs"], tt["E1"]], w=[t_AR[pc]])
                P.dve(lambda e: e.tensor_tensor(tmp["t2"][:], tmp["kk"][:], tmp["av"][:], ALU.mult),
                      r=[tt["kk"], tt["av"], tt["t2"]], w=[tt["t2"]])
                P.dve(lambda e, pc=pc: e.tensor_tensor(BT[pc][:], tmp["t2"][:], tmp["E2"][:], ALU.mult),
                      r=[tt["t2"], tt["E2"]], w=[t_BT[pc]])
                P.pool(lambda e, pc=pc: e.tensor_tensor(KTt[pc][:], tmp["km"][:], tmp["E2"][:], ALU.mult),
                       r=[tt["km"], tt["E2"]], w=[t_KT[pc]])
                P.dve(lambda e, pc=pc: e.scalar_tensor_tensor(PRD[pc][:], tmp["rs"][:], prm[:, pc, 7:8], tmp["km"][:],
                                                              ALU.mult, ALU.mult),
                      r=[tt["rs"], tt["km"], t_prm], w=[t_PRD[pc]])
            for c in range(NBK):
                bb = nblk % 2
                nblk += 1
                r0 = t0 + c * 128
                blk = slice(c * 128, (c + 1) * 128)
                P.dma("sp", vt[bb][:], pt[r0:r0 + 128, 0:512], w=[t_vt[bb]])
                if first and c == 0:
                    P.dve(lambda e, bb=bb: e.memset(vpv[bb][0:1, :], 0.0), w=[t_vpv[bb]])
                    P.dma("sp", vpv[bb][1:128, :], pt[r0:r0 + 127, 0:512], w=[t_vpv[bb]])
                else:
                    P.dma("sp", vpv[bb][:], pt[r0 - 1:r0 + 127, 0:512], w=[t_vpv[bb]])
                P.pool(lambda e, bb=bb: e.tensor_tensor(vpv[bb][:], vpv[bb][:], vt[bb][:], ALU.subtract),
                       r=[t_vpv[bb], t_vt[bb]], w=[t_vpv[bb]])
                P.pool(lambda e, bb=bb: e.tensor_tensor(vpv[bb][:], vpv[bb][:], bt[:, 0, :], ALU.mult),
                       r=[t_vpv[bb], t_bt], w=[t_vpv[bb]])
                P.pool(lambda e, bb=bb: e.tensor_tensor(vs[bb][:], vpv[bb][:], vt[bb][:], ALU.add),
                       r=[t_vpv[bb], t_vt[bb]], w=[t_vs[bb]])
                P.act(lambda e, bb=bb: e.copy(vsb[bb][:], vs[bb][:]), r=[t_vs[bb]], w=[t_vsb[bb]])
                P.pe(lambda e, blk=blk: e.matmul(pGt[:], sgd[:, blk], gBb[:], start=True, stop=True),
                     r=[t_sgd, t_lw], w=[t_pG])
                P.act(lambda e, bb=bb: e.copy(gg[bb][:], pGt[:]), r=[t_pG], w=[t_gg[bb]])
                for pc in range(4):
                    P.pe(lambda e, pc=pc, blk=blk: e.matmul(pC[:, 320 + 2 * pc:322 + 2 * pc], PRD[pc][:, blk], blkind,
                                                            start=True, stop=True), r=[t_PRD[pc], t_cb], w=[t_pC])
                P.dve(lambda e, bb=bb: e.tensor_copy(bs[bb][:], pC[:, 320:328]), r=[t_pC], w=[t_bs[bb]])
                for grp in range(2):
                    gp = ngrp % 2
                    ngrp += 1
                    heads = []
                    for k in range(4):
                        pc_ = 2 * grp + k // 2
                        heads.append((pc_, k % 2, gp * 4 + k, k))
                    for (pc, ab, hb, bk) in heads:
                        po = ab * 64
                        AR = ARt[pc][:, c, :]
                        P.pe(lambda e, bk=bk, pc=pc, po=po, AR=AR, blk=blk: e.matmul(
                            pG4[bk][:, 0:128], AR[po:po + 64, 0:128], BT[pc][po:po + 64, blk], start=True, stop=True),
                            r=[t_BT[pc], t_AR[pc]], w=[t_G4[bk]])
                        P.pe(lambda e, bk=bk, pc=pc, po=po, AR=AR, blk=blk: e.matmul(
                            pG4[bk][:, 128:384], BT[pc][po:po + 64, blk], AR[po:po + 64, :], start=True, stop=True),
                            r=[t_BT[pc], t_AR[pc]], w=[t_G4[bk]])
                    for (pc, ab, hb, bk) in heads:
                        P.dve(lambda e, hb=hb, bk=bk: e.tensor_tensor(Pm[hb][:], pG4[bk][:, 0:128], mask_gt, ALU.mult),
                              r=[t_G4[bk], t_cst], w=[t_Pm[hb]])
                        P.dve(lambda e, hb=hb, bk=bk: e.tensor_tensor(PT[hb][:], pG4[bk][:, 128:256], mask_lt, ALU.mult),
                              r=[t_G4[bk], t_cst], w=[t_PT[hb]])
                        P.dve(lambda e, hb=hb, bk=bk: e.tensor_tensor(ArbT[hb][:], pG4[bk][:, 256:384], mask_le, ALU.mult),
                              r=[t_G4[bk], t_cst], w=[t_Arb[hb]])
                        P.pool(lambda e, hb=hb: e.tensor_tensor(MT[hb][:], PT[hb][:], idb, ALU.add),
                               r=[t_PT[hb], t_cb], w=[t_MT[hb]])
                    for (pc, ab, hb, bk) in heads:
                        po = ab * 64
                        AR = ARt[pc][:, c, :]
                        P.pe(lambda e, bk=bk, pc=pc, po=po, AR=AR, blk=blk: e.matmul(
                            pG4[bk][:, 0:256], KTt[pc][po:po + 64, blk], AR[po:po + 64, :], start=True, stop=True),
                            r=[t_KT[pc], t_AR[pc]], w=[t_G4[bk]])
                    for (pc, ab, hb, bk) in heads:
                        P.dve(lambda e, hb=hb, bk=bk: e.tensor_tensor(AakT[hb][:], pG4[bk][:, 0:128], mask_lt, ALU.mult),
                              r=[t_G4[bk], t_cst], w=[t_Aak[hb]])
                        P.dve(lambda e, hb=hb, bk=bk: e.tensor_tensor(ArkT[hb][:], pG4[bk][:, 128:256], mask_le, ALU.mult),
                              r=[t_G4[bk], t_cst], w=[t_Ark[hb]])
                    for lvl in range(6):
                        lastl = (lvl == 5)
                        for (pc, ab, hb, bk) in heads:
                            P.pe(lambda e, bk=bk, hb=hb: e.matmul(pG4[bk][:, 0:128], PT[hb][:], Pm[hb][:], start=True, stop=True),
                                 r=[t_PT[hb], t_Pm[hb]], w=[t_G4[bk]])
                            if not lastl:
                                P.pe(lambda e, bk=bk, hb=hb: e.matmul(pG4[bk][:, 128:256], Pm[hb][:], PT[hb][:], start=True,
                                                                      stop=True),
                                     r=[t_PT[hb], t_Pm[hb]], w=[t_G4[bk]])
                        for (pc, ab, hb, bk) in heads:
                            P.act(lambda e, bk=bk, hb=hb: e.copy(Pm[hb][:], pG4[bk][:, 0:128]), r=[t_G4[bk]], w=[t_Pm[hb]])
                            if not lastl:
                                P.act(lambda e, bk=bk, hb=hb: e.copy(PT[hb][:], pG4[bk][:, 128:256]),
                                      r=[t_G4[bk]], w=[t_PT[hb]])
                        for (pc, ab, hb, bk) in heads:
                            P.pe(lambda e, bk=bk, hb=hb: e.matmul(pG4[bk][:, 256:384], Pm[hb][:], MT[hb][:], start=True, stop=True),
                                 r=[t_Pm[hb], t_MT[hb]], w=[t_G4[bk]])
                        for (pc, ab, hb, bk) in heads:
                            P.dve(lambda e, bk=bk, hb=hb: e.tensor_tensor(MT[hb][:], pG4[bk][:, 256:384], MT[hb][:], ALU.add),
                                  r=[t_G4[bk], t_MT[hb]], w=[t_MT[hb]])
                    for pc in (2 * grp, 2 * grp + 1):
                        par = npar % 2
                        npar += 1
                        AR = ARt[pc][:, c, :]
                        hbs = [gp * 4 + 2 * (pc % 2) + ab_ for ab_ in range(2)]
                        for ab in range(2):
                            po = ab * 64
                            hb = hbs[ab]
                            h = 2 * pc + ab
                            P.pe(lambda e, AR=AR, po=po, pc=pc, ab=ab: e.matmul(
                                pC[:, ab * 64:(ab + 1) * 64], AR[po:po + 64, 0:128], Tb[pc][po:po + 64, :],
                                start=True, stop=False), r=[t_AR[pc], t_Tb[pc]], w=[t_pC])
                            P.pe(lambda e, hb=hb, bb=bb, h=h, ab=ab: e.matmul(
                                pC[:, ab * 64:(ab + 1) * 64], AakT[hb][:], vsb[bb][:, h * 64:(h + 1) * 64],
                                start=False, stop=True), r=[t_Aak[hb], t_vsb[bb]], w=[t_pC])
                        P.act(lambda e, par=par: e.copy(rhs_sb[par][:], pC[:, 0:128]), r=[t_pC], w=[t_rhs[par]])
                        for ab in range(2):
                            hb = hbs[ab]
                            P.pe(lambda e, hb=hb, par=par, ab=ab: e.matmul(
                                pC[:, 128 + ab * 64:128 + (ab + 1) * 64], MT[hb][:], rhs_sb[par][:, ab * 64:(ab + 1) * 64],
                                start=True, stop=True), r=[t_MT[hb], t_rhs[par]], w=[t_pC])
                        P.dve(lambda e, par=par: e.tensor_copy(u_sb[par][:], pC[:, 128:256]), r=[t_pC], w=[t_usb[par]])
                        for ab in range(2):
                            po = ab * 64
                            hb = hbs[ab]
                            h = 2 * pc + ab
                            ycol = slice(h * 64, (h + 1) * 64)
                            P.pe(lambda e, AR=AR, po=po, pc=pc, ycol=ycol: e.matmul(
                                pY[:, ycol], AR[po:po + 64, 128:256], Tb[pc][po:po + 64, :], start=True, stop=False),
                                r=[t_AR[pc], t_Tb[pc]], w=[t_pY])
                            P.pe(lambda e, hb=hb, par=par, ab=ab, ycol=ycol: e.matmul(
                                pY[:, ycol], ArbT[hb][:], u_sb[par][:, ab * 64:(ab + 1) * 64], start=False, stop=False),
                                r=[t_Arb[hb], t_usb[par]], w=[t_pY])
                            P.pe(lambda e, hb=hb, bb=bb, ycol=ycol: e.matmul(
                                pY[:, ycol], ArkT[hb][:], vsb[bb][:, ycol], start=False, stop=True),
                                r=[t_Ark[hb], t_vsb[bb]], w=[t_pY])
                        P.pe(lambda e, pc=pc, blk=blk: e.transpose(pTr[:, 0:128], BT[pc][:, blk], idb),
                             r=[t_BT[pc], t_cb], w=[t_pTr])
                        P.pe(lambda e, pc=pc, blk=blk: e.transpose(pTr[:, 128:256], KTt[pc][:, blk], idb),
                             r=[t_KT[pc], t_cb], w=[t_pTr])
                        P.act(lambda e, par=par: e.copy(Bp[par][0][:, 0:64], pTr[:, 0:64]), r=[t_pTr], w=[t_Bp[par][0]])
                        P.act(lambda e, par=par: e.copy(Bp[par][1][:, 64:128], pTr[:, 64:128]), r=[t_pTr], w=[t_Bp[par][1]])
                        P.dve(lambda e, par=par: e.tensor_copy(Kp[par][0][:, 0:64], pTr[:, 128:192]),
                              r=[t_pTr], w=[t_Kp[par][0]])
                        P.dve(lambda e, par=par: e.tensor_copy(Kp[par][1][:, 64:128], pTr[:, 192:256]),
                              r=[t_pTr], w=[t_Kp[par][1]])
                        for ab in range(2):
                            h = 2 * pc + ab
                            P.pe(lambda e, par=par, ab=ab: e.matmul(pC[:, 256:320], Bp[par][ab][:],
                                                                    u_sb[par][:, ab * 64:(ab + 1) * 64],
                                                                    start=(ab == 0), stop=False),
                                 r=[t_Bp[par][ab], t_usb[par]], w=[t_pC])
                        for ab in range(2):
                            h = 2 * pc + ab
                            P.pe(lambda e, par=par, ab=ab, bb=bb, h=h: e.matmul(pC[:, 256:320], Kp[par][ab][:],
                                                                               vsb[bb][:, h * 64:(h + 1) * 64],
                                                                               start=False, stop=(ab == 1)),
                                 r=[t_Kp[par][ab], t_vsb[bb]], w=[t_pC])
                        P.dve(lambda e, pc=pc: e.tensor_tensor(Tf[pc][:], pC[:, 256:320], Tf[pc][:], ALU.add),
                              r=[t_pC, t_Tf[pc]], w=[t_Tf[pc]])
                        P.pool(lambda e, pc=pc, c=c: e.tensor_scalar(Tf[pc][:], Tf[pc][:], WC[pc][:, c:c + 1], None, ALU.mult),
                               r=[t_Tf[pc], t_WC[pc]], w=[t_Tf[pc]])
                        P.act(lambda e, pc=pc: e.copy(Tb[pc][:], Tf[pc][:]), r=[t_Tf[pc]], w=[t_Tb[pc]])
                P.act(lambda e, bb=bb: e.copy(yy[bb][:], pY[:]), r=[t_pY], w=[t_yy[bb]])
                head_norm_tm(P, yy[bb], t_yy[bb], 8, 64, 64e-5, st[bb], t_st[bb])
                P.pool(lambda e, bb=bb: e.tensor_tensor(yy[bb][:], yy[bb][:], bt[:, 1, :], ALU.mult),
                       r=[t_yy[bb], t_bt], w=[t_yy[bb]])
                P.pool(lambda e, bb=bb: e.tensor_tensor(yy[bb][:], yy[bb][:], bt[:, 2, :], ALU.add),
                       r=[t_yy[bb], t_bt], w=[t_yy[bb]])
                for h in range(8):
                    hc = slice(h * 64, (h + 1) * 64)
                    P.dve(lambda e, bb=bb, h=h, hc=hc: e.scalar_tensor_tensor(
                        yy[bb][:, hc], vs[bb][:, hc], bs[bb][:, h:h + 1], yy[bb][:, hc], ALU.mult, ALU.add),
                        r=[t_vs[bb], t_bs[bb], t_yy[bb]], w=[t_yy[bb]])
                P.dve(lambda e, bb=bb: e.tensor_tensor(yo[bb][:], yy[bb][:], gg[bb][:], ALU.mult),
                      r=[t_yy[bb], t_gg[bb]], w=[t_yo[bb]])
                P.dma("sp", yt[r0:r0 + 128, 0:512], yo[bb][:], r=[t_yo[bb]], sig=t_yo[bb])
    P.end()


def odd_layer_mix(P, cx, l, o_idx, x_in, x_out, T, S):
    n = "L%d" % l
    proj_phase(P, cx, n + "pj", x_in, cx.od_w_in[o_idx], ODD_IN, ODD_F, ODD_T, cx.pf, cx.pt, T)
    rwkv_phase(P, cx, n + "rk", o_idx, cx.pf, cx.pt, cx.yt, T, S)
    gla_phase(P, cx, n + "gl", o_idx, cx.pf, cx.pt, cx.yt, T, S)
    outproj_phase(P, cx, n + "op", x_in, x_out, cx.od_w_out[o_idx], None, 0, cx.yt, 1024,
                  cx.ln_g[l, 1], cx.ln_b[l, 1], T)


def build_full(T, S):
    nc = bass.Bass("TRN2", target_bir_lowering=False)
    cx = Ctx()
    cx.T = T
    dt = nc.dram_tensor
    cx.x = dt("x", [T, D], F32, kind="ExternalInput").ap()
    cx.consts = dt("consts", [128, NCONST], F32, kind="ExternalInput").ap()
    for nm, shape in PARAM_SHAPES:
        setattr(cx, nm, dt(nm, list(shape), F32, kind="ExternalInput").ap())
    cx.out = dt("out", [T, D], F32, kind="ExternalOutput").ap()
    cx.xa = dt("xa", [T, D], F32).ap()
    cx.xb = dt("xb", [T, D], F32).ap()
    cx.pf = dt("pf", [2048, T], F32).ap()
    cx.pt = dt("pt", [T, 1536], F32).ap()
    cx.yf = dt("yf", [512, T], BF16).ap()
    cx.yt = dt("yt", [T, 1024], BF16).ap()
    P = Prog(nc)
    cx.P = P
    cur = cx.x
    bufs = [cx.xa, cx.xb]
    nb = 0

    def nxt(last=False):
        nonlocal nb
        if last:
            return cx.out
        b = bufs[nb % 2]
        nb += 1
        return b

    for l in range(DEPTH):
        d = nxt()
        ffn_phase(P, cx, "L%df1" % l, cur, d, cx.ffn1_wi[l], cx.ffn1_wo[l], cx.ln_g[l, 0], cx.ln_b[l, 0], T)
        cur = d
        d = nxt()
        if l % 2 == 0:
            even_layer_mix(P, cx, l, l // 2, cur, d, T, S)
        else:
            odd_layer_mix(P, cx, l, l // 2, cur, d, T, S)
        cur = d
        d = nxt(last=(l == DEPTH - 1))
        ffn_phase(P, cx, "L%df2" % l, cur, d, cx.ffn2_wi[l], cx.ffn2_wo[l], cx.ln_g[l, 2], cx.ln_b[l, 2], T)
        cur = d
    P.ges.close()
    return nc, cx


def kernel(**inputs):
    x = np.ascontiguousarray(np.asarray(inputs["x"], dtype=np.float32))
    B, S, _ = x.shape
    per = B // NCORES
    T = per * S
    nc, cx = build_full(T, S)
    consts = make_consts()
    params = {nm: np.ascontiguousarray(np.asarray(inputs[nm], dtype=np.float32)) for nm, _ in PARAM_SHAPES}
    in_maps = []
    for c in range(NCORES):
        m = dict(params)
        m["x"] = x[c * per:(c + 1) * per].reshape(T, D)
        m["consts"] = consts
        in_maps.append(m)
    res = run_bass_kernel_spmd(nc, in_maps, core_ids=list(range(NCORES)))
    outs = [np.asarray(r["out"]).reshape(per, S, D) for r in res.results]
    return np.concatenate(outs, axis=0).astype(np.float32)
```
